# Optimizing a Trainium2 kernel written in Bass

```python
import jax, jax.numpy as jnp
from jax import lax
import numpy as np

D_MODEL = 1024
BATCH = 4
SEQ = 4096
DEPTH = 4

HEAD_DIM = 64
ATTN_WIDTH = D_MODEL // 2
CONV_WIDTH = D_MODEL - ATTN_WIDTH
N_ATTN_HEADS = ATTN_WIDTH // HEAD_DIM
N_CONV_GROUPS = CONV_WIDTH // HEAD_DIM
IN_PROJ_WIDTH = 3 * ATTN_WIDTH + 3 * CONV_WIDTH
CONV_K = 3
ROPE_DIM = HEAD_DIM // 4
ROPE_THETA = 500000.0
DILATED_BRANCHES = ((128, 1), (512, 4), (2048, 16))
FFN_HIDDEN = ((8 * D_MODEL // 3 + 255) // 256) * 256
RMS_EPS = 1e-6
NEG_INF = -1e30

kernel_name = "hybrid_dilated_attn_shortconv_encoder"


def rms_norm(x, g):
    xf = x.astype(jnp.float32)
    y = xf * lax.rsqrt(jnp.mean(xf * xf, axis=-1, keepdims=True) + RMS_EPS)
    return (y * g.astype(jnp.float32)).astype(x.dtype)


def rotary_tables(positions):
    inv_freq = ROPE_THETA ** (-jnp.arange(0, ROPE_DIM, 2, dtype=jnp.float32) / ROPE_DIM)
    ang = positions.astype(jnp.float32)[..., None] * inv_freq
    return jnp.cos(ang)[:, None], jnp.sin(ang)[:, None]


def apply_partial_rotary(t, cos, sin):
    tf = t.astype(jnp.float32)
    half = ROPE_DIM // 2
    t1 = tf[..., :half]
    t2 = tf[..., half:ROPE_DIM]
    out = jnp.concatenate([t1 * cos - t2 * sin, t2 * cos + t1 * sin, tf[..., ROPE_DIM:]], axis=-1)
    return out.astype(t.dtype)


def banded_attention(q, k, v, half):
    L, dh = q.shape[-2], q.shape[-1]
    lead = q.shape[:-2]
    nb = -(-L // half)
    lp = nb * half
    pad_q = [(0, 0)] * len(lead) + [(0, lp - L), (0, 0)]
    pad_kv = [(0, 0)] * len(lead) + [(half, lp - L + half), (0, 0)]
    qb = jnp.pad(q, pad_q).reshape(*lead, nb, half, dh).astype(jnp.float32)

    def windows(t):
        tb = jnp.pad(t, pad_kv).reshape(*lead, nb + 2, half, dh).astype(jnp.float32)
        return jnp.concatenate([tb[..., :-2, :, :], tb[..., 1:-1, :, :], tb[..., 2:, :, :]], axis=-2)

    kw = windows(k)
    vw = windows(v)
    s = jnp.einsum('...nqd,...nkd->...nqk', qb, kw) * (dh ** -0.5)
    qi = jnp.arange(nb)[:, None] * half + jnp.arange(half)[None, :]
    ki = jnp.arange(nb)[:, None] * half + jnp.arange(3 * half)[None, :] - half
    valid = (jnp.abs(qi[:, :, None] - ki[:, None, :]) <= half) & ((ki >= 0) & (ki < L))[:, None, :]
    s = jnp.where(valid, s, NEG_INF)
    lse = jax.nn.logsumexp(s, axis=-1)
    p = jnp.exp(s - lse[..., None])
    o = jnp.einsum('...nqk,...nkd->...nqd', p, vw)
    o = o.reshape(*lead, lp, dh)[..., :L, :]
    lse = lse.reshape(*lead, lp)[..., :L]
    return o, lse


def dilated_mixture_attention(q, k, v):
    b, h, s, dh = q.shape
    outs, lses = [], []
    for window, dil in DILATED_BRANCHES:
        L = s // dil
        half = window // (2 * dil)

        def by_residue(t):
            return t.reshape(b, h, L, dil, dh).swapaxes(2, 3)

        o, lse = banded_attention(by_residue(q), by_residue(k), by_residue(v), half)
        outs.append(o.swapaxes(2, 3).reshape(b, h, s, dh))
        lses.append(lse.swapaxes(2, 3).reshape(b, h, s))
    w = jax.nn.softmax(jnp.stack(lses, axis=0), axis=0)
    o = jnp.sum(w[..., None] * jnp.stack(outs, axis=0), axis=0)
    return o.astype(q.dtype)


def short_conv(u, w):
    c = u.shape[-1]
    return lax.conv_general_dilated(
        u, w.reshape(CONV_K, 1, c).astype(u.dtype), window_strides=(1,),
        padding=[(CONV_K // 2, CONV_K // 2)], dimension_numbers=('NWC', 'WIO', 'NWC'),
        feature_group_count=c)


def setup_inputs(seed: int = 0) -> dict:
    key = jax.random.key(seed)
    ks = jax.random.split(key, 16)
    f32 = jnp.float32

    def gain(k, shape):
        return 1.0 + 0.02 * jax.random.normal(k, shape, f32)

    x = jax.random.normal(ks[0], (BATCH, SEQ, D_MODEL), f32)
    offsets = jax.random.randint(ks[1], (BATCH, 1), 0, 1024, dtype=jnp.int32)
    positions = (jnp.arange(SEQ, dtype=jnp.int32)[None, :] + offsets).astype(jnp.int32)
    return {
        "x": x,
        "positions": positions,
        "pre_mix_norm": gain(ks[2], (DEPTH, D_MODEL)),
        "w_in": jax.random.normal(ks[3], (DEPTH, D_MODEL, IN_PROJ_WIDTH), f32) * D_MODEL ** -0.5,
        "conv_w": jax.random.normal(ks[4], (DEPTH, CONV_K, CONV_WIDTH), f32) * CONV_K ** -0.5,
        "attn_out_norm": gain(ks[5], (DEPTH, ATTN_WIDTH)),
        "conv_out_norm": gain(ks[6], (DEPTH, CONV_WIDTH)),
        "w_out": jax.random.normal(ks[7], (DEPTH, D_MODEL, D_MODEL), f32) * D_MODEL ** -0.5,
        "post_mix_norm": gain(ks[8], (DEPTH, D_MODEL)),
        "pre_ffn_norm": gain(ks[9], (DEPTH, D_MODEL)),
        "w_gate_up": jax.random.normal(ks[10], (DEPTH, D_MODEL, 2 * FFN_HIDDEN), f32) * D_MODEL ** -0.5,
        "w_down": jax.random.normal(ks[11], (DEPTH, FFN_HIDDEN, D_MODEL), f32) * FFN_HIDDEN ** -0.5,
        "post_ffn_norm": gain(ks[12], (DEPTH, D_MODEL)),
    }


def reference(x, positions, pre_mix_norm, w_in, conv_w, attn_out_norm, conv_out_norm,
              w_out, post_mix_norm, pre_ffn_norm, w_gate_up, w_down, post_ffn_norm):
    b, s, _ = x.shape
    cos, sin = rotary_tables(positions)
    split_points = np.cumsum([ATTN_WIDTH] * 3 + [CONV_WIDTH] * 2).tolist()

    def heads(t):
        return t.reshape(b, s, N_ATTN_HEADS, HEAD_DIM).transpose(0, 2, 1, 3)

    for l in range(DEPTH):
        h = rms_norm(x, pre_mix_norm[l])
        proj = jnp.einsum('bsd,de->bse', h, w_in[l])
        q, k, v, conv_u, gate_b, gate_c = jnp.split(proj, split_points, axis=-1)
        q = apply_partial_rotary(heads(q), cos, sin)
        k = apply_partial_rotary(heads(k), cos, sin)
        attn = dilated_mixture_attention(q, k, heads(v))
        attn = attn.transpose(0, 2, 1, 3).reshape(b, s, ATTN_WIDTH)
        conv_y = gate_b * short_conv(gate_c * conv_u, conv_w[l])
        merged = jnp.concatenate([rms_norm(attn, attn_out_norm[l]),
                                  rms_norm(conv_y, conv_out_norm[l])], axis=-1)
        mix = jnp.einsum('bse,ed->bsd', merged, w_out[l])
        x = x + rms_norm(mix, post_mix_norm[l])
        h = rms_norm(x, pre_ffn_norm[l])
        g, u = jnp.split(jnp.einsum('bsd,df->bsf', h, w_gate_up[l]), 2, axis=-1)
        f = jnp.einsum('bsf,fd->bsd', jax.nn.silu(g) * u, w_down[l])
        x = x + rms_norm(f, post_ffn_norm[l])
    return x
```

```python
import collections
import contextlib
import math

import numpy as np
import ml_dtypes

import concourse.bass as bass
import concourse.mybir as mybir
from concourse.bass_utils import run_bass_kernel_spmd

F32 = mybir.dt.float32
BF16 = mybir.dt.bfloat16
I32 = mybir.dt.int32
ALU = mybir.AluOpType
AF = mybir.ActivationFunctionType
NPBF = ml_dtypes.bfloat16

FUSED = False
DEPTH = 4
D = 1024
T = 2048
TB = 512
NTB = 4
FFN = 2816
NEG = -30000.0
EPS = 1e-6
GT_PER_LAYER = 52
REPLICA_GROUPS = [[0, 1], [2, 3], [4, 5], [6, 7]]
STOP = None
ENG_NAMES = ("pe", "act", "dve", "pool", "sp")


def sl(start, n, step=1):
    return slice(start, start + (n - 1) * step + 1, step)


class Op:
    __slots__ = ("idx", "eng", "fn", "reads", "writes", "dma", "tag", "inc", "count",
                 "waits", "marked", "known", "kdma")

    def __init__(self, idx, eng, fn, reads, writes, dma, tag, inc):
        self.idx = idx; self.eng = eng; self.fn = fn
        self.reads = tuple(reads); self.writes = tuple(writes)
        self.dma = dma; self.tag = tag; self.inc = inc
        self.count = 0; self.waits = []; self.marked = False
        self.known = None; self.kdma = None


class Sched:
    def __init__(self):
        self.ops = []
        self.last_write = {}
        self.readers = collections.defaultdict(dict)
        self.tag_count = collections.defaultdict(int)
        self.tag_inc = {}
        self.eng_known = {e: {} for e in ENG_NAMES}
        self.eng_kdma = {e: {} for e in ENG_NAMES}
        self.eng_ops = {e: [] for e in ENG_NAMES}

    def op(self, eng, fn, reads=(), writes=(), dma=False, tag=None, inc=16, nophase=False):
        i = len(self.ops)
        reads = list(reads); writes = list(writes)
        if not nophase:
            reads.append("PHASE")
        for r in list(reads):
            if r.startswith("ps"):
                reads.remove(r)
                if r not in writes:
                    writes.append(r)
        o = Op(i, eng, fn, reads, writes, dma, tag, inc)
        self.ops.append(o)
        deps = {}
        for r in o.reads:
            j = self.last_write.get(r)
            if j is not None:
                deps[j] = "RAW"
        for r in o.writes:
            j = self.last_write.get(r)
            if j is not None and j not in deps:
                deps[j] = "WAW"
            for j in self.readers[r].values():
                if j not in deps:
                    deps[j] = "WAR"
        known = self.eng_known[eng]
        kdma = self.eng_kdma[eng]
        waits = {}
        for j, kind in sorted(deps.items()):
            d = self.ops[j]
            if d.dma:
                if kdma.get(d.tag, 0) >= d.count:
                    continue
                c = self.tag_count[d.tag]
                waits[("dma", d.tag)] = c
                kdma[d.tag] = c
            else:
                if d.eng == eng:
                    if eng in ("pe", "sp") or kind != "RAW":
                        continue
                if known.get(d.eng, -1) >= j:
                    continue
                waits[("eng", d.eng)] = max(waits.get(("eng", d.eng), -1), j)
        for key, v in waits.items():
            if key[0] == "eng":
                d = self.ops[v]
                d.marked = True
                known[key[1]] = max(known.get(key[1], -1), v)
                for e2, k2 in d.known.items():
                    if known.get(e2, -1) < k2:
                        known[e2] = k2
                for t2, c2 in d.kdma.items():
                    if kdma.get(t2, 0) < c2:
                        kdma[t2] = c2
        o.waits = list(waits.items())
        if dma:
            assert self.tag_inc.setdefault(tag, inc) == inc
            self.tag_count[tag] += 1
            o.count = self.tag_count[tag]
        o.known = dict(known)
        o.kdma = dict(kdma)
        for r in o.reads:
            self.readers[r][("dma", i) if dma else eng] = i
        for r in o.writes:
            self.last_write[r] = i
            self.readers[r] = {}
        self.eng_ops[eng].append(o)
        return o

    def emit(self, nc, final_wait_tags=()):
        rank = {}
        for e in ENG_NAMES:
            c = 0
            for o in self.eng_ops[e]:
                if o.marked and not o.dma:
                    c += 1
                    rank[o.idx] = c
        with contextlib.ExitStack() as st:
            esem = {e: st.enter_context(nc.semaphore("s_" + e)) for e in ("pe", "act", "dve", "pool")}
            tsem = {t: st.enter_context(nc.semaphore("t_" + str(t))) for t in sorted(self.tag_count)}
            block = st.enter_context(nc.Block())
            handles = {"pe": block.tensor, "act": block.scalar, "dve": block.vector,
                       "pool": block.gpsimd, "sp": block.sync}

            def body(ename):
                def _f(eng):
                    for o in self.eng_ops[ename]:
                        for key, v in o.waits:
                            if key[0] == "eng":
                                eng.wait_ge(esem[key[1]], rank[v])
                            else:
                                eng.wait_ge(tsem[key[1]], v * self.tag_inc[key[1]])
                        ins = o.fn(eng)
                        if o.dma:
                            ins.then_inc(tsem[o.tag], o.inc)
                        elif o.marked:
                            ins.then_inc(esem[ename], 1)
                    if ename == "sp":
                        for t in final_wait_tags:
                            eng.wait_ge(tsem[t], self.tag_count[t] * self.tag_inc[t])
                return _f

            for e in ENG_NAMES:
                handles[e](body(e))


class SbAlloc:
    BASE = 16512
    LIMIT = 229344

    def __init__(self, nc):
        self.nc = nc
        self.off = self.BASE
        self.n = 0

    def at(self, name, shape, dtype, offset):
        nbytes = int(np.prod(shape[1:])) * (2 if dtype == BF16 else 4)
        assert offset % 32 == 0 and offset + nbytes <= self.LIMIT, (name, offset, nbytes)
        self.n += 1
        return self.nc.alloc_sbuf_tensor_at("%s_%d" % (name, self.n), list(shape), dtype, offset=offset), nbytes

    def new(self, name, shape, dtype):
        t, nbytes = self.at(name, shape, dtype, self.off)
        self.off += (nbytes + 31) // 32 * 32
        return t


def build_program(L):
    nc = bass.Bass("TRN2", target_bir_lowering=False)
    S = Sched()

    def dram(name, shape, dt, kind):
        return nc.dram_tensor(name, list(shape), dt, kind=kind)

    xT = dram("xT", [D, T], F32, "ExternalInput").ap()
    pos = dram("pos", [1, T], I32, "ExternalInput").ap()
    w_in = dram("w_in", [L, D, 3072], F32, "ExternalInput").ap()
    w_out = dram("w_out", [L, D, D], F32, "ExternalInput").ap()
    w_gu = dram("w_gu", [L, D, 2 * FFN], F32, "ExternalInput").ap()
    w_dn = dram("w_dn", [L, FFN, D], F32, "ExternalInput").ap()
    gt_d = dram("gt", [128, L * GT_PER_LAYER], F32, "ExternalInput").ap()
    cbf_d = dram("cbf", [128, 384 + 1024], BF16, "ExternalInput").ap()
    cf_d = dram("cf", [128, 4], F32, "ExternalInput").ap()
    yT = dram("yT", [D, T], F32, "ExternalOutput").ap()
    q_scr_t = dram("q_scr", [512, T], BF16, "Internal")
    sndk_t = dram("snd_k", [512, T], BF16, "Internal")
    sndv_t = dram("snd_v", [512, T], BF16, "Internal")
    sndz_t = dram("snd_z", [8, T], BF16, "Internal")
    rcvk_t = dram("rcv_k", [1024, T], BF16, "Internal")
    rcvv_t = dram("rcv_v", [1024, T], BF16, "Internal")
    rcvz_t = dram("rcv_z", [16, T], BF16, "Internal")
    q_scr = q_scr_t.ap()
    snd_k, snd_v, snd_z = sndk_t.ap(), sndv_t.ap(), sndz_t.ap()
    rcv_k, rcv_v, rcv_z = rcvk_t.ap(), rcvv_t.ap(), rcvz_t.ap()

    def allgather(src, dst, reads, wres, tag):
        S.op("pool", lambda e: e.collective_compute("AllGather", ALU.bypass, replica_groups=REPLICA_GROUPS, ins=[src], outs=[dst]),
             reads=reads, writes=[wres], dma=True, tag=tag, inc=1)

    A = SbAlloc(nc)
    x_sb = A.new("x", [128, 8, T], F32)
    cs = A.new("cs", [128, 2, T], BF16)
    cbf = A.new("cbf", [128, 384 + 1024], BF16)
    cf = A.new("cf", [128, 4], F32)
    gt = A.new("gt", [128, L * GT_PER_LAYER], F32)
    onesf = A.new("onesf", [128, 128], F32)
    selo = A.new("selo", [128, 128], F32)
    bar = A.new("bar", [128, 8], F32)
    NW = 3
    wsl = [A.new("w%d" % i, [128, 4096], BF16) for i in range(NW)]
    sq = A.new("sq", [128, 8, TB], BF16)
    rs = [A.new("rs%d" % i, [128, TB], F32) for i in range(2)]
    rq = [A.new("rq%d" % i, [128, TB], BF16) for i in range(2)]
    stg = [A.new("stg%d" % i, [128, TB], BF16) for i in range(4)]
    zh = A.new("zh", [128, 2, 4, 16], BF16)
    regB = A.off
    tmpf = [A.new("tmpf%d" % i, [128, TB], F32) for i in range(4)]
    oacc_e, _ = A.at("oacc_e", [128, T], F32, regB)
    arena = A.off
    hm, _ = A.at("hm", [128, 8, T], BF16, arena)
    regA = arena + 32768
    z, _ = A.at("z", [128, 4, T + 2], BF16, regA)
    bc, _ = A.at("bc", [128, 4, T], BF16, regA + 16416)
    k_ext, _ = A.at("k_ext", [128, 4096], BF16, regA)
    v_ext, _ = A.at("v_ext", [128, 4096], BF16, regA + 8192)
    vt, _ = A.at("vt", [128, 32, 160], BF16, regA + 16384)
    q_sb, _ = A.at("q_sb", [128, T], BF16, regA + 16384 + 10240)
    o2 = regA + 32800
    q16, _ = A.at("q16", [128, T], BF16, o2)
    pbuf = [A.at("pb%d" % i, [128, 256], BF16, o2 + 4096 + 512 * i)[0] for i in range(4)]
    oacc_o, _ = A.at("oacc_o", [128, T], F32, o2 + 4096 + 2048)
    mixbuf, _ = A.at("mixbuf", [128, 8, TB], BF16, regA)
    h2, _ = A.at("h2", [128, 8, 1024], BF16, arena)
    hid, _ = A.at("hid", [128, 22, 1024], BF16, arena + 16384)
    f_bf, _ = A.at("f_bf", [128, 8, 1024], BF16, arena + 16384 + 45056)
    assert arena + 16384 + 45056 + 16384 <= A.LIMIT
    su_i, _ = A.at("su_i", [128, T], I32, arena)
    su_ang, _ = A.at("su_ang", [128, T], F32, arena + 8192)
    su_a2, _ = A.at("su_a2", [128, T], F32, arena + 16384)
    su_kf, _ = A.at("su_kf", [128, T], F32, arena + 24576)
    su_ki, _ = A.at("su_ki", [128, T], I32, arena + 32768)
    su_r, _ = A.at("su_r", [128, T], F32, arena + 40960)

    ps = [nc.alloc_psum_tensor("ps%d" % i, [128, TB], F32) for i in range(7)]
    ps7 = nc.alloc_psum_tensor("ps7", [128, 1024], BF16)

    ident = cbf[:, 0:128]
    ones_bf = cbf[:, 128:256]
    perm = cbf[:, 256:384]

    def maskv(v):
        return cbf[:, 384 + 256 * v: 384 + 256 * (v + 1)]

    def barrier():
        S.op("pool", lambda e: e.memset(bar[:], 0.0), writes=["PHASE"], nophase=True)

    def finish():
        for c in range(8):
            S.op("sp", (lambda c_: lambda e: e.dma_start(out=yT[c_ * 128:(c_ + 1) * 128, :], in_=x_sb[:, c_, :]))(c),
                 reads=["x%d" % tb for tb in range(NTB)], dma=True, tag="yout")
        S.emit(nc, final_wait_tags=["yout"])
        return nc

    S.op("sp", lambda e: e.dma_start(out=cbf[:], in_=cbf_d), writes=["cbf"], dma=True, tag="c0")
    S.op("sp", lambda e: e.dma_start(out=cf[:], in_=cf_d), writes=["cf"], dma=True, tag="c0")
    S.op("sp", lambda e: e.dma_start(out=gt[:], in_=gt_d), writes=["gt"], dma=True, tag="c0")
    S.op("sp", lambda e: e.dma_start(out=su_i[:], in_=pos.partition_broadcast(128)), writes=["su_i"], dma=True, tag="c1")
    for c in range(8):
        S.op("sp", (lambda c: lambda e: e.dma_start(out=x_sb[:, c, :], in_=xT[c * 128:(c + 1) * 128, :]))(c),
             writes=["x%d" % tb for tb in range(NTB)], dma=True, tag="xin")
    S.op("pool", lambda e: e.memset(onesf[:], 1.0), writes=["onesf"])
    S.op("pool", lambda e: e.memset(selo[:, 0:64], 0.0), writes=["selo"])
    S.op("pool", lambda e: e.memset(selo[:, 64:128], 1.0), writes=["selo"])
    TWO_PI = 2.0 * math.pi
    S.op("dve", lambda e: e.tensor_copy(su_ang[:], su_i[:]), reads=["su_i"], writes=["su_ang"])
    S.op("dve", lambda e: e.tensor_scalar(su_ang[:], su_ang[:], cf[:, 0:1], None, ALU.mult), reads=["su_ang", "cf"], writes=["su_ang"])
    for which, shift in ((1, 0.0), (0, 0.5 * math.pi)):
        S.op("dve", (lambda sh: lambda e: e.tensor_scalar(su_a2[:], su_ang[:], sh, None, ALU.add))(shift), reads=["su_ang"], writes=["su_a2"])
        S.op("dve", lambda e: e.tensor_scalar(su_kf[:], su_a2[:], 1.0 / TWO_PI, None, ALU.mult), reads=["su_a2"], writes=["su_kf"])
        S.op("dve", lambda e: e.tensor_copy(su_ki[:], su_kf[:]), reads=["su_kf"], writes=["su_ki"])
        S.op("dve", lambda e: e.tensor_copy(su_kf[:], su_ki[:]), reads=["su_ki"], writes=["su_kf"])
        S.op("dve", lambda e: e.scalar_tensor_tensor(su_r[:], su_kf[:], -TWO_PI, su_a2[:], ALU.mult, ALU.add), reads=["su_kf", "su_a2"], writes=["su_r"])
        S.op("dve", lambda e: e.tensor_scalar(su_kf[:], su_r[:], math.pi, TWO_PI, ALU.is_gt, ALU.mult), reads=["su_r"], writes=["su_kf"])
        S.op("dve", lambda e: e.tensor_tensor(su_r[:], su_r[:], su_kf[:], ALU.subtract), reads=["su_r", "su_kf"], writes=["su_r"])
        S.op("dve", lambda e: e.tensor_scalar(su_kf[:], su_r[:], -math.pi, TWO_PI, ALU.is_lt, ALU.mult), reads=["su_r"], writes=["su_kf"])
        S.op("dve", lambda e: e.tensor_tensor(su_r[:], su_r[:], su_kf[:], ALU.add), reads=["su_r", "su_kf"], writes=["su_r"])
        S.op("act", (lambda w: lambda e: e.activation(cs[:, w, :], su_r[:], AF.Sin))(which), reads=["su_r"], writes=["cs"])
    barrier()
    if STOP == "setup":
        return finish()

    wtiles = []
    for l in range(L):
        for c0 in (1536, 2560, 512, 1024, 0, 2048):
            wtiles.append(("in", l, c0))
        for c0 in (0, 512):
            wtiles.append(("out", l, c0))
        for hf in range(2):
            for f in range(11):
                wtiles.append(("gu", l, f))
            for mp in range(4):
                for kh in range(2):
                    wtiles.append(("dn", l, (mp, kh)))
    wstate = {"issued": 0}

    def w_issue(upto):
        while wstate["issued"] <= min(upto, len(wtiles) - 1):
            i = wstate["issued"]
            kind, l, a = wtiles[i]
            s = i % NW
            slot = wsl[s]
            if kind in ("in", "out"):
                src = (w_in if kind == "in" else w_out)[l].rearrange("(kc p) n -> p kc n", p=128)[:, :, a:a + 512]
                dst = slot[:].rearrange("p (kc n) -> p kc n", kc=8)
                S.op("pool", (lambda d_, s_: lambda e: e.dma_start(out=d_, in_=s_))(dst, src),
                     writes=["w%d" % s], dma=True, tag="w%d" % s, nophase=True)
            elif kind == "gu":
                v = w_gu[l].rearrange("(kc p) n -> p kc n", p=128)
                dst = slot[:].rearrange("p (kc n) -> p kc n", kc=8)
                S.op("pool", (lambda d_, s_: lambda e: e.dma_start(out=d_, in_=s_))(dst[:, :, 0:256], v[:, :, a * 256:(a + 1) * 256]),
                     writes=["w%d" % s], dma=True, tag="w%d" % s, nophase=True)
                S.op("pool", (lambda d_, s_: lambda e: e.dma_start(out=d_, in_=s_))(dst[:, :, 256:512], v[:, :, FFN + a * 256:FFN + (a + 1) * 256]),
                     writes=["w%d" % s], dma=True, tag="w%d" % s, nophase=True)
            else:
                mp, kh = a
                v = w_dn[l].rearrange("(kc p) n -> p kc n", p=128)
                dst = slot[:, 0:2816].rearrange("p (kc n) -> p kc n", kc=11)
                S.op("pool", (lambda d_, s_: lambda e: e.dma_start(out=d_, in_=s_))(dst, v[:, kh * 11:(kh + 1) * 11, mp * 256:(mp + 1) * 256]),
                     writes=["w%d" % s], dma=True, tag="w%d" % s, nophase=True)
            wstate["issued"] += 1

    wcur = {"i": 0}

    def w_get():
        i = wcur["i"]
        w_issue(i)
        return i % NW, wsl[i % NW]

    def w_done():
        wcur["i"] += 1
        w_issue(wcur["i"] - 1 + NW)

    w_issue(NW - 1)

    cnt = {"mm": 0, "ss": 0, "rs": 0, "sq": 0, "tmpf": 0, "stg": 0, "rq": 0}

    def rr(key, n):
        v = cnt[key] % n
        cnt[key] += 1
        return v

    def mm_group(bank, pairs, reads):
        def fn(e):
            ins = None
            n = len(pairs)
            for i, (a_, b_) in enumerate(pairs):
                ins = e.matmul(ps[bank][:, :], a_, b_, start=(i == 0), stop=(i == n - 1))
            return ins
        S.op("pe", fn, reads=reads, writes=["ps%d" % bank])

    def rstd_from(ssbank, dim):
        r = rr("rs", 2)
        S.op("dve", lambda e: e.tensor_scalar(rs[r][:], ps[ssbank][:, :], 1.0 / dim, EPS, ALU.mult, ALU.add),
             reads=["ps%d" % ssbank], writes=["rs%d" % r])
        S.op("act", lambda e: e.activation(rs[r][:], rs[r][:], AF.Sqrt), reads=["rs%d" % r], writes=["rs%d" % r])
        S.op("dve", lambda e: e.reciprocal(rs[r][:], rs[r][:]), reads=["rs%d" % r], writes=["rs%d" % r])
        return r

    def sq_ones(src_ap_fn, nch, src_res, ssbank, first, last):
        base = 0 if (cnt["sq"] % 2 == 0) else 4
        cnt["sq"] += 1
        assert nch <= 4
        S.op("act", lambda e: e.activation(sq[:, base:base + nch, :], src_ap_fn(), AF.Square),
             reads=src_res, writes=["sq%d" % base])

        def fn(e):
            ins = None
            for c in range(nch):
                ins = e.matmul(ps[ssbank][:, :], ones_bf, sq[:, base + c, :], start=(first and c == 0), stop=(last and c == nch - 1))
            return ins
        S.op("pe", fn, reads=["sq%d" % base, "cbf"], writes=["ps%d" % ssbank])

    def norm_x_to(dst_fn, dst_res, tbg, gcol0):
        ssb = 4 + rr("ss", 2)
        for half in range(2):
            sq_ones((lambda h_: lambda: x_sb[:, 4 * h_:4 * h_ + 4, tbg * TB:(tbg + 1) * TB])(half), 4, ["x%d" % tbg], ssb, half == 0, half == 1)
        r = rstd_from(ssb, D)
        for c in range(8):
            S.op("dve", (lambda c_: lambda e: e.scalar_tensor_tensor(dst_fn(c_), x_sb[:, c_, tbg * TB:(tbg + 1) * TB], gt[:, gcol0 + c_:gcol0 + c_ + 1], rs[r][:], ALU.mult, ALU.mult))(c),
                 reads=["x%d" % tbg, "gt", "rs%d" % r], writes=[dst_res])

    for l in range(L):
        g0 = l * GT_PER_LAYER
        for tb in range(NTB):
            norm_x_to((lambda tb_: lambda c_: hm[:, c_, tb_ * TB:(tb_ + 1) * TB])(tb), "h%d" % tb, tb, g0 + 0)

        def rotary_store(bank, c, tb, dst_dram, dst_res):
            r = rr("rq", 2)
            S.op("act", lambda e: e.activation(rq[r][:], ps[bank][:, :], AF.Copy), reads=["ps%d" % bank], writes=["rq%d" % r])
            S.op("pe", lambda e: e.matmul(ps[6][:, :], perm, rq[r][:], start=True, stop=True), reads=["rq%d" % r, "cbf"], writes=["ps6"])
            t1 = rr("tmpf", 4); t2 = rr("tmpf", 4)
            S.op("dve", lambda e: e.tensor_tensor(tmpf[t1][:], ps[bank][:, :], cs[:, 0, tb * TB:(tb + 1) * TB], ALU.mult),
                 reads=["ps%d" % bank, "cs"], writes=["tmpf%d" % t1])
            S.op("dve", lambda e: e.tensor_tensor(tmpf[t2][:], ps[6][:, :], cs[:, 1, tb * TB:(tb + 1) * TB], ALU.mult),
                 reads=["ps6", "cs"], writes=["tmpf%d" % t2])
            s = rr("stg", 4)
            S.op("pool", lambda e: e.tensor_tensor(stg[s][:], tmpf[t1][:], tmpf[t2][:], ALU.add),
                 reads=["tmpf%d" % t1, "tmpf%d" % t2], writes=["stg%d" % s])
            S.op("sp", lambda e: e.dma_start(out=dst_dram[c * 128:(c + 1) * 128, tb * TB:(tb + 1) * TB], in_=stg[s][:]),
                 reads=["stg%d" % s], writes=[dst_res], dma=True, tag="stg%d" % s)

        if STOP == "P1a":
            return finish()
        for kind in ("u", "C", "k", "v", "q", "B"):
            if STOP == "P1u" and kind == "C":
                return finish()
            if STOP == "P1C" and kind == "k":
                return finish()
            if STOP == "P1k" and kind == "v":
                return finish()
            ws, slot = w_get()
            wv = slot[:].rearrange("p (kc n) -> p kc n", kc=8)
            for mi in range(4):
                for tb in range(NTB):
                    bank = rr("mm", 4)
                    mm_group(bank, [(wv[:, kc, mi * 128:(mi + 1) * 128], hm[:, kc, tb * TB:(tb + 1) * TB]) for kc in range(8)],
                             ["w%d" % ws, "h%d" % tb])
                    zs = z[:, mi, 1 + tb * TB:1 + (tb + 1) * TB]
                    if kind == "u":
                        S.op("act", (lambda b_, o_: lambda e: e.activation(o_, ps[b_][:, :], AF.Copy))(bank, zs),
                             reads=["ps%d" % bank], writes=["z%d" % mi])
                    elif kind == "C":
                        S.op("dve", (lambda b_, o_: lambda e: e.tensor_tensor(o_, o_, ps[b_][:, :], ALU.mult))(bank, zs),
                             reads=["ps%d" % bank, "z%d" % mi], writes=["z%d" % mi])
                    elif kind == "B":
                        S.op("act", (lambda b_, o_: lambda e: e.activation(o_, ps[b_][:, :], AF.Copy))(bank, bc[:, mi, tb * TB:(tb + 1) * TB]),
                             reads=["ps%d" % bank], writes=["bc%d" % mi])
                    elif kind == "k":
                        rotary_store(bank, mi, tb, snd_k, "snd%d" % mi)
                    elif kind == "q":
                        rotary_store(bank, mi, tb, q_scr, "qs%d" % mi)
                    else:
                        s = rr("stg", 4)
                        S.op("act", (lambda b_, s_: lambda e: e.activation(stg[s_][:], ps[b_][:, :], AF.Copy))(bank, s),
                             reads=["ps%d" % bank], writes=["stg%d" % s])
                        S.op("sp", (lambda s_, mi_, tb_: lambda e: e.dma_start(out=snd_v[mi_ * 128:(mi_ + 1) * 128, tb_ * TB:(tb_ + 1) * TB], in_=stg[s_][:]))(s, mi, tb),
                             reads=["stg%d" % s], writes=["snd%d" % (4 + mi)], dma=True, tag="stg%d" % s)
            w_done()
            if kind == "C":
                for f, c0 in ((0, 1), (1, T - 15)):
                    dst = bass.AP(sndz_t, f * 8192, [[16, 128], [2048, 4], [1, 16]])
                    S.op("sp", (lambda d_, c0_: lambda e: e.dma_start(out=d_, in_=z[:, :, c0_:c0_ + 16]))(dst, c0),
                         reads=["z%d" % i for i in range(4)], writes=["sndz"], dma=True, tag="sndz")
                allgather(snd_z, rcv_z, ["sndz"], "rcvz", "ccz")
            if kind == "k":
                allgather(snd_k, rcv_k, ["snd%d" % i for i in range(4)], "rcvk", "cck")
            if kind == "v":
                allgather(snd_v, rcv_v, ["snd%d" % i for i in range(4, 8)], "rcvv", "ccv")
        if STOP == "P1g":
            return finish()
        for f, (rank, slab) in enumerate(((0, 1), (1, 0))):
            src = bass.AP(rcvz_t, rank * 16384 + slab * 8192, [[16, 128], [2048, 4], [1, 16]])
            S.op("sp", (lambda f_, s_: lambda e: e.dma_start(out=zh[:, f_, :, :], in_=s_))(f, src),
                 reads=["rcvz"], writes=["zh"], dma=True, tag="ldzh")
        S.op("dve", lambda e: e.tensor_scalar(z[:, :, 0:1], zh[:, 0, :, 15:16], cf[:, 1:2], None, ALU.mult),
             reads=["zh", "cf"], writes=["z%d" % i for i in range(4)])
        S.op("dve", lambda e: e.tensor_scalar(z[:, :, T + 1:T + 2], zh[:, 1, :, 0:1], cf[:, 2:3], None, ALU.mult),
             reads=["zh", "cf"], writes=["z%d" % i for i in range(4)])
        for c in range(4):
            for tb in range(NTB):
                t1 = rr("tmpf", 4)
                wc = lambda k, c_=c: gt[:, g0 + 40 + 4 * k + c_:g0 + 40 + 4 * k + c_ + 1]
                zz = lambda k, c_=c, tb_=tb: z[:, c_, tb_ * TB + k:tb_ * TB + k + TB]
                S.op("dve", (lambda t_, zz_, wc_: lambda e: e.tensor_scalar(tmpf[t_][:], zz_(0), wc_(0), None, ALU.mult))(t1, zz, wc),
                     reads=["z%d" % c, "gt"], writes=["tmpf%d" % t1])
                S.op("dve", (lambda t_, zz_, wc_: lambda e: e.scalar_tensor_tensor(tmpf[t_][:], zz_(1), wc_(1), tmpf[t_][:], ALU.mult, ALU.add))(t1, zz, wc),
                     reads=["z%d" % c, "gt", "tmpf%d" % t1], writes=["tmpf%d" % t1])
                S.op("dve", (lambda t_, zz_, wc_: lambda e: e.scalar_tensor_tensor(tmpf[t_][:], zz_(2), wc_(2), tmpf[t_][:], ALU.mult, ALU.add))(t1, zz, wc),
                     reads=["z%d" % c, "gt", "tmpf%d" % t1], writes=["tmpf%d" % t1])
                S.op("pool", (lambda t_, c_, tb_: lambda e: e.tensor_tensor(hm[:, 4 + c_, tb_ * TB:(tb_ + 1) * TB], tmpf[t_][:], bc[:, c_, tb_ * TB:(tb_ + 1) * TB], ALU.mult))(t1, c, tb),
                     reads=["tmpf%d" % t1, "bc%d" % c], writes=["hmB%d" % tb])
        barrier()
        if STOP == "P1":
            return finish()

        S.op("pool", lambda e: e.memset(vt[:], 1.0), writes=["vt"])
        blk = 0
        for hp in range(4):
            S.op("sp", (lambda hp_: lambda e: e.dma_start(out=q_sb[:], in_=q_scr[hp_ * 128:(hp_ + 1) * 128, :]))(hp),
                 reads=["qs%d" % hp], writes=["q_sb"], dma=True, tag="ldq")
            for (buf, sn, rc, rres, soff, res, tg) in ((k_ext, snd_k, rcv_k, "rcvk", 0, "k_ext", "ldk"), (v_ext, snd_v, rcv_v, "rcvv", 4, "v_ext", "ldv")):
                r0 = hp * 128
                S.op("sp", (lambda b_, rc_, r_: lambda e: e.dma_start(out=b_[:, 0:1024], in_=rc_[r_:r_ + 128, 1024:2048]))(buf, rc, r0),
                     reads=[rres], writes=[res], dma=True, tag=tg)
                S.op("sp", (lambda b_, sn_, r_: lambda e: e.dma_start(out=b_[:, 1024:3072], in_=sn_[r_:r_ + 128, :]))(buf, sn, r0),
                     reads=["snd%d" % (soff + hp)], writes=[res], dma=True, tag=tg)
                S.op("sp", (lambda b_, rc_, r_: lambda e: e.dma_start(out=b_[:, 3072:4096], in_=rc_[512 + r_:512 + r_ + 128, 0:1024]))(buf, rc, r0),
                     reads=[rres], writes=[res], dma=True, tag=tg)
            S.op("pool", lambda e: e.tensor_copy(q16[:].rearrange("p (r m) -> p r m", r=16), q_sb[:].rearrange("p (m r) -> p r m", r=16)),
                 reads=["q_sb"], writes=["q16"])
            for d in (1, 4, 16):
                nblk = 16 // d
                ntile = d * (nblk + 1)
                for t0 in range(0, ntile, 8):
                    nt = min(8, ntile - t0)

                    def trfn(e, t0=t0, nt=nt, d=d, nblk=nblk):
                        ins = None
                        for i in range(nt):
                            rho, j = divmod(t0 + i, nblk + 1)
                            e0 = 1024 + rho + d * (128 * j - 64)
                            ins = e.transpose(ps7[:, i * 128:(i + 1) * 128], v_ext[:, sl(e0, 128, d)], ident)
                        return ins
                    S.op("pe", trfn, reads=["v_ext", "cbf"], writes=["ps7"])
                    pv = lambda nt=nt: ps7[:, 0:nt * 128].rearrange("p (t c) -> p t c", c=128)
                    S.op("act", (lambda t0_, nt_, pv_: lambda e: e.activation(vt[:, t0_:t0_ + nt_, 0:64], pv_()[:, :, 0:64], AF.Copy))(t0, nt, pv),
                         reads=["ps7"], writes=["vt"])
                    S.op("dve", (lambda t0_, nt_, pv_: lambda e: e.tensor_copy(vt[:, t0_:t0_ + nt_, 96:160], pv_()[:, :, 64:128]))(t0, nt, pv),
                         reads=["ps7"], writes=["vt"])
                blocks = [(rho, j) for rho in range(d) for j in range(nblk)]
                for bi, (rho, j) in enumerate(blocks):
                    par = blk % 2
                    blk += 1
                    variant = (1 if j == 0 else 0) + (2 if j == nblk - 1 else 0)
                    if d == 16:
                        qsrc = lambda p0, rho=rho: q16[p0:p0 + 64, rho * 128:(rho + 1) * 128]
                    else:
                        qsrc = lambda p0, rho=rho, j=j, d=d: q_sb[p0:p0 + 64, sl(rho + d * 128 * j, 128, d)]
                    eA = 1024 + rho + d * (128 * j - 64)
                    eB = eA + d * 128
                    tA = rho * (nblk + 1) + j
                    for hpar, p0 in ((0, 0), (1, 64)):
                        sbank = 2 * hpar + par

                        def sfn(e, sbank=sbank, p0=p0, eA=eA, eB=eB, d=d, qsrc=qsrc, variant=variant):
                            e.matmul(ps[sbank][:, 0:256], ident, maskv(variant), start=True, stop=False)
                            e.matmul(ps[sbank][:, 0:128], k_ext[p0:p0 + 64, sl(eA, 128, d)], qsrc(p0), start=False, stop=False)
                            return e.matmul(ps[sbank][:, 128:256], k_ext[p0:p0 + 64, sl(eB, 128, d)], qsrc(p0), start=False, stop=True)
                        S.op("pe", sfn, reads=["k_ext", "q16" if d == 16 else "q_sb", "cbf"], writes=["ps%d" % sbank])
                        pb = 2 * hpar + par
                        S.op("act", (lambda pb_, sb_: lambda e: e.activation(pbuf[pb_][:], ps[sb_][:, 0:256], AF.Exp, scale=0.125))(pb, sbank),
                             reads=["ps%d" % sbank], writes=["pb%d" % pb])
                    obank = 4 + (bi // 2) % 2
                    osl = bi % 2
                    for hpar in (0, 1):
                        pb = 2 * hpar + par

                        def ofn(e, hpar=hpar, pb=pb, obank=obank, osl=osl, tA=tA):
                            if hpar == 0:
                                o_ap = ps[obank][0:65, osl * 128:(osl + 1) * 128]
                                la, lb = vt[:, tA, 0:65], vt[:, tA + 1, 0:65]
                            else:
                                o_ap = ps[obank][:, 256 + osl * 128:256 + (osl + 1) * 128]
                                la, lb = vt[:, tA, 32:160], vt[:, tA + 1, 32:160]
                            e.matmul(o_ap, la, pbuf[pb][:, 0:128], start=True, stop=False)
                            return e.matmul(o_ap, lb, pbuf[pb][:, 128:256], start=False, stop=True)
                        S.op("pe", ofn, reads=["vt", "pb%d" % pb], writes=["ps%d" % obank])
                    if bi % 2 == 1 or bi == len(blocks) - 1:
                        nb = 2 if bi % 2 == 1 else 1
                        b0 = bi - (nb - 1)
                        rho0, j0 = blocks[b0]
                        if d == 16:
                            dst = lambda t_, p_, rho0=rho0, nb=nb: t_[p_, :].rearrange("p (m r) -> p r m", r=16)[:, rho0:rho0 + nb, :]
                            src = lambda ap_: ap_.rearrange("p (r m) -> p r m", m=128)
                        else:
                            dst = lambda t_, p_, rho0=rho0, j0=j0, d=d, nb=nb: t_[p_, sl(rho0 + d * 128 * j0, 128 * nb, d)]
                            src = lambda ap_: ap_
                        for hpar in (0, 1):
                            if hpar == 0:
                                pp = slice(0, 65); o_t = oacc_e; c0 = 0; res = "oacc_e"
                            else:
                                pp = slice(0, 128); o_t = oacc_o; c0 = 256; res = "oacc_o"
                            if d == 1:
                                S.op("dve", (lambda o_t_, pp_, c0_, nb_, ob_, dst_, src_: lambda e: e.tensor_copy(dst_(o_t_, pp_), src_(ps[ob_][pp_, c0_:c0_ + 128 * nb_])))(o_t, pp, c0, nb, obank, dst, src),
                                     reads=["ps%d" % obank], writes=[res])
                            else:
                                S.op("dve", (lambda o_t_, pp_, c0_, nb_, ob_, dst_, src_: lambda e: e.tensor_tensor(dst_(o_t_, pp_), dst_(o_t_, pp_), src_(ps[ob_][pp_, c0_:c0_ + 128 * nb_]), ALU.add))(o_t, pp, c0, nb, obank, dst, src),
                                     reads=["ps%d" % obank, res], writes=[res])
            S.op("dve", lambda e: e.reciprocal(oacc_e[64:65, :], oacc_e[64:65, :]), reads=["oacc_e"], writes=["oacc_e"])
            S.op("dve", lambda e: e.reciprocal(oacc_o[32:33, :], oacc_o[32:33, :]), reads=["oacc_o"], writes=["oacc_o"])
            for tb in range(NTB):
                cs_ = slice(tb * TB, (tb + 1) * TB)
                S.op("pe", (lambda c_: lambda e: e.matmul(ps[6][0:64, :], onesf[64:65, 0:64], oacc_e[64:65, c_], start=True, stop=True))(cs_),
                     reads=["oacc_e", "onesf"], writes=["ps6"])
                S.op("pe", (lambda c_: lambda e: e.matmul(ps[0][:, :], selo[32:33, :], oacc_o[32:33, c_], start=True, stop=True))(cs_),
                     reads=["oacc_o", "selo"], writes=["ps0"])
                S.op("dve", (lambda c_, hp_: lambda e: e.tensor_tensor(hm[0:64, hp_, c_], oacc_e[0:64, c_], ps[6][0:64, :], ALU.mult))(cs_, hp),
                     reads=["oacc_e", "ps6"], writes=["hmA%d" % tb])
                S.op("dve", (lambda c_, hp_: lambda e: e.tensor_tensor(hm[64:128, hp_, c_], oacc_o[64:128, c_], ps[0][64:128, :], ALU.mult))(cs_, hp),
                     reads=["oacc_o", "ps0"], writes=["hmA%d" % tb])
        barrier()
        if STOP == "P2":
            return finish()

        i0 = wcur["i"]
        w_issue(i0 + 1)
        ws0, ws1 = i0 % NW, (i0 + 1) % NW
        slot0, slot1 = wsl[ws0], wsl[ws1]
        wo = [slot0[:].rearrange("p (kc n) -> p kc n", kc=8), slot1[:].rearrange("p (kc n) -> p kc n", kc=8)]
        wres = ["w%d" % ws0, "w%d" % ws1]
        for tb in range(NTB):
            cs_ = slice(tb * TB, (tb + 1) * TB)
            rr_ = []
            for grp, gcol in ((0, g0 + 8), (1, g0 + 12)):
                ssb = 4 + rr("ss", 2)
                sq_ones((lambda g_, c_: lambda: hm[:, 4 * g_:4 * g_ + 4, c_])(grp, cs_), 4, ["hmA%d" % tb, "hmB%d" % tb], ssb, True, True)
                r = rstd_from(ssb, 512)
                for c in range(4):
                    cc = 4 * grp + c
                    S.op("dve", (lambda cc_, c_, r_, gc_: lambda e: e.scalar_tensor_tensor(hm[:, cc_, c_], hm[:, cc_, c_], gt[:, gc_:gc_ + 1], rs[r_][:], ALU.mult, ALU.mult))(cc, cs_, r, gcol + c),
                         reads=["hmA%d" % tb, "hmB%d" % tb, "gt", "rs%d" % r], writes=["hmA%d" % tb if grp == 0 else "hmB%d" % tb])
            ssb = 4 + rr("ss", 2)
            for m in range(8):
                bank = rr("mm", 4)
                mm_group(bank, [(wo[m // 4][:, kc, (m % 4) * 128:(m % 4 + 1) * 128], hm[:, kc, cs_]) for kc in range(8)],
                         [wres[m // 4], "hmA%d" % tb, "hmB%d" % tb])
                S.op("act", (lambda b_, m_: lambda e: e.activation(mixbuf[:, m_, :], ps[b_][:, :], AF.Copy))(bank, m),
                     reads=["ps%d" % bank], writes=["mixbuf"])
                sqs = 0 if (cnt["sq"] % 2 == 0) else 4
                cnt["sq"] += 1
                S.op("act", (lambda b_, s_: lambda e: e.activation(sq[:, s_, :], ps[b_][:, :], AF.Square))(bank, sqs),
                     reads=["ps%d" % bank], writes=["sq%d" % sqs])
                S.op("pe", (lambda s_, m_, sb_: lambda e: e.matmul(ps[sb_][:, :], ones_bf, sq[:, s_, :], start=(m_ == 0), stop=(m_ == 7)))(sqs, m, ssb),
                     reads=["sq%d" % sqs, "cbf"], writes=["ps%d" % ssb])
            r = rstd_from(ssb, D)
            for m in range(8):
                t1 = rr("tmpf", 4)
                S.op("dve", (lambda t_, m_, r_: lambda e: e.scalar_tensor_tensor(tmpf[t_][:], mixbuf[:, m_, :], gt[:, g0 + 16 + m_:g0 + 17 + m_], rs[r_][:], ALU.mult, ALU.mult))(t1, m, r),
                     reads=["mixbuf", "gt", "rs%d" % r], writes=["tmpf%d" % t1])
                S.op("pool", (lambda t_, m_, c_: lambda e: e.tensor_tensor(x_sb[:, m_, c_], x_sb[:, m_, c_], tmpf[t_][:], ALU.add))(t1, m, cs_),
                     reads=["tmpf%d" % t1, "x%d" % tb], writes=["x%d" % tb])
        w_done(); w_done()
        barrier()
        if STOP == "P3":
            return finish()

        for hf in range(2):
            for tb2 in range(2):
                tbg = hf * 2 + tb2
                norm_x_to((lambda tb2_: lambda c_: h2[:, c_, tb2_ * TB:(tb2_ + 1) * TB])(tb2), "h2_%d" % tb2, tbg, g0 + 24)
            for f in range(11):
                ws, slot = w_get()
                wv = slot[:].rearrange("p (kc n) -> p kc n", kc=8)
                for tb2 in range(2):
                    for mi in range(2):
                        bg = rr("mm", 4)
                        mm_group(bg, [(wv[:, kc, mi * 128:(mi + 1) * 128], h2[:, kc, tb2 * TB:(tb2 + 1) * TB]) for kc in range(8)],
                                 ["w%d" % ws, "h2_%d" % tb2])
                        bu = rr("mm", 4)
                        mm_group(bu, [(wv[:, kc, 256 + mi * 128:256 + (mi + 1) * 128], h2[:, kc, tb2 * TB:(tb2 + 1) * TB]) for kc in range(8)],
                                 ["w%d" % ws, "h2_%d" % tb2])
                        t1 = rr("tmpf", 4)
                        S.op("act", (lambda t_, b_: lambda e: e.activation(tmpf[t_][:], ps[b_][:, :], AF.Silu))(t1, bg),
                             reads=["ps%d" % bg], writes=["tmpf%d" % t1])
                        S.op("dve", (lambda t_, b_, fc_, tb2_: lambda e: e.tensor_tensor(hid[:, fc_, tb2_ * TB:(tb2_ + 1) * TB], tmpf[t_][:], ps[b_][:, :], ALU.mult))(t1, bu, 2 * f + mi, tb2),
                             reads=["tmpf%d" % t1, "ps%d" % bu], writes=["hid%d" % tb2])
                w_done()
            ssbs = [4, 5]
            for mp in range(4):
                slots = []
                for kh in range(2):
                    ws, slot = w_get()
                    wv = slot[:, 0:2816].rearrange("p (kc n) -> p kc n", kc=11)
                    for mi in range(2):
                        for tb2 in range(2):
                            bank = mi * 2 + tb2

                            def dfn(e, wv=wv, mi=mi, tb2=tb2, bank=bank, kh=kh):
                                ins = None
                                for kk in range(11):
                                    ins = e.matmul(ps[bank][:, :], wv[:, kk, mi * 128:(mi + 1) * 128], hid[:, kh * 11 + kk, tb2 * TB:(tb2 + 1) * TB],
                                                   start=(kh == 0 and kk == 0), stop=(kh == 1 and kk == 10))
                                return ins
                            S.op("pe", dfn, reads=["w%d" % ws, "hid%d" % tb2], writes=["ps%d" % bank])
                    w_done()
                for mi in range(2):
                    for tb2 in range(2):
                        bank = mi * 2 + tb2
                        m = mp * 2 + mi
                        S.op("act", (lambda b_, m_, tb2_: lambda e: e.activation(f_bf[:, m_, tb2_ * TB:(tb2_ + 1) * TB], ps[b_][:, :], AF.Copy))(bank, m, tb2),
                             reads=["ps%d" % bank], writes=["f_bf%d" % tb2])
                        sqs = 0 if (cnt["sq"] % 2 == 0) else 4
                        cnt["sq"] += 1
                        S.op("act", (lambda b_, s_: lambda e: e.activation(sq[:, s_, :], ps[b_][:, :], AF.Square))(bank, sqs),
                             reads=["ps%d" % bank], writes=["sq%d" % sqs])
                        S.op("pe", (lambda s_, m_, sb_: lambda e: e.matmul(ps[sb_][:, :], ones_bf, sq[:, s_, :], start=(m_ == 0), stop=(m_ == 7)))(sqs, m, ssbs[tb2]),
                             reads=["sq%d" % sqs, "cbf"], writes=["ps%d" % ssbs[tb2]])
            for tb2 in range(2):
                tbg = hf * 2 + tb2
                r = rstd_from(ssbs[tb2], D)
                for m in range(8):
                    t1 = rr("tmpf", 4)
                    S.op("dve", (lambda t_, m_, r_, tb2_: lambda e: e.scalar_tensor_tensor(tmpf[t_][:], f_bf[:, m_, tb2_ * TB:(tb2_ + 1) * TB], gt[:, g0 + 32 + m_:g0 + 33 + m_], rs[r_][:], ALU.mult, ALU.mult))(t1, m, r, tb2),
                         reads=["f_bf%d" % tb2, "gt", "rs%d" % r], writes=["tmpf%d" % t1])
                    S.op("pool", (lambda t_, m_, tbg_: lambda e: e.tensor_tensor(x_sb[:, m_, tbg_ * TB:(tbg_ + 1) * TB], x_sb[:, m_, tbg_ * TB:(tbg_ + 1) * TB], tmpf[t_][:], ALU.add))(t1, m, tbg),
                         reads=["tmpf%d" % t1, "x%d" % tbg], writes=["x%d" % tbg])
        barrier()

    return finish()


def _const_tables(validP, validN):
    cbf = np.zeros((128, 384 + 1024), np.float32)
    cbf[:, 0:128] = np.eye(128)
    cbf[:, 128:256] = 1.0
    perm = np.zeros((128, 128), np.float32)
    for par in range(2):
        for dd in range(8):
            p = par * 64 + dd
            perm[p + 8, p] = -1.0
            perm[p, p + 8] = 1.0
    cbf[:, 256:384] = perm
    a = np.arange(128)[:, None]
    b = np.arange(128)[None, :]
    mA = np.where(b <= a, 0.0, NEG)
    mB = np.where(b >= a, 0.0, NEG)
    for v in range(4):
        ma = mA.copy(); mb = mB.copy()
        if (v & 1) and not validP:
            ma[0:64, :] = NEG
        if (v & 2) and not validN:
            mb[64:128, :] = NEG
        cbf[:, 384 + 256 * v:384 + 256 * v + 128] = ma
        cbf[:, 384 + 256 * v + 128:384 + 256 * (v + 1)] = mb
    cf = np.zeros((128, 4), np.float32)
    inv = np.float32(500000.0) ** (-(np.arange(0, 16, 2, dtype=np.float32)) / np.float32(16.0))
    for p in range(128):
        dd = p % 64
        if dd < 16:
            cf[p, 0] = inv[dd % 8]
    cf[:, 1] = 1.0 if validP else 0.0
    cf[:, 2] = 1.0 if validN else 0.0
    return cbf.astype(NPBF), cf


def _gain_table(layers, pre_mix_norm, conv_w, attn_out_norm, conv_out_norm, post_mix_norm, pre_ffn_norm, post_ffn_norm):
    gtab = np.zeros((128, len(layers) * GT_PER_LAYER), np.float32)

    def put(col, vec):
        n = vec.shape[0] // 128
        gtab[:, col:col + n] = vec.reshape(n, 128).T
    for i, l in enumerate(layers):
        b = i * GT_PER_LAYER
        put(b + 0, pre_mix_norm[l]); put(b + 8, attn_out_norm[l]); put(b + 12, conv_out_norm[l])
        put(b + 16, post_mix_norm[l]); put(b + 24, pre_ffn_norm[l]); put(b + 32, post_ffn_norm[l])
        for k in range(3):
            put(b + 40 + 4 * k, conv_w[l, k])
    return gtab


_PROG = {}


def _run(L, layers, xTs, positions, P):
    if L not in _PROG:
        _PROG[L] = build_program(L)
    nc = _PROG[L]
    gtab = _gain_table(layers, P["pre_mix_norm"], P["conv_w"], P["attn_out_norm"], P["conv_out_norm"],
                       P["post_mix_norm"], P["pre_ffn_norm"], P["post_ffn_norm"])
    lsel = slice(layers[0], layers[-1] + 1)
    wi = np.ascontiguousarray(P["w_in"][lsel]); wo = np.ascontiguousarray(P["w_out"][lsel])
    wg = np.ascontiguousarray(P["w_gate_up"][lsel]); wd = np.ascontiguousarray(P["w_down"][lsel])
    in_maps = []
    for c in range(8):
        b, r = divmod(c, 2)
        cbf, cf = _const_tables(validP=(r == 1), validN=(r == 0))
        in_maps.append({
            "xT": xTs[c],
            "pos": np.ascontiguousarray(positions[b:b + 1, r * T:(r + 1) * T]).astype(np.int32),
            "w_in": wi, "w_out": wo, "w_gu": wg, "w_dn": wd,
            "gt": gtab, "cbf": cbf, "cf": cf,
        })
    res = run_bass_kernel_spmd(nc, in_maps, core_ids=list(range(8)))
    return [np.asarray(res.results[c]["yT"]) for c in range(8)]


def kernel(x, positions, pre_mix_norm, w_in, conv_w, attn_out_norm, conv_out_norm, w_out,
           post_mix_norm, pre_ffn_norm, w_gate_up, w_down, post_ffn_norm):
    P = dict(pre_mix_norm=np.asarray(pre_mix_norm, np.float32), w_in=np.asarray(w_in, np.float32),
             conv_w=np.asarray(conv_w, np.float32), attn_out_norm=np.asarray(attn_out_norm, np.float32),
             conv_out_norm=np.asarray(conv_out_norm, np.float32), w_out=np.asarray(w_out, np.float32),
             post_mix_norm=np.asarray(post_mix_norm, np.float32), pre_ffn_norm=np.asarray(pre_ffn_norm, np.float32),
             w_gate_up=np.asarray(w_gate_up, np.float32), w_down=np.asarray(w_down, np.float32),
             post_ffn_norm=np.asarray(post_ffn_norm, np.float32))
    x = np.asarray(x, np.float32)
    positions = np.asarray(positions)
    xTs = []
    for c in range(8):
        b, r = divmod(c, 2)
        xTs.append(np.ascontiguousarray(x[b, r * T:(r + 1) * T, :].T))
    if FUSED:
        xTs = _run(DEPTH, list(range(DEPTH)), xTs, positions, P)
    else:
        for l in range(DEPTH):
            xTs = _run(1, [l], xTs, positions, P)
    out = np.empty((4, 4096, D), np.float32)
    for c in range(8):
        b, r = divmod(c, 2)
        out[b, r * T:(r + 1) * T, :] = xTs[c].T
    return out
```

```python
import collections
import contextlib
import math

import numpy as np
import ml_dtypes

import concourse.bass as bass
import concourse.mybir as mybir
from concourse.bass_utils import run_bass_kernel_spmd

F32 = mybir.dt.float32
BF16 = mybir.dt.bfloat16
I32 = mybir.dt.int32
ALU = mybir.AluOpType
AF = mybir.ActivationFunctionType
NPBF = ml_dtypes.bfloat16

FUSED = True
DEPTH = 4
D = 1024
T = 2048
TB = 512
NTB = 4
FFN = 2816
NEG = -30000.0
EPS = 1e-6
GT_PER_LAYER = 52
REPLICA_GROUPS = [[0, 1], [2, 3], [4, 5], [6, 7]]
STOP = None
ENG_NAMES = ("pe", "act", "dve", "pool", "sp")


def sl(start, n, step=1):
    return slice(start, start + (n - 1) * step + 1, step)


class Op:
    __slots__ = ("idx", "eng", "fn", "reads", "writes", "dma", "tag", "inc", "count",
                 "waits", "marked", "known", "kdma")

    def __init__(self, idx, eng, fn, reads, writes, dma, tag, inc):
        self.idx = idx; self.eng = eng; self.fn = fn
        self.reads = tuple(reads); self.writes = tuple(writes)
        self.dma = dma; self.tag = tag; self.inc = inc
        self.count = 0; self.waits = []; self.marked = False
        self.known = None; self.kdma = None


class Sched:
    def __init__(self):
        self.ops = []
        self.last_write = {}
        self.readers = collections.defaultdict(dict)
        self.tag_count = collections.defaultdict(int)
        self.tag_inc = {}
        self.eng_known = {e: {} for e in ENG_NAMES}
        self.eng_kdma = {e: {} for e in ENG_NAMES}
        self.eng_ops = {e: [] for e in ENG_NAMES}

    def op(self, eng, fn, reads=(), writes=(), dma=False, tag=None, inc=16, nophase=False):
        i = len(self.ops)
        reads = list(reads); writes = list(writes)
        if not nophase:
            reads.append("PHASE")
        for r in list(reads):
            if r.startswith("ps"):
                reads.remove(r)
                if r not in writes:
                    writes.append(r)
        o = Op(i, eng, fn, reads, writes, dma, tag, inc)
        self.ops.append(o)
        deps = {}
        for r in o.reads:
            j = self.last_write.get(r)
            if j is not None:
                deps[j] = "RAW"
        for r in o.writes:
            j = self.last_write.get(r)
            if j is not None and j not in deps:
                deps[j] = "WAW"
            for j in self.readers[r].values():
                if j not in deps:
                    deps[j] = "WAR"
        known = self.eng_known[eng]
        kdma = self.eng_kdma[eng]
        waits = {}
        for j, kind in sorted(deps.items()):
            d = self.ops[j]
            if d.dma:
                if kdma.get(d.tag, 0) >= d.count:
                    continue
                c = self.tag_count[d.tag]
                waits[("dma", d.tag)] = c
                kdma[d.tag] = c
            else:
                if d.eng == eng:
                    if eng in ("pe", "sp") or kind != "RAW":
                        continue
                if known.get(d.eng, -1) >= j:
                    continue
                waits[("eng", d.eng)] = max(waits.get(("eng", d.eng), -1), j)
        for key, v in waits.items():
            if key[0] == "eng":
                d = self.ops[v]
                d.marked = True
                known[key[1]] = max(known.get(key[1], -1), v)
                for e2, k2 in d.known.items():
                    if known.get(e2, -1) < k2:
                        known[e2] = k2
                for t2, c2 in d.kdma.items():
                    if kdma.get(t2, 0) < c2:
                        kdma[t2] = c2
        o.waits = list(waits.items())
        if dma:
            assert self.tag_inc.setdefault(tag, inc) == inc
            self.tag_count[tag] += 1
            o.count = self.tag_count[tag]
        o.known = dict(known)
        o.kdma = dict(kdma)
        for r in o.reads:
            self.readers[r][("dma", i) if dma else eng] = i
        for r in o.writes:
            self.last_write[r] = i
            self.readers[r] = {}
        self.eng_ops[eng].append(o)
        return o

    def emit(self, nc, final_wait_tags=()):
        rank = {}
        for e in ENG_NAMES:
            c = 0
            for o in self.eng_ops[e]:
                if o.marked and not o.dma:
                    c += 1
                    rank[o.idx] = c
        with contextlib.ExitStack() as st:
            esem = {e: st.enter_context(nc.semaphore("s_" + e)) for e in ("pe", "act", "dve", "pool")}
            tsem = {t: st.enter_context(nc.semaphore("t_" + str(t))) for t in sorted(self.tag_count)}
            block = st.enter_context(nc.Block())
            handles = {"pe": block.tensor, "act": block.scalar, "dve": block.vector,
                       "pool": block.gpsimd, "sp": block.sync}

            def body(ename):
                def _f(eng):
                    for o in self.eng_ops[ename]:
                        for key, v in o.waits:
                            if key[0] == "eng":
                                eng.wait_ge(esem[key[1]], rank[v])
                            else:
                                eng.wait_ge(tsem[key[1]], v * self.tag_inc[key[1]])
                        ins = o.fn(eng)
                        if o.dma:
                            ins.then_inc(tsem[o.tag], o.inc)
                        elif o.marked:
                            ins.then_inc(esem[ename], 1)
                    if ename == "sp":
                        for t in final_wait_tags:
                            eng.wait_ge(tsem[t], self.tag_count[t] * self.tag_inc[t])
                return _f

            for e in ENG_NAMES:
                handles[e](body(e))


class SbAlloc:
    BASE = 16512
    LIMIT = 229344

    def __init__(self, nc):
        self.nc = nc
        self.off = self.BASE
        self.n = 0

    def at(self, name, shape, dtype, offset):
        nbytes = int(np.prod(shape[1:])) * (2 if dtype == BF16 else 4)
        assert offset % 32 == 0 and offset + nbytes <= self.LIMIT, (name, offset, nbytes)
        self.n += 1
        return self.nc.alloc_sbuf_tensor_at("%s_%d" % (name, self.n), list(shape), dtype, offset=offset), nbytes

    def new(self, name, shape, dtype):
        t, nbytes = self.at(name, shape, dtype, self.off)
        self.off += (nbytes + 31) // 32 * 32
        return t


def build_program(L):
    nc = bass.Bass("TRN2", target_bir_lowering=False)
    S = Sched()

    def dram(name, shape, dt, kind):
        return nc.dram_tensor(name, list(shape), dt, kind=kind)

    xT = dram("xT", [D, T], F32, "ExternalInput").ap()
    pos = dram("pos", [1, T], I32, "ExternalInput").ap()
    w_in = dram("w_in", [L, D, 3072], F32, "ExternalInput").ap()
    w_out = dram("w_out", [L, D, D], F32, "ExternalInput").ap()
    w_gu = dram("w_gu", [L, D, 2 * FFN], F32, "ExternalInput").ap()
    w_dn = dram("w_dn", [L, FFN, D], F32, "ExternalInput").ap()
    gt_d = dram("gt", [128, L * GT_PER_LAYER], F32, "ExternalInput").ap()
    cbf_d = dram("cbf", [128, 384 + 1024], BF16, "ExternalInput").ap()
    cf_d = dram("cf", [128, 4], F32, "ExternalInput").ap()
    yT = dram("yT", [D, T], F32, "ExternalOutput").ap()
    q_scr_t = dram("q_scr", [512, T], BF16, "Internal")
    sndk_t = dram("snd_k", [512, T], BF16, "Internal")
    sndv_t = dram("snd_v", [512, T], BF16, "Internal")
    sndz_t = dram("snd_z", [8, T], BF16, "Internal")
    rcvk_t = dram("rcv_k", [1024, T], BF16, "Internal")
    rcvv_t = dram("rcv_v", [1024, T], BF16, "Internal")
    rcvz_t = dram("rcv_z", [16, T], BF16, "Internal")
    q_scr = q_scr_t.ap()
    snd_k, snd_v, snd_z = sndk_t.ap(), sndv_t.ap(), sndz_t.ap()
    rcv_k, rcv_v, rcv_z = rcvk_t.ap(), rcvv_t.ap(), rcvz_t.ap()

    def allgather(src, dst, reads, wres, tag):
        S.op("pool", lambda e: e.collective_compute("AllGather", ALU.bypass, replica_groups=REPLICA_GROUPS, ins=[src], outs=[dst]),
             reads=reads, writes=[wres], dma=True, tag=tag, inc=1)

    A = SbAlloc(nc)
    x_sb = A.new("x", [128, 8, T], F32)
    cs = A.new("cs", [128, 2, T], BF16)
    cbf = A.new("cbf", [128, 384 + 1024], BF16)
    cf = A.new("cf", [128, 4], F32)
    gt = A.new("gt", [128, L * GT_PER_LAYER], F32)
    onesf = A.new("onesf", [128, 128], F32)
    selo = A.new("selo", [128, 128], F32)
    bar = A.new("bar", [128, 8], F32)
    NW = 3
    wsl = [A.new("w%d" % i, [128, 4096], BF16) for i in range(NW)]
    sq = A.new("sq", [128, 8, TB], BF16)
    rs = [A.new("rs%d" % i, [128, TB], F32) for i in range(2)]
    rq = [A.new("rq%d" % i, [128, TB], BF16) for i in range(2)]
    stg = [A.new("stg%d" % i, [128, TB], BF16) for i in range(4)]
    zh = A.new("zh", [128, 2, 4, 16], BF16)
    regB = A.off
    tmpf = [A.new("tmpf%d" % i, [128, TB], F32) for i in range(4)]
    oacc_e, _ = A.at("oacc_e", [128, T], F32, regB)
    arena = A.off
    hm, _ = A.at("hm", [128, 8, T], BF16, arena)
    regA = arena + 32768
    z, _ = A.at("z", [128, 4, T + 2], BF16, regA)
    bc, _ = A.at("bc", [128, 4, T], BF16, regA + 16416)
    k_ext, _ = A.at("k_ext", [128, 4096], BF16, regA)
    v_ext, _ = A.at("v_ext", [128, 4096], BF16, regA + 8192)
    vt, _ = A.at("vt", [128, 32, 160], BF16, regA + 16384)
    q_sb, _ = A.at("q_sb", [128, T], BF16, regA + 16384 + 10240)
    o2 = regA + 32800
    q16, _ = A.at("q16", [128, T], BF16, o2)
    pbuf = [A.at("pb%d" % i, [128, 256], BF16, o2 + 4096 + 512 * i)[0] for i in range(4)]
    oacc_o, _ = A.at("oacc_o", [128, T], F32, o2 + 4096 + 2048)
    mixbuf, _ = A.at("mixbuf", [128, 8, TB], BF16, regA)
    h2, _ = A.at("h2", [128, 8, 1024], BF16, arena)
    hid, _ = A.at("hid", [128, 22, 1024], BF16, arena + 16384)
    f_bf, _ = A.at("f_bf", [128, 8, 1024], BF16, arena + 16384 + 45056)
    assert arena + 16384 + 45056 + 16384 <= A.LIMIT
    su_i, _ = A.at("su_i", [128, T], I32, arena)
    su_ang, _ = A.at("su_ang", [128, T], F32, arena + 8192)
    su_a2, _ = A.at("su_a2", [128, T], F32, arena + 16384)
    su_kf, _ = A.at("su_kf", [128, T], F32, arena + 24576)
    su_ki, _ = A.at("su_ki", [128, T], I32, arena + 32768)
    su_r, _ = A.at("su_r", [128, T], F32, arena + 40960)

    ps = [nc.alloc_psum_tensor("ps%d" % i, [128, TB], F32) for i in range(7)]
    ps7 = nc.alloc_psum_tensor("ps7", [128, 1024], BF16)

    ident = cbf[:, 0:128]
    ones_bf = cbf[:, 128:256]
    perm = cbf[:, 256:384]

    def maskv(v):
        return cbf[:, 384 + 256 * v: 384 + 256 * (v + 1)]

    def barrier():
        S.op("pool", lambda e: e.memset(bar[:], 0.0), writes=["PHASE"], nophase=True)

    def finish():
        for c in range(8):
            S.op("sp", (lambda c_: lambda e: e.dma_start(out=yT[c_ * 128:(c_ + 1) * 128, :], in_=x_sb[:, c_, :]))(c),
                 reads=["x%d" % tb for tb in range(NTB)], dma=True, tag="yout")
        S.emit(nc, final_wait_tags=["yout"])
        return nc

    S.op("sp", lambda e: e.dma_start(out=cbf[:], in_=cbf_d), writes=["cbf"], dma=True, tag="c0")
    S.op("sp", lambda e: e.dma_start(out=cf[:], in_=cf_d), writes=["cf"], dma=True, tag="c0")
    S.op("sp", lambda e: e.dma_start(out=gt[:], in_=gt_d), writes=["gt"], dma=True, tag="c0")
    S.op("sp", lambda e: e.dma_start(out=su_i[:], in_=pos.partition_broadcast(128)), writes=["su_i"], dma=True, tag="c1")
    for c in range(8):
        S.op("sp", (lambda c: lambda e: e.dma_start(out=x_sb[:, c, :], in_=xT[c * 128:(c + 1) * 128, :]))(c),
             writes=["x%d" % tb for tb in range(NTB)], dma=True, tag="xin")
    S.op("pool", lambda e: e.memset(onesf[:], 1.0), writes=["onesf"])
    S.op("pool", lambda e: e.memset(selo[:, 0:64], 0.0), writes=["selo"])
    S.op("pool", lambda e: e.memset(selo[:, 64:128], 1.0), writes=["selo"])
    TWO_PI = 2.0 * math.pi
    S.op("dve", lambda e: e.tensor_copy(su_ang[:], su_i[:]), reads=["su_i"], writes=["su_ang"])
    S.op("dve", lambda e: e.tensor_scalar(su_ang[:], su_ang[:], cf[:, 0:1], None, ALU.mult), reads=["su_ang", "cf"], writes=["su_ang"])
    for which, shift in ((1, 0.0), (0, 0.5 * math.pi)):
        S.op("dve", (lambda sh: lambda e: e.tensor_scalar(su_a2[:], su_ang[:], sh, None, ALU.add))(shift), reads=["su_ang"], writes=["su_a2"])
        S.op("dve", lambda e: e.tensor_scalar(su_kf[:], su_a2[:], 1.0 / TWO_PI, None, ALU.mult), reads=["su_a2"], writes=["su_kf"])
        S.op("dve", lambda e: e.tensor_copy(su_ki[:], su_kf[:]), reads=["su_kf"], writes=["su_ki"])
        S.op("dve", lambda e: e.tensor_copy(su_kf[:], su_ki[:]), reads=["su_ki"], writes=["su_kf"])
        S.op("dve", lambda e: e.scalar_tensor_tensor(su_r[:], su_kf[:], -TWO_PI, su_a2[:], ALU.mult, ALU.add), reads=["su_kf", "su_a2"], writes=["su_r"])
        S.op("dve", lambda e: e.tensor_scalar(su_kf[:], su_r[:], math.pi, TWO_PI, ALU.is_gt, ALU.mult), reads=["su_r"], writes=["su_kf"])
        S.op("dve", lambda e: e.tensor_tensor(su_r[:], su_r[:], su_kf[:], ALU.subtract), reads=["su_r", "su_kf"], writes=["su_r"])
        S.op("dve", lambda e: e.tensor_scalar(su_kf[:], su_r[:], -math.pi, TWO_PI, ALU.is_lt, ALU.mult), reads=["su_r"], writes=["su_kf"])
        S.op("dve", lambda e: e.tensor_tensor(su_r[:], su_r[:], su_kf[:], ALU.add), reads=["su_r", "su_kf"], writes=["su_r"])
        S.op("act", (lambda w: lambda e: e.activation(cs[:, w, :], su_r[:], AF.Sin))(which), reads=["su_r"], writes=["cs"])
    barrier()
    if STOP == "setup":
        return finish()

    wtiles = []
    for l in range(L):
        for c0 in (1536, 2560, 512, 1024, 0, 2048):
            wtiles.append(("in", l, c0))
        for c0 in (0, 512):
            wtiles.append(("out", l, c0))
        for hf in range(2):
            for f in range(11):
                wtiles.append(("gu", l, f))
            for mp in range(4):
                for kh in range(2):
                    wtiles.append(("dn", l, (mp, kh)))
    wstate = {"issued": 0}

    def w_issue(upto):
        while wstate["issued"] <= min(upto, len(wtiles) - 1):
            i = wstate["issued"]
            kind, l, a = wtiles[i]
            s = i % NW
            slot = wsl[s]
            if kind in ("in", "out"):
                src = (w_in if kind == "in" else w_out)[l].rearrange("(kc p) n -> p kc n", p=128)[:, :, a:a + 512]
                dst = slot[:].rearrange("p (kc n) -> p kc n", kc=8)
                S.op("pool", (lambda d_, s_: lambda e: e.dma_start(out=d_, in_=s_))(dst, src),
                     writes=["w%d" % s], dma=True, tag="w%d" % s, nophase=True)
            elif kind == "gu":
                v = w_gu[l].rearrange("(kc p) n -> p kc n", p=128)
                dst = slot[:].rearrange("p (kc n) -> p kc n", kc=8)
                S.op("pool", (lambda d_, s_: lambda e: e.dma_start(out=d_, in_=s_))(dst[:, :, 0:256], v[:, :, a * 256:(a + 1) * 256]),
                     writes=["w%d" % s], dma=True, tag="w%d" % s, nophase=True)
                S.op("pool", (lambda d_, s_: lambda e: e.dma_start(out=d_, in_=s_))(dst[:, :, 256:512], v[:, :, FFN + a * 256:FFN + (a + 1) * 256]),
                     writes=["w%d" % s], dma=True, tag="w%d" % s, nophase=True)
            else:
                mp, kh = a
                v = w_dn[l].rearrange("(kc p) n -> p kc n", p=128)
                dst = slot[:, 0:2816].rearrange("p (kc n) -> p kc n", kc=11)
                S.op("pool", (lambda d_, s_: lambda e: e.dma_start(out=d_, in_=s_))(dst, v[:, kh * 11:(kh + 1) * 11, mp * 256:(mp + 1) * 256]),
                     writes=["w%d" % s], dma=True, tag="w%d" % s, nophase=True)
            wstate["issued"] += 1

    wcur = {"i": 0}

    def w_get():
        i = wcur["i"]
        w_issue(i)
        return i % NW, wsl[i % NW]

    def w_done():
        wcur["i"] += 1
        w_issue(wcur["i"] - 1 + NW)

    w_issue(NW - 1)

    cnt = {"mm": 0, "ss": 0, "rs": 0, "sq": 0, "tmpf": 0, "stg": 0, "rq": 0}

    def rr(key, n):
        v = cnt[key] % n
        cnt[key] += 1
        return v

    def mm_group(bank, pairs, reads):
        def fn(e):
            ins = None
            n = len(pairs)
            for i, (a_, b_) in enumerate(pairs):
                ins = e.matmul(ps[bank][:, :], a_, b_, start=(i == 0), stop=(i == n - 1))
            return ins
        S.op("pe", fn, reads=reads, writes=["ps%d" % bank])

    def rstd_from(ssbank, dim):
        r = rr("rs", 2)
        S.op("dve", lambda e: e.tensor_scalar(rs[r][:], ps[ssbank][:, :], 1.0 / dim, EPS, ALU.mult, ALU.add),
             reads=["ps%d" % ssbank], writes=["rs%d" % r])
        S.op("act", lambda e: e.activation(rs[r][:], rs[r][:], AF.Sqrt), reads=["rs%d" % r], writes=["rs%d" % r])
        S.op("dve", lambda e: e.reciprocal(rs[r][:], rs[r][:]), reads=["rs%d" % r], writes=["rs%d" % r])
        return r

    def sq_ones(src_ap_fn, nch, src_res, ssbank, first, last):
        base = 0 if (cnt["sq"] % 2 == 0) else 4
        cnt["sq"] += 1
        assert nch <= 4
        S.op("act", lambda e: e.activation(sq[:, base:base + nch, :], src_ap_fn(), AF.Square),
             reads=src_res, writes=["sq%d" % base])

        def fn(e):
            ins = None
            for c in range(nch):
                ins = e.matmul(ps[ssbank][:, :], ones_bf, sq[:, base + c, :], start=(first and c == 0), stop=(last and c == nch - 1))
            return ins
        S.op("pe", fn, reads=["sq%d" % base, "cbf"], writes=["ps%d" % ssbank])

    def norm_x_to(dst_fn, dst_res, tbg, gcol0):
        ssb = 4 + rr("ss", 2)
        for half in range(2):
            sq_ones((lambda h_: lambda: x_sb[:, 4 * h_:4 * h_ + 4, tbg * TB:(tbg + 1) * TB])(half), 4, ["x%d" % tbg], ssb, half == 0, half == 1)
        r = rstd_from(ssb, D)
        for c in range(8):
            S.op("dve", (lambda c_: lambda e: e.scalar_tensor_tensor(dst_fn(c_), x_sb[:, c_, tbg * TB:(tbg + 1) * TB], gt[:, gcol0 + c_:gcol0 + c_ + 1], rs[r][:], ALU.mult, ALU.mult))(c),
                 reads=["x%d" % tbg, "gt", "rs%d" % r], writes=[dst_res])

    for l in range(L):
        g0 = l * GT_PER_LAYER
        for tb in range(NTB):
            norm_x_to((lambda tb_: lambda c_: hm[:, c_, tb_ * TB:(tb_ + 1) * TB])(tb), "h%d" % tb, tb, g0 + 0)

        def rotary_store(bank, c, tb, dst_dram, dst_res):
            r = rr("rq", 2)
            S.op("act", lambda e: e.activation(rq[r][:], ps[bank][:, :], AF.Copy), reads=["ps%d" % bank], writes=["rq%d" % r])
            S.op("pe", lambda e: e.matmul(ps[6][:, :], perm, rq[r][:], start=True, stop=True), reads=["rq%d" % r, "cbf"], writes=["ps6"])
            t1 = rr("tmpf", 4); t2 = rr("tmpf", 4)
            S.op("dve", lambda e: e.tensor_tensor(tmpf[t1][:], ps[bank][:, :], cs[:, 0, tb * TB:(tb + 1) * TB], ALU.mult),
                 reads=["ps%d" % bank, "cs"], writes=["tmpf%d" % t1])
            S.op("dve", lambda e: e.tensor_tensor(tmpf[t2][:], ps[6][:, :], cs[:, 1, tb * TB:(tb + 1) * TB], ALU.mult),
                 reads=["ps6", "cs"], writes=["tmpf%d" % t2])
            s = rr("stg", 4)
            S.op("pool", lambda e: e.tensor_tensor(stg[s][:], tmpf[t1][:], tmpf[t2][:], ALU.add),
                 reads=["tmpf%d" % t1, "tmpf%d" % t2], writes=["stg%d" % s])
            S.op("sp", lambda e: e.dma_start(out=dst_dram[c * 128:(c + 1) * 128, tb * TB:(tb + 1) * TB], in_=stg[s][:]),
                 reads=["stg%d" % s], writes=[dst_res], dma=True, tag="stg%d" % s)

        if STOP == "P1a":
            return finish()
        for kind in ("u", "C", "k", "v", "q", "B"):
            if STOP == "P1u" and kind == "C":
                return finish()
            if STOP == "P1C" and kind == "k":
                return finish()
            if STOP == "P1k" and kind == "v":
                return finish()
            ws, slot = w_get()
            wv = slot[:].rearrange("p (kc n) -> p kc n", kc=8)
            for mi in range(4):
                for tb in range(NTB):
                    bank = rr("mm", 4)
                    mm_group(bank, [(wv[:, kc, mi * 128:(mi + 1) * 128], hm[:, kc, tb * TB:(tb + 1) * TB]) for kc in range(8)],
                             ["w%d" % ws, "h%d" % tb])
                    zs = z[:, mi, 1 + tb * TB:1 + (tb + 1) * TB]
                    if kind == "u":
                        S.op("act", (lambda b_, o_: lambda e: e.activation(o_, ps[b_][:, :], AF.Copy))(bank, zs),
                             reads=["ps%d" % bank], writes=["z%d" % mi])
                    elif kind == "C":
                        S.op("dve", (lambda b_, o_: lambda e: e.tensor_tensor(o_, o_, ps[b_][:, :], ALU.mult))(bank, zs),
                             reads=["ps%d" % bank, "z%d" % mi], writes=["z%d" % mi])
                    elif kind == "B":
                        S.op("act", (lambda b_, o_: lambda e: e.activation(o_, ps[b_][:, :], AF.Copy))(bank, bc[:, mi, tb * TB:(tb + 1) * TB]),
                             reads=["ps%d" % bank], writes=["bc%d" % mi])
                    elif kind == "k":
                        rotary_store(bank, mi, tb, snd_k, "snd%d" % mi)
                    elif kind == "q":
                        rotary_store(bank, mi, tb, q_scr, "qs%d" % mi)
                    else:
                        s = rr("stg", 4)
                        S.op("act", (lambda b_, s_: lambda e: e.activation(stg[s_][:], ps[b_][:, :], AF.Copy))(bank, s),
                             reads=["ps%d" % bank], writes=["stg%d" % s])
                        S.op("sp", (lambda s_, mi_, tb_: lambda e: e.dma_start(out=snd_v[mi_ * 128:(mi_ + 1) * 128, tb_ * TB:(tb_ + 1) * TB], in_=stg[s_][:]))(s, mi, tb),
                             reads=["stg%d" % s], writes=["snd%d" % (4 + mi)], dma=True, tag="stg%d" % s)
            w_done()
            if kind == "C":
                for f, c0 in ((0, 1), (1, T - 15)):
                    dst = bass.AP(sndz_t, f * 8192, [[16, 128], [2048, 4], [1, 16]])
                    S.op("sp", (lambda d_, c0_: lambda e: e.dma_start(out=d_, in_=z[:, :, c0_:c0_ + 16]))(dst, c0),
                         reads=["z%d" % i for i in range(4)], writes=["sndz"], dma=True, tag="sndz")
                allgather(snd_z, rcv_z, ["sndz"], "rcvz", "ccz")
            if kind == "k":
                allgather(snd_k, rcv_k, ["snd%d" % i for i in range(4)], "rcvk", "cck")
            if kind == "v":
                allgather(snd_v, rcv_v, ["snd%d" % i for i in range(4, 8)], "rcvv", "ccv")
        if STOP == "P1g":
            return finish()
        for f, (rank, slab) in enumerate(((0, 1), (1, 0))):
            src = bass.AP(rcvz_t, rank * 16384 + slab * 8192, [[16, 128], [2048, 4], [1, 16]])
            S.op("sp", (lambda f_, s_: lambda e: e.dma_start(out=zh[:, f_, :, :], in_=s_))(f, src),
                 reads=["rcvz"], writes=["zh"], dma=True, tag="ldzh")
        S.op("dve", lambda e: e.tensor_scalar(z[:, :, 0:1], zh[:, 0, :, 15:16], cf[:, 1:2], None, ALU.mult),
             reads=["zh", "cf"], writes=["z%d" % i for i in range(4)])
        S.op("dve", lambda e: e.tensor_scalar(z[:, :, T + 1:T + 2], zh[:, 1, :, 0:1], cf[:, 2:3], None, ALU.mult),
             reads=["zh", "cf"], writes=["z%d" % i for i in range(4)])
        for c in range(4):
            for tb in range(NTB):
                t1 = rr("tmpf", 4)
                wc = lambda k, c_=c, g0_=g0: gt[:, g0_ + 40 + 4 * k + c_:g0_ + 40 + 4 * k + c_ + 1]
                zz = lambda k, c_=c, tb_=tb: z[:, c_, tb_ * TB + k:tb_ * TB + k + TB]
                S.op("dve", (lambda t_, zz_, wc_: lambda e: e.tensor_scalar(tmpf[t_][:], zz_(0), wc_(0), None, ALU.mult))(t1, zz, wc),
                     reads=["z%d" % c, "gt"], writes=["tmpf%d" % t1])
                S.op("dve", (lambda t_, zz_, wc_: lambda e: e.scalar_tensor_tensor(tmpf[t_][:], zz_(1), wc_(1), tmpf[t_][:], ALU.mult, ALU.add))(t1, zz, wc),
                     reads=["z%d" % c, "gt", "tmpf%d" % t1], writes=["tmpf%d" % t1])
                S.op("dve", (lambda t_, zz_, wc_: lambda e: e.scalar_tensor_tensor(tmpf[t_][:], zz_(2), wc_(2), tmpf[t_][:], ALU.mult, ALU.add))(t1, zz, wc),
                     reads=["z%d" % c, "gt", "tmpf%d" % t1], writes=["tmpf%d" % t1])
                S.op("pool", (lambda t_, c_, tb_: lambda e: e.tensor_tensor(hm[:, 4 + c_, tb_ * TB:(tb_ + 1) * TB], tmpf[t_][:], bc[:, c_, tb_ * TB:(tb_ + 1) * TB], ALU.mult))(t1, c, tb),
                     reads=["tmpf%d" % t1, "bc%d" % c], writes=["hmB%d" % tb])
        barrier()
        if STOP == "P1":
            return finish()

        S.op("pool", lambda e: e.memset(vt[:], 1.0), writes=["vt"])
        blk = 0
        for hp in range(4):
            S.op("sp", (lambda hp_: lambda e: e.dma_start(out=q_sb[:], in_=q_scr[hp_ * 128:(hp_ + 1) * 128, :]))(hp),
                 reads=["qs%d" % hp], writes=["q_sb"], dma=True, tag="ldq")
            for (buf, sn, rc, rres, soff, res, tg) in ((k_ext, snd_k, rcv_k, "rcvk", 0, "k_ext", "ldk"), (v_ext, snd_v, rcv_v, "rcvv", 4, "v_ext", "ldv")):
                r0 = hp * 128
                S.op("sp", (lambda b_, rc_, r_: lambda e: e.dma_start(out=b_[:, 0:1024], in_=rc_[r_:r_ + 128, 1024:2048]))(buf, rc, r0),
                     reads=[rres], writes=[res], dma=True, tag=tg)
                S.op("sp", (lambda b_, sn_, r_: lambda e: e.dma_start(out=b_[:, 1024:3072], in_=sn_[r_:r_ + 128, :]))(buf, sn, r0),
                     reads=["snd%d" % (soff + hp)], writes=[res], dma=True, tag=tg)
                S.op("sp", (lambda b_, rc_, r_: lambda e: e.dma_start(out=b_[:, 3072:4096], in_=rc_[512 + r_:512 + r_ + 128, 0:1024]))(buf, rc, r0),
                     reads=[rres], writes=[res], dma=True, tag=tg)
            S.op("pool", lambda e: e.tensor_copy(q16[:].rearrange("p (r m) -> p r m", r=16), q_sb[:].rearrange("p (m r) -> p r m", r=16)),
                 reads=["q_sb"], writes=["q16"])
            for d in (1, 4, 16):
                nblk = 16 // d
                ntile = d * (nblk + 1)
                for t0 in range(0, ntile, 8):
                    nt = min(8, ntile - t0)

                    def trfn(e, t0=t0, nt=nt, d=d, nblk=nblk):
                        ins = None
                        for i in range(nt):
                            rho, j = divmod(t0 + i, nblk + 1)
                            e0 = 1024 + rho + d * (128 * j - 64)
                            ins = e.transpose(ps7[:, i * 128:(i + 1) * 128], v_ext[:, sl(e0, 128, d)], ident)
                        return ins
                    S.op("pe", trfn, reads=["v_ext", "cbf"], writes=["ps7"])
                    pv = lambda nt=nt: ps7[:, 0:nt * 128].rearrange("p (t c) -> p t c", c=128)
                    S.op("act", (lambda t0_, nt_, pv_: lambda e: e.activation(vt[:, t0_:t0_ + nt_, 0:64], pv_()[:, :, 0:64], AF.Copy))(t0, nt, pv),
                         reads=["ps7"], writes=["vt"])
                    S.op("dve", (lambda t0_, nt_, pv_: lambda e: e.tensor_copy(vt[:, t0_:t0_ + nt_, 96:160], pv_()[:, :, 64:128]))(t0, nt, pv),
                         reads=["ps7"], writes=["vt"])
                blocks = [(rho, j) for rho in range(d) for j in range(nblk)]
                for bi, (rho, j) in enumerate(blocks):
                    par = blk % 2
                    blk += 1
                    variant = (1 if j == 0 else 0) + (2 if j == nblk - 1 else 0)
                    if d == 16:
                        qsrc = lambda p0, rho=rho: q16[p0:p0 + 64, rho * 128:(rho + 1) * 128]
                    else:
                        qsrc = lambda p0, rho=rho, j=j, d=d: q_sb[p0:p0 + 64, sl(rho + d * 128 * j, 128, d)]
                    eA = 1024 + rho + d * (128 * j - 64)
                    eB = eA + d * 128
                    tA = rho * (nblk + 1) + j
                    for hpar, p0 in ((0, 0), (1, 64)):
                        sbank = 2 * hpar + par

                        def sfn(e, sbank=sbank, p0=p0, eA=eA, eB=eB, d=d, qsrc=qsrc, variant=variant):
                            e.matmul(ps[sbank][:, 0:256], ident, maskv(variant), start=True, stop=False)
                            e.matmul(ps[sbank][:, 0:128], k_ext[p0:p0 + 64, sl(eA, 128, d)], qsrc(p0), start=False, stop=False)
                            return e.matmul(ps[sbank][:, 128:256], k_ext[p0:p0 + 64, sl(eB, 128, d)], qsrc(p0), start=False, stop=True)
                        S.op("pe", sfn, reads=["k_ext", "q16" if d == 16 else "q_sb", "cbf"], writes=["ps%d" % sbank])
                        pb = 2 * hpar + par
                        S.op("act", (lambda pb_, sb_: lambda e: e.activation(pbuf[pb_][:], ps[sb_][:, 0:256], AF.Exp, scale=0.125))(pb, sbank),
                             reads=["ps%d" % sbank], writes=["pb%d" % pb])
                    obank = 4 + (bi // 2) % 2
                    osl = bi % 2
                    for hpar in (0, 1):
                        pb = 2 * hpar + par

                        def ofn(e, hpar=hpar, pb=pb, obank=obank, osl=osl, tA=tA):
                            if hpar == 0:
                                o_ap = ps[obank][0:65, osl * 128:(osl + 1) * 128]
                                la, lb = vt[:, tA, 0:65], vt[:, tA + 1, 0:65]
                            else:
                                o_ap = ps[obank][:, 256 + osl * 128:256 + (osl + 1) * 128]
                                la, lb = vt[:, tA, 32:160], vt[:, tA + 1, 32:160]
                            e.matmul(o_ap, la, pbuf[pb][:, 0:128], start=True, stop=False)
                            return e.matmul(o_ap, lb, pbuf[pb][:, 128:256], start=False, stop=True)
                        S.op("pe", ofn, reads=["vt", "pb%d" % pb], writes=["ps%d" % obank])
                    if bi % 2 == 1 or bi == len(blocks) - 1:
                        nb = 2 if bi % 2 == 1 else 1
                        b0 = bi - (nb - 1)
                        rho0, j0 = blocks[b0]
                        if d == 16:
                            dst = lambda t_, p_, rho0=rho0, nb=nb: t_[p_, :].rearrange("p (m r) -> p r m", r=16)[:, rho0:rho0 + nb, :]
                            src = lambda ap_: ap_.rearrange("p (r m) -> p r m", m=128)
                        else:
                            dst = lambda t_, p_, rho0=rho0, j0=j0, d=d, nb=nb: t_[p_, sl(rho0 + d * 128 * j0, 128 * nb, d)]
                            src = lambda ap_: ap_
                        for hpar in (0, 1):
                            if hpar == 0:
                                pp = slice(0, 65); o_t = oacc_e; c0 = 0; res = "oacc_e"
                            else:
                                pp = slice(0, 128); o_t = oacc_o; c0 = 256; res = "oacc_o"
                            if d == 1:
                                S.op("dve", (lambda o_t_, pp_, c0_, nb_, ob_, dst_, src_: lambda e: e.tensor_copy(dst_(o_t_, pp_), src_(ps[ob_][pp_, c0_:c0_ + 128 * nb_])))(o_t, pp, c0, nb, obank, dst, src),
                                     reads=["ps%d" % obank], writes=[res])
                            else:
                                S.op("dve", (lambda o_t_, pp_, c0_, nb_, ob_, dst_, src_: lambda e: e.tensor_tensor(dst_(o_t_, pp_), dst_(o_t_, pp_), src_(ps[ob_][pp_, c0_:c0_ + 128 * nb_]), ALU.add))(o_t, pp, c0, nb, obank, dst, src),
                                     reads=["ps%d" % obank, res], writes=[res])
            S.op("dve", lambda e: e.reciprocal(oacc_e[64:65, :], oacc_e[64:65, :]), reads=["oacc_e"], writes=["oacc_e"])
            S.op("dve", lambda e: e.reciprocal(oacc_o[32:33, :], oacc_o[32:33, :]), reads=["oacc_o"], writes=["oacc_o"])
            for tb in range(NTB):
                cs_ = slice(tb * TB, (tb + 1) * TB)
                S.op("pe", (lambda c_: lambda e: e.matmul(ps[6][0:64, :], onesf[64:65, 0:64], oacc_e[64:65, c_], start=True, stop=True))(cs_),
                     reads=["oacc_e", "onesf"], writes=["ps6"])
                S.op("pe", (lambda c_: lambda e: e.matmul(ps[0][:, :], selo[32:33, :], oacc_o[32:33, c_], start=True, stop=True))(cs_),
                     reads=["oacc_o", "selo"], writes=["ps0"])
                S.op("dve", (lambda c_, hp_: lambda e: e.tensor_tensor(hm[0:64, hp_, c_], oacc_e[0:64, c_], ps[6][0:64, :], ALU.mult))(cs_, hp),
                     reads=["oacc_e", "ps6"], writes=["hmA%d" % tb])
                S.op("dve", (lambda c_, hp_: lambda e: e.tensor_tensor(hm[64:128, hp_, c_], oacc_o[64:128, c_], ps[0][64:128, :], ALU.mult))(cs_, hp),
                     reads=["oacc_o", "ps0"], writes=["hmA%d" % tb])
        barrier()
        if STOP == "P2":
            return finish()

        i0 = wcur["i"]
        w_issue(i0 + 1)
        ws0, ws1 = i0 % NW, (i0 + 1) % NW
        slot0, slot1 = wsl[ws0], wsl[ws1]
        wo = [slot0[:].rearrange("p (kc n) -> p kc n", kc=8), slot1[:].rearrange("p (kc n) -> p kc n", kc=8)]
        wres = ["w%d" % ws0, "w%d" % ws1]
        for tb in range(NTB):
            cs_ = slice(tb * TB, (tb + 1) * TB)
            rr_ = []
            for grp, gcol in ((0, g0 + 8), (1, g0 + 12)):
                ssb = 4 + rr("ss", 2)
                sq_ones((lambda g_, c_: lambda: hm[:, 4 * g_:4 * g_ + 4, c_])(grp, cs_), 4, ["hmA%d" % tb, "hmB%d" % tb], ssb, True, True)
                r = rstd_from(ssb, 512)
                for c in range(4):
                    cc = 4 * grp + c
                    S.op("dve", (lambda cc_, c_, r_, gc_: lambda e: e.scalar_tensor_tensor(hm[:, cc_, c_], hm[:, cc_, c_], gt[:, gc_:gc_ + 1], rs[r_][:], ALU.mult, ALU.mult))(cc, cs_, r, gcol + c),
                         reads=["hmA%d" % tb, "hmB%d" % tb, "gt", "rs%d" % r], writes=["hmA%d" % tb if grp == 0 else "hmB%d" % tb])
            ssb = 4 + rr("ss", 2)
            for m in range(8):
                bank = rr("mm", 4)
                mm_group(bank, [(wo[m // 4][:, kc, (m % 4) * 128:(m % 4 + 1) * 128], hm[:, kc, cs_]) for kc in range(8)],
                         [wres[m // 4], "hmA%d" % tb, "hmB%d" % tb])
                S.op("act", (lambda b_, m_: lambda e: e.activation(mixbuf[:, m_, :], ps[b_][:, :], AF.Copy))(bank, m),
                     reads=["ps%d" % bank], writes=["mixbuf"])
                sqs = 0 if (cnt["sq"] % 2 == 0) else 4
                cnt["sq"] += 1
                S.op("act", (lambda b_, s_: lambda e: e.activation(sq[:, s_, :], ps[b_][:, :], AF.Square))(bank, sqs),
                     reads=["ps%d" % bank], writes=["sq%d" % sqs])
                S.op("pe", (lambda s_, m_, sb_: lambda e: e.matmul(ps[sb_][:, :], ones_bf, sq[:, s_, :], start=(m_ == 0), stop=(m_ == 7)))(sqs, m, ssb),
                     reads=["sq%d" % sqs, "cbf"], writes=["ps%d" % ssb])
            r = rstd_from(ssb, D)
            for m in range(8):
                t1 = rr("tmpf", 4)
                S.op("dve", (lambda t_, m_, r_, gc_: lambda e: e.scalar_tensor_tensor(tmpf[t_][:], mixbuf[:, m_, :], gt[:, gc_:gc_ + 1], rs[r_][:], ALU.mult, ALU.mult))(t1, m, r, g0 + 16 + m),
                     reads=["mixbuf", "gt", "rs%d" % r], writes=["tmpf%d" % t1])
                S.op("pool", (lambda t_, m_, c_: lambda e: e.tensor_tensor(x_sb[:, m_, c_], x_sb[:, m_, c_], tmpf[t_][:], ALU.add))(t1, m, cs_),
                     reads=["tmpf%d" % t1, "x%d" % tb], writes=["x%d" % tb])
        w_done(); w_done()
        barrier()
        if STOP == "P3":
            return finish()

        for hf in range(2):
            for tb2 in range(2):
                tbg = hf * 2 + tb2
                norm_x_to((lambda tb2_: lambda c_: h2[:, c_, tb2_ * TB:(tb2_ + 1) * TB])(tb2), "h2_%d" % tb2, tbg, g0 + 24)
            for f in range(11):
                ws, slot = w_get()
                wv = slot[:].rearrange("p (kc n) -> p kc n", kc=8)
                for tb2 in range(2):
                    for mi in range(2):
                        bg = rr("mm", 4)
                        mm_group(bg, [(wv[:, kc, mi * 128:(mi + 1) * 128], h2[:, kc, tb2 * TB:(tb2 + 1) * TB]) for kc in range(8)],
                                 ["w%d" % ws, "h2_%d" % tb2])
                        bu = rr("mm", 4)
                        mm_group(bu, [(wv[:, kc, 256 + mi * 128:256 + (mi + 1) * 128], h2[:, kc, tb2 * TB:(tb2 + 1) * TB]) for kc in range(8)],
                                 ["w%d" % ws, "h2_%d" % tb2])
                        t1 = rr("tmpf", 4)
                        S.op("act", (lambda t_, b_: lambda e: e.activation(tmpf[t_][:], ps[b_][:, :], AF.Silu))(t1, bg),
                             reads=["ps%d" % bg], writes=["tmpf%d" % t1])
                        S.op("dve", (lambda t_, b_, fc_, tb2_: lambda e: e.tensor_tensor(hid[:, fc_, tb2_ * TB:(tb2_ + 1) * TB], tmpf[t_][:], ps[b_][:, :], ALU.mult))(t1, bu, 2 * f + mi, tb2),
                             reads=["tmpf%d" % t1, "ps%d" % bu], writes=["hid%d" % tb2])
                w_done()
            ssbs = [4, 5]
            for mp in range(4):
                slots = []
                for kh in range(2):
                    ws, slot = w_get()
                    wv = slot[:, 0:2816].rearrange("p (kc n) -> p kc n", kc=11)
                    for mi in range(2):
                        for tb2 in range(2):
                            bank = mi * 2 + tb2

                            def dfn(e, wv=wv, mi=mi, tb2=tb2, bank=bank, kh=kh):
                                ins = None
                                for kk in range(11):
                                    ins = e.matmul(ps[bank][:, :], wv[:, kk, mi * 128:(mi + 1) * 128], hid[:, kh * 11 + kk, tb2 * TB:(tb2 + 1) * TB],
                                                   start=(kh == 0 and kk == 0), stop=(kh == 1 and kk == 10))
                                return ins
                            S.op("pe", dfn, reads=["w%d" % ws, "hid%d" % tb2], writes=["ps%d" % bank])
                    w_done()
                for mi in range(2):
                    for tb2 in range(2):
                        bank = mi * 2 + tb2
                        m = mp * 2 + mi
                        S.op("act", (lambda b_, m_, tb2_: lambda e: e.activation(f_bf[:, m_, tb2_ * TB:(tb2_ + 1) * TB], ps[b_][:, :], AF.Copy))(bank, m, tb2),
                             reads=["ps%d" % bank], writes=["f_bf%d" % tb2])
                        sqs = 0 if (cnt["sq"] % 2 == 0) else 4
                        cnt["sq"] += 1
                        S.op("act", (lambda b_, s_: lambda e: e.activation(sq[:, s_, :], ps[b_][:, :], AF.Square))(bank, sqs),
                             reads=["ps%d" % bank], writes=["sq%d" % sqs])
                        S.op("pe", (lambda s_, m_, sb_: lambda e: e.matmul(ps[sb_][:, :], ones_bf, sq[:, s_, :], start=(m_ == 0), stop=(m_ == 7)))(sqs, m, ssbs[tb2]),
                             reads=["sq%d" % sqs, "cbf"], writes=["ps%d" % ssbs[tb2]])
            for tb2 in range(2):
                tbg = hf * 2 + tb2
                r = rstd_from(ssbs[tb2], D)
                for m in range(8):
                    t1 = rr("tmpf", 4)
                    S.op("dve", (lambda t_, m_, r_, tb2_, gc_: lambda e: e.scalar_tensor_tensor(tmpf[t_][:], f_bf[:, m_, tb2_ * TB:(tb2_ + 1) * TB], gt[:, gc_:gc_ + 1], rs[r_][:], ALU.mult, ALU.mult))(t1, m, r, tb2, g0 + 32 + m),
                         reads=["f_bf%d" % tb2, "gt", "rs%d" % r], writes=["tmpf%d" % t1])
                    S.op("pool", (lambda t_, m_, tbg_: lambda e: e.tensor_tensor(x_sb[:, m_, tbg_ * TB:(tbg_ + 1) * TB], x_sb[:, m_, tbg_ * TB:(tbg_ + 1) * TB], tmpf[t_][:], ALU.add))(t1, m, tbg),
                         reads=["tmpf%d" % t1, "x%d" % tbg], writes=["x%d" % tbg])
        barrier()

    return finish()


def _const_tables(validP, validN):
    cbf = np.zeros((128, 384 + 1024), np.float32)
    cbf[:, 0:128] = np.eye(128)
    cbf[:, 128:256] = 1.0
    perm = np.zeros((128, 128), np.float32)
    for par in range(2):
        for dd in range(8):
            p = par * 64 + dd
            perm[p + 8, p] = -1.0
            perm[p, p + 8] = 1.0
    cbf[:, 256:384] = perm
    a = np.arange(128)[:, None]
    b = np.arange(128)[None, :]
    mA = np.where(b <= a, 0.0, NEG)
    mB = np.where(b >= a, 0.0, NEG)
    for v in range(4):
        ma = mA.copy(); mb = mB.copy()
        if (v & 1) and not validP:
            ma[0:64, :] = NEG
        if (v & 2) and not validN:
            mb[64:128, :] = NEG
        cbf[:, 384 + 256 * v:384 + 256 * v + 128] = ma
        cbf[:, 384 + 256 * v + 128:384 + 256 * (v + 1)] = mb
    cf = np.zeros((128, 4), np.float32)
    inv = np.float32(500000.0) ** (-(np.arange(0, 16, 2, dtype=np.float32)) / np.float32(16.0))
    for p in range(128):
        dd = p % 64
        if dd < 16:
            cf[p, 0] = inv[dd % 8]
    cf[:, 1] = 1.0 if validP else 0.0
    cf[:, 2] = 1.0 if validN else 0.0
    return cbf.astype(NPBF), cf


def _gain_table(layers, pre_mix_norm, conv_w, attn_out_norm, conv_out_norm, post_mix_norm, pre_ffn_norm, post_ffn_norm):
    gtab = np.zeros((128, len(layers) * GT_PER_LAYER), np.float32)

    def put(col, vec):
        n = vec.shape[0] // 128
        gtab[:, col:col + n] = vec.reshape(n, 128).T
    for i, l in enumerate(layers):
        b = i * GT_PER_LAYER
        put(b + 0, pre_mix_norm[l]); put(b + 8, attn_out_norm[l]); put(b + 12, conv_out_norm[l])
        put(b + 16, post_mix_norm[l]); put(b + 24, pre_ffn_norm[l]); put(b + 32, post_ffn_norm[l])
        for k in range(3):
            put(b + 40 + 4 * k, conv_w[l, k])
    return gtab


_PROG = {}


def _run(L, layers, xTs, positions, P):
    if L not in _PROG:
        _PROG[L] = build_program(L)
    nc = _PROG[L]
    gtab = _gain_table(layers, P["pre_mix_norm"], P["conv_w"], P["attn_out_norm"], P["conv_out_norm"],
                       P["post_mix_norm"], P["pre_ffn_norm"], P["post_ffn_norm"])
    lsel = slice(layers[0], layers[-1] + 1)
    wi = np.ascontiguousarray(P["w_in"][lsel]); wo = np.ascontiguousarray(P["w_out"][lsel])
    wg = np.ascontiguousarray(P["w_gate_up"][lsel]); wd = np.ascontiguousarray(P["w_down"][lsel])
    in_maps = []
    for c in range(8):
        b, r = divmod(c, 2)
        cbf, cf = _const_tables(validP=(r == 1), validN=(r == 0))
        in_maps.append({
            "xT": xTs[c],
            "pos": np.ascontiguousarray(positions[b:b + 1, r * T:(r + 1) * T]).astype(np.int32),
            "w_in": wi, "w_out": wo, "w_gu": wg, "w_dn": wd,
            "gt": gtab, "cbf": cbf, "cf": cf,
        })
    res = run_bass_kernel_spmd(nc, in_maps, core_ids=list(range(8)))
    return [np.asarray(res.results[c]["yT"]) for c in range(8)]


def kernel(x, positions, pre_mix_norm, w_in, conv_w, attn_out_norm, conv_out_norm, w_out,
           post_mix_norm, pre_ffn_norm, w_gate_up, w_down, post_ffn_norm):
    P = dict(pre_mix_norm=np.asarray(pre_mix_norm, np.float32), w_in=np.asarray(w_in, np.float32),
             conv_w=np.asarray(conv_w, np.float32), attn_out_norm=np.asarray(attn_out_norm, np.float32),
             conv_out_norm=np.asarray(conv_out_norm, np.float32), w_out=np.asarray(w_out, np.float32),
             post_mix_norm=np.asarray(post_mix_norm, np.float32), pre_ffn_norm=np.asarray(pre_ffn_norm, np.float32),
             w_gate_up=np.asarray(w_gate_up, np.float32), w_down=np.asarray(w_down, np.float32),
             post_ffn_norm=np.asarray(post_ffn_norm, np.float32))
    x = np.asarray(x, np.float32)
    positions = np.asarray(positions)
    xTs = []
    for c in range(8):
        b, r = divmod(c, 2)
        xTs.append(np.ascontiguousarray(x[b, r * T:(r + 1) * T, :].T))
    if FUSED:
        xTs = _run(DEPTH, list(range(DEPTH)), xTs, positions, P)
    else:
        for l in range(DEPTH):
            xTs = _run(1, [l], xTs, positions, P)
    out = np.empty((4, 4096, D), np.float32)
    for c in range(8):
        b, r = divmod(c, 2)
        out[b, r * T:(r + 1) * T, :] = xTs[c].T
    return out
```

```python
import collections
import contextlib
import math

import numpy as np
import ml_dtypes

import concourse.bass as bass
import concourse.mybir as mybir
from concourse.bass_utils import run_bass_kernel_spmd

F32 = mybir.dt.float32
BF16 = mybir.dt.bfloat16
I32 = mybir.dt.int32
ALU = mybir.AluOpType
AF = mybir.ActivationFunctionType
NPBF = ml_dtypes.bfloat16

FUSED = True
DEPTH = 4
D = 1024
T = 2048
TB = 512
NTB = 4
FFN = 2816
NEG = -30000.0
EPS = 1e-6
GT_PER_LAYER = 52
REPLICA_GROUPS = [[0, 1], [2, 3], [4, 5], [6, 7]]
STOP = None
ENG_NAMES = ("pe", "act", "dve", "pool", "sp")


def sl(start, n, step=1):
    return slice(start, start + (n - 1) * step + 1, step)


class Op:
    __slots__ = ("idx", "eng", "fn", "reads", "writes", "dma", "tag", "inc", "count",
                 "waits", "marked", "known", "kdma")

    def __init__(self, idx, eng, fn, reads, writes, dma, tag, inc):
        self.idx = idx; self.eng = eng; self.fn = fn
        self.reads = tuple(reads); self.writes = tuple(writes)
        self.dma = dma; self.tag = tag; self.inc = inc
        self.count = 0; self.waits = []; self.marked = False
        self.known = None; self.kdma = None


class Sched:
    def __init__(self):
        self.ops = []
        self.last_write = {}
        self.readers = collections.defaultdict(dict)
        self.tag_count = collections.defaultdict(int)
        self.tag_inc = {}
        self.eng_known = {e: {} for e in ENG_NAMES}
        self.eng_kdma = {e: {} for e in ENG_NAMES}
        self.eng_ops = {e: [] for e in ENG_NAMES}

    def op(self, eng, fn, reads=(), writes=(), dma=False, tag=None, inc=16, nophase=False):
        i = len(self.ops)
        reads = list(reads); writes = list(writes)
        if not nophase:
            reads.append("PHASE")
        for r in list(reads):
            if r.startswith("ps"):
                reads.remove(r)
                if r not in writes:
                    writes.append(r)
        o = Op(i, eng, fn, reads, writes, dma, tag, inc)
        self.ops.append(o)
        deps = {}
        for r in o.reads:
            j = self.last_write.get(r)
            if j is not None:
                deps[j] = "RAW"
        for r in o.writes:
            j = self.last_write.get(r)
            if j is not None and j not in deps:
                deps[j] = "WAW"
            for j in self.readers[r].values():
                if j not in deps:
                    deps[j] = "WAR"
        known = self.eng_known[eng]
        kdma = self.eng_kdma[eng]
        waits = {}
        for j, kind in sorted(deps.items()):
            d = self.ops[j]
            if d.dma:
                if kdma.get(d.tag, 0) >= d.count:
                    continue
                c = self.tag_count[d.tag]
                waits[("dma", d.tag)] = c
                kdma[d.tag] = c
            else:
                if d.eng == eng:
                    if eng in ("pe", "sp") or kind != "RAW":
                        continue
                if known.get(d.eng, -1) >= j:
                    continue
                waits[("eng", d.eng)] = max(waits.get(("eng", d.eng), -1), j)
        for key, v in waits.items():
            if key[0] == "eng":
                d = self.ops[v]
                d.marked = True
                known[key[1]] = max(known.get(key[1], -1), v)
                for e2, k2 in d.known.items():
                    if known.get(e2, -1) < k2:
                        known[e2] = k2
                for t2, c2 in d.kdma.items():
                    if kdma.get(t2, 0) < c2:
                        kdma[t2] = c2
        o.waits = list(waits.items())
        if dma:
            assert self.tag_inc.setdefault(tag, inc) == inc
            self.tag_count[tag] += 1
            o.count = self.tag_count[tag]
        o.known = dict(known)
        o.kdma = dict(kdma)
        for r in o.reads:
            self.readers[r][("dma", i) if dma else eng] = i
        for r in o.writes:
            self.last_write[r] = i
            self.readers[r] = {}
        self.eng_ops[eng].append(o)
        return o

    def emit(self, nc, final_wait_tags=()):
        rank = {}
        for e in ENG_NAMES:
            c = 0
            for o in self.eng_ops[e]:
                if o.marked and not o.dma:
                    c += 1
                    rank[o.idx] = c
        with contextlib.ExitStack() as st:
            esem = {e: st.enter_context(nc.semaphore("s_" + e)) for e in ("pe", "act", "dve", "pool")}
            tsem = {t: st.enter_context(nc.semaphore("t_" + str(t))) for t in sorted(self.tag_count)}
            block = st.enter_context(nc.Block())
            handles = {"pe": block.tensor, "act": block.scalar, "dve": block.vector,
                       "pool": block.gpsimd, "sp": block.sync}

            def body(ename):
                def _f(eng):
                    for o in self.eng_ops[ename]:
                        for key, v in o.waits:
                            if key[0] == "eng":
                                eng.wait_ge(esem[key[1]], rank[v])
                            else:
                                eng.wait_ge(tsem[key[1]], v * self.tag_inc[key[1]])
                        ins = o.fn(eng)
                        if o.dma:
                            ins.then_inc(tsem[o.tag], o.inc)
                        elif o.marked:
                            ins.then_inc(esem[ename], 1)
                    if ename == "sp":
                        for t in final_wait_tags:
                            eng.wait_ge(tsem[t], self.tag_count[t] * self.tag_inc[t])
                return _f

            for e in ENG_NAMES:
                handles[e](body(e))


class SbAlloc:
    BASE = 16512
    LIMIT = 229344

    def __init__(self, nc):
        self.nc = nc
        self.off = self.BASE
        self.n = 0

    def at(self, name, shape, dtype, offset):
        nbytes = int(np.prod(shape[1:])) * (2 if dtype == BF16 else 4)
        assert offset % 32 == 0 and offset + nbytes <= self.LIMIT, (name, offset, nbytes)
        self.n += 1
        return self.nc.alloc_sbuf_tensor_at("%s_%d" % (name, self.n), list(shape), dtype, offset=offset), nbytes

    def new(self, name, shape, dtype):
        t, nbytes = self.at(name, shape, dtype, self.off)
        self.off += (nbytes + 31) // 32 * 32
        return t


def build_program(L):
    nc = bass.Bass("TRN2", target_bir_lowering=False)
    S = Sched()

    def dram(name, shape, dt, kind):
        return nc.dram_tensor(name, list(shape), dt, kind=kind)

    xT = dram("xT", [D, T], F32, "ExternalInput").ap()
    pos = dram("pos", [1, T], I32, "ExternalInput").ap()
    w_in = dram("w_in", [L, D, 3072], F32, "ExternalInput").ap()
    w_out = dram("w_out", [L, D, D], F32, "ExternalInput").ap()
    w_gu = dram("w_gu", [L, D, 2 * FFN], F32, "ExternalInput").ap()
    w_dn = dram("w_dn", [L, FFN, D], F32, "ExternalInput").ap()
    gt_d = dram("gt", [128, L * GT_PER_LAYER], F32, "ExternalInput").ap()
    cbf_d = dram("cbf", [128, 384 + 1024], BF16, "ExternalInput").ap()
    cf_d = dram("cf", [128, 4], F32, "ExternalInput").ap()
    yT = dram("yT", [D, T], F32, "ExternalOutput").ap()
    q_scr_t = dram("q_scr", [512, T], BF16, "Internal")
    sndk_t = dram("snd_k", [512, T], BF16, "Internal")
    sndv_t = dram("snd_v", [512, T], BF16, "Internal")
    sndz_t = dram("snd_z", [8, T], BF16, "Internal")
    rcvk_t = dram("rcv_k", [1024, T], BF16, "Internal")
    rcvv_t = dram("rcv_v", [1024, T], BF16, "Internal")
    rcvz_t = dram("rcv_z", [16, T], BF16, "Internal")
    q_scr = q_scr_t.ap()
    snd_k, snd_v, snd_z = sndk_t.ap(), sndv_t.ap(), sndz_t.ap()
    rcv_k, rcv_v, rcv_z = rcvk_t.ap(), rcvv_t.ap(), rcvz_t.ap()

    def allgather(src, dst, reads, wres, tag):
        S.op("pool", lambda e: e.collective_compute("AllGather", ALU.bypass, replica_groups=REPLICA_GROUPS, ins=[src], outs=[dst]),
             reads=reads, writes=[wres], dma=True, tag=tag, inc=1)

    A = SbAlloc(nc)
    x_sb = A.new("x", [128, 8, T], F32)
    cs = A.new("cs", [128, 2, T], BF16)
    cbf = A.new("cbf", [128, 384 + 1024], BF16)
    cf = A.new("cf", [128, 4], F32)
    gt = A.new("gt", [128, L * GT_PER_LAYER], F32)
    onesf = A.new("onesf", [128, 128], F32)
    selo = A.new("selo", [128, 128], F32)
    bar = A.new("bar", [128, 8], F32)
    NW = 3
    wsl = [A.new("w%d" % i, [128, 4096], BF16) for i in range(NW)]
    sq = A.new("sq", [128, 8, TB], BF16)
    rs = [A.new("rs%d" % i, [128, TB], F32) for i in range(3)]
    rq = [A.new("rq%d" % i, [128, TB], BF16) for i in range(2)]
    stg = [A.new("stg%d" % i, [128, TB], BF16) for i in range(4)]
    zh = A.new("zh", [128, 2, 4, 16], BF16)
    regB = A.off
    tmpf = [A.new("tmpf%d" % i, [128, TB], F32) for i in range(4)]
    oacc_e, _ = A.at("oacc_e", [128, T], F32, regB)
    arena = A.off
    hm, _ = A.at("hm", [128, 8, T], BF16, arena)
    regA = arena + 32768
    z, _ = A.at("z", [128, 4, T + 2], BF16, regA)
    bc, _ = A.at("bc", [128, 4, T], BF16, regA + 16416)
    k_ext, _ = A.at("k_ext", [128, 4096], BF16, regA)
    v_ext, _ = A.at("v_ext", [128, 4096], BF16, regA + 8192)
    vt, _ = A.at("vt", [128, 32, 160], BF16, regA + 16384)
    q_sb, _ = A.at("q_sb", [128, T], BF16, regA + 16384 + 10240)
    o2 = regA + 32800
    q16, _ = A.at("q16", [128, T], BF16, o2)
    pbuf = [A.at("pb%d" % i, [128, 256], BF16, o2 + 4096 + 512 * i)[0] for i in range(4)]
    oacc_o, _ = A.at("oacc_o", [128, T], F32, o2 + 4096 + 2048)
    mixbufs = [A.at("mixbuf%d" % i, [128, 8, TB], BF16, regA + 8192 * i)[0] for i in range(2)]
    h2, _ = A.at("h2", [128, 8, 1024], BF16, arena)
    hid, _ = A.at("hid", [128, 22, 1024], BF16, arena + 16384)
    f_bf, _ = A.at("f_bf", [128, 8, 1024], BF16, arena + 16384 + 45056)
    assert arena + 16384 + 45056 + 16384 <= A.LIMIT
    su_i, _ = A.at("su_i", [128, T], I32, arena)
    su_ang, _ = A.at("su_ang", [128, T], F32, arena + 8192)
    su_a2, _ = A.at("su_a2", [128, T], F32, arena + 16384)
    su_kf, _ = A.at("su_kf", [128, T], F32, arena + 24576)
    su_ki, _ = A.at("su_ki", [128, T], I32, arena + 32768)
    su_r, _ = A.at("su_r", [128, T], F32, arena + 40960)

    ps = [nc.alloc_psum_tensor("ps%d" % i, [128, TB], F32) for i in range(6)]
    psb = [nc.alloc_psum_tensor("ps%d" % i, [128, 1024], BF16) for i in (6, 7)]

    ident = cbf[:, 0:128]
    ones_bf = cbf[:, 128:256]
    perm = cbf[:, 256:384]

    def maskv(v):
        return cbf[:, 384 + 256 * v: 384 + 256 * (v + 1)]

    def barrier():
        S.op("pool", lambda e: e.memset(bar[:], 0.0), writes=["PHASE"], nophase=True)

    def finish():
        for c in range(8):
            S.op("sp", (lambda c_: lambda e: e.dma_start(out=yT[c_ * 128:(c_ + 1) * 128, :], in_=x_sb[:, c_, :]))(c),
                 reads=["x%d" % tb for tb in range(NTB)], dma=True, tag="yout")
        S.emit(nc, final_wait_tags=["yout"])
        return nc

    S.op("sp", lambda e: e.dma_start(out=cbf[:], in_=cbf_d), writes=["cbf"], dma=True, tag="c0")
    S.op("sp", lambda e: e.dma_start(out=cf[:], in_=cf_d), writes=["cf"], dma=True, tag="c0")
    S.op("sp", lambda e: e.dma_start(out=gt[:], in_=gt_d), writes=["gt"], dma=True, tag="c0")
    S.op("sp", lambda e: e.dma_start(out=su_i[:], in_=pos.partition_broadcast(128)), writes=["su_i"], dma=True, tag="c1")
    for c in range(8):
        S.op("sp", (lambda c: lambda e: e.dma_start(out=x_sb[:, c, :], in_=xT[c * 128:(c + 1) * 128, :]))(c),
             writes=["x%d" % tb for tb in range(NTB)], dma=True, tag="xin")
    S.op("pool", lambda e: e.memset(onesf[:], 1.0), writes=["onesf"])
    S.op("pool", lambda e: e.memset(selo[:, 0:64], 0.0), writes=["selo"])
    S.op("pool", lambda e: e.memset(selo[:, 64:128], 1.0), writes=["selo"])
    TWO_PI = 2.0 * math.pi
    S.op("dve", lambda e: e.tensor_copy(su_ang[:], su_i[:]), reads=["su_i"], writes=["su_ang"])
    S.op("dve", lambda e: e.tensor_scalar(su_ang[:], su_ang[:], cf[:, 0:1], None, ALU.mult), reads=["su_ang", "cf"], writes=["su_ang"])
    for which, shift in ((1, 0.0), (0, 0.5 * math.pi)):
        S.op("dve", (lambda sh: lambda e: e.tensor_scalar(su_a2[:], su_ang[:], sh, None, ALU.add))(shift), reads=["su_ang"], writes=["su_a2"])
        S.op("dve", lambda e: e.tensor_scalar(su_kf[:], su_a2[:], 1.0 / TWO_PI, None, ALU.mult), reads=["su_a2"], writes=["su_kf"])
        S.op("dve", lambda e: e.tensor_copy(su_ki[:], su_kf[:]), reads=["su_kf"], writes=["su_ki"])
        S.op("dve", lambda e: e.tensor_copy(su_kf[:], su_ki[:]), reads=["su_ki"], writes=["su_kf"])
        S.op("dve", lambda e: e.scalar_tensor_tensor(su_r[:], su_kf[:], -TWO_PI, su_a2[:], ALU.mult, ALU.add), reads=["su_kf", "su_a2"], writes=["su_r"])
        S.op("dve", lambda e: e.tensor_scalar(su_kf[:], su_r[:], math.pi, TWO_PI, ALU.is_gt, ALU.mult), reads=["su_r"], writes=["su_kf"])
        S.op("dve", lambda e: e.tensor_tensor(su_r[:], su_r[:], su_kf[:], ALU.subtract), reads=["su_r", "su_kf"], writes=["su_r"])
        S.op("dve", lambda e: e.tensor_scalar(su_kf[:], su_r[:], -math.pi, TWO_PI, ALU.is_lt, ALU.mult), reads=["su_r"], writes=["su_kf"])
        S.op("dve", lambda e: e.tensor_tensor(su_r[:], su_r[:], su_kf[:], ALU.add), reads=["su_r", "su_kf"], writes=["su_r"])
        S.op("act", (lambda w: lambda e: e.activation(cs[:, w, :], su_r[:], AF.Sin))(which), reads=["su_r"], writes=["cs"])
    barrier()
    if STOP == "setup":
        return finish()

    wtiles = []
    for l in range(L):
        for c0 in (1536, 2560, 512, 1024, 0, 2048):
            wtiles.append(("in", l, c0))
        for c0 in (0, 512):
            wtiles.append(("out", l, c0))
        for hf in range(2):
            for f in range(11):
                wtiles.append(("gu", l, f))
            for mp in range(4):
                for kh in range(2):
                    wtiles.append(("dn", l, (mp, kh)))
    wstate = {"issued": 0}

    def w_issue(upto):
        while wstate["issued"] <= min(upto, len(wtiles) - 1):
            i = wstate["issued"]
            kind, l, a = wtiles[i]
            s = i % NW
            slot = wsl[s]
            if kind in ("in", "out"):
                src = (w_in if kind == "in" else w_out)[l].rearrange("(kc p) n -> p kc n", p=128)[:, :, a:a + 512]
                dst = slot[:].rearrange("p (kc n) -> p kc n", kc=8)
                S.op("pool", (lambda d_, s_: lambda e: e.dma_start(out=d_, in_=s_))(dst, src),
                     writes=["w%d" % s], dma=True, tag="w%d" % s, nophase=True)
            elif kind == "gu":
                v = w_gu[l].rearrange("(kc p) n -> p kc n", p=128)
                dst = slot[:].rearrange("p (kc n) -> p kc n", kc=8)
                S.op("pool", (lambda d_, s_: lambda e: e.dma_start(out=d_, in_=s_))(dst[:, :, 0:256], v[:, :, a * 256:(a + 1) * 256]),
                     writes=["w%d" % s], dma=True, tag="w%d" % s, nophase=True)
                S.op("pool", (lambda d_, s_: lambda e: e.dma_start(out=d_, in_=s_))(dst[:, :, 256:512], v[:, :, FFN + a * 256:FFN + (a + 1) * 256]),
                     writes=["w%d" % s], dma=True, tag="w%d" % s, nophase=True)
            else:
                mp, kh = a
                v = w_dn[l].rearrange("(kc p) n -> p kc n", p=128)
                dst = slot[:, 0:2816].rearrange("p (kc n) -> p kc n", kc=11)
                S.op("pool", (lambda d_, s_: lambda e: e.dma_start(out=d_, in_=s_))(dst, v[:, kh * 11:(kh + 1) * 11, mp * 256:(mp + 1) * 256]),
                     writes=["w%d" % s], dma=True, tag="w%d" % s, nophase=True)
            wstate["issued"] += 1

    wcur = {"i": 0}

    def w_get():
        i = wcur["i"]
        w_issue(i)
        return i % NW, wsl[i % NW]

    def w_done():
        wcur["i"] += 1
        w_issue(wcur["i"] - 1 + NW)

    w_issue(NW - 1)

    cnt = {"mm": 0, "ss": 0, "rs": 0, "sq": 0, "tmpf": 0, "stg": 0, "rq": 0, "trb": 0, "mixb": 0}

    def rr(key, n):
        v = cnt[key] % n
        cnt[key] += 1
        return v

    def mm_group(bank, pairs, reads):
        def fn(e):
            ins = None
            n = len(pairs)
            for i, (a_, b_) in enumerate(pairs):
                ins = e.matmul(ps[bank][:, :], a_, b_, start=(i == 0), stop=(i == n - 1))
            return ins
        S.op("pe", fn, reads=reads, writes=["ps%d" % bank])

    def rstd_from(ssbank, dim):
        r = rr("rs", 3)
        S.op("dve", lambda e: e.tensor_scalar(rs[r][:], ps[ssbank][:, :], 1.0 / dim, EPS, ALU.mult, ALU.add),
             reads=["ps%d" % ssbank], writes=["rs%d" % r])
        S.op("act", lambda e: e.activation(rs[r][:], rs[r][:], AF.Sqrt), reads=["rs%d" % r], writes=["rs%d" % r])
        S.op("dve", lambda e: e.reciprocal(rs[r][:], rs[r][:]), reads=["rs%d" % r], writes=["rs%d" % r])
        return r

    def sq_ones(src_ap_fn, nch, src_res, ssbank, first, last):
        base = 0 if (cnt["sq"] % 2 == 0) else 4
        cnt["sq"] += 1
        assert nch <= 4
        S.op("act", lambda e: e.activation(sq[:, base:base + nch, :], src_ap_fn(), AF.Square),
             reads=src_res, writes=["sq%d" % base])

        def fn(e):
            ins = None
            for c in range(nch):
                ins = e.matmul(ps[ssbank][:, :], ones_bf, sq[:, base + c, :], start=(first and c == 0), stop=(last and c == nch - 1))
            return ins
        S.op("pe", fn, reads=["sq%d" % base, "cbf"], writes=["ps%d" % ssbank])

    def norm_x_to(dst_fn, dst_res, tbg, gcol0):
        ssb = 4 + rr("ss", 2)
        for half in range(2):
            sq_ones((lambda h_: lambda: x_sb[:, 4 * h_:4 * h_ + 4, tbg * TB:(tbg + 1) * TB])(half), 4, ["x%d" % tbg], ssb, half == 0, half == 1)
        r = rstd_from(ssb, D)
        for c in range(8):
            S.op("dve", (lambda c_: lambda e: e.scalar_tensor_tensor(dst_fn(c_), x_sb[:, c_, tbg * TB:(tbg + 1) * TB], gt[:, gcol0 + c_:gcol0 + c_ + 1], rs[r][:], ALU.mult, ALU.mult))(c),
                 reads=["x%d" % tbg, "gt", "rs%d" % r], writes=[dst_res])

    for l in range(L):
        g0 = l * GT_PER_LAYER
        for tb in range(NTB):
            norm_x_to((lambda tb_: lambda c_: hm[:, c_, tb_ * TB:(tb_ + 1) * TB])(tb), "h%d" % tb, tb, g0 + 0)

        pending = []

        def rotary_store(bank, c, tb, dst_dram, dst_res):
            r = rr("rq", 2)
            S.op("act", lambda e: e.activation(rq[r][:], ps[bank][:, :], AF.Copy), reads=["ps%d" % bank], writes=["rq%d" % r])

            def rest():
                S.op("pe", lambda e: e.matmul(ps[3][:, :], perm, rq[r][:], start=True, stop=True), reads=["rq%d" % r, "cbf"], writes=["ps3"])
                t1 = rr("tmpf", 4); t2 = rr("tmpf", 4)
                S.op("dve", lambda e: e.tensor_tensor(tmpf[t1][:], ps[bank][:, :], cs[:, 0, tb * TB:(tb + 1) * TB], ALU.mult),
                     reads=["ps%d" % bank, "cs"], writes=["tmpf%d" % t1])
                S.op("dve", lambda e: e.tensor_tensor(tmpf[t2][:], ps[3][:, :], cs[:, 1, tb * TB:(tb + 1) * TB], ALU.mult),
                     reads=["ps3", "cs"], writes=["tmpf%d" % t2])
                s_ = rr("stg", 4)
                S.op("pool", lambda e: e.tensor_tensor(stg[s_][:], tmpf[t1][:], tmpf[t2][:], ALU.add),
                     reads=["tmpf%d" % t1, "tmpf%d" % t2], writes=["stg%d" % s_])
                S.op("sp", lambda e: e.dma_start(out=dst_dram[c * 128:(c + 1) * 128, tb * TB:(tb + 1) * TB], in_=stg[s_][:]),
                     reads=["stg%d" % s_], writes=[dst_res], dma=True, tag="stg%d" % s_)
            pending.append(rest)

        def flush_pending(keep=0):
            while len(pending) > keep:
                pending.pop(0)()

        if STOP == "P1a":
            return finish()
        for kind in ("u", "C", "k", "v", "q", "B"):
            if STOP == "P1u" and kind == "C":
                return finish()
            if STOP == "P1C" and kind == "k":
                return finish()
            if STOP == "P1k" and kind == "v":
                return finish()
            ws, slot = w_get()
            wv = slot[:].rearrange("p (kc n) -> p kc n", kc=8)
            for mi in range(4):
                for tb in range(NTB):
                    bank = rr("mm", 3)
                    mm_group(bank, [(wv[:, kc, mi * 128:(mi + 1) * 128], hm[:, kc, tb * TB:(tb + 1) * TB]) for kc in range(8)],
                             ["w%d" % ws, "h%d" % tb])
                    flush_pending()
                    zs = z[:, mi, 1 + tb * TB:1 + (tb + 1) * TB]
                    if kind == "u":
                        S.op("act", (lambda b_, o_: lambda e: e.activation(o_, ps[b_][:, :], AF.Copy))(bank, zs),
                             reads=["ps%d" % bank], writes=["z%d" % mi])
                    elif kind == "C":
                        S.op("dve", (lambda b_, o_: lambda e: e.tensor_tensor(o_, o_, ps[b_][:, :], ALU.mult))(bank, zs),
                             reads=["ps%d" % bank, "z%d" % mi], writes=["z%d" % mi])
                    elif kind == "B":
                        S.op("act", (lambda b_, o_: lambda e: e.activation(o_, ps[b_][:, :], AF.Copy))(bank, bc[:, mi, tb * TB:(tb + 1) * TB]),
                             reads=["ps%d" % bank], writes=["bc%d" % mi])
                    elif kind == "k":
                        rotary_store(bank, mi, tb, snd_k, "snd%d" % mi)
                    elif kind == "q":
                        rotary_store(bank, mi, tb, q_scr, "qs%d" % mi)
                    else:
                        s = rr("stg", 4)
                        S.op("act", (lambda b_, s_: lambda e: e.activation(stg[s_][:], ps[b_][:, :], AF.Copy))(bank, s),
                             reads=["ps%d" % bank], writes=["stg%d" % s])
                        S.op("sp", (lambda s_, mi_, tb_: lambda e: e.dma_start(out=snd_v[mi_ * 128:(mi_ + 1) * 128, tb_ * TB:(tb_ + 1) * TB], in_=stg[s_][:]))(s, mi, tb),
                             reads=["stg%d" % s], writes=["snd%d" % (4 + mi)], dma=True, tag="stg%d" % s)
            flush_pending()
            w_done()
            if kind == "C":
                for f, c0 in ((0, 1), (1, T - 15)):
                    dst = bass.AP(sndz_t, f * 8192, [[16, 128], [2048, 4], [1, 16]])
                    S.op("sp", (lambda d_, c0_: lambda e: e.dma_start(out=d_, in_=z[:, :, c0_:c0_ + 16]))(dst, c0),
                         reads=["z%d" % i for i in range(4)], writes=["sndz"], dma=True, tag="sndz")
                allgather(snd_z, rcv_z, ["sndz"], "rcvz", "ccz")
            if kind == "k":
                allgather(snd_k, rcv_k, ["snd%d" % i for i in range(4)], "rcvk", "cck")
            if kind == "v":
                allgather(snd_v, rcv_v, ["snd%d" % i for i in range(4, 8)], "rcvv", "ccv")
        if STOP == "P1g":
            return finish()
        for f, (rank, slab) in enumerate(((0, 1), (1, 0))):
            src = bass.AP(rcvz_t, rank * 16384 + slab * 8192, [[16, 128], [2048, 4], [1, 16]])
            S.op("sp", (lambda f_, s_: lambda e: e.dma_start(out=zh[:, f_, :, :], in_=s_))(f, src),
                 reads=["rcvz"], writes=["zh"], dma=True, tag="ldzh")
        S.op("dve", lambda e: e.tensor_scalar(z[:, :, 0:1], zh[:, 0, :, 15:16], cf[:, 1:2], None, ALU.mult),
             reads=["zh", "cf"], writes=["z%d" % i for i in range(4)])
        S.op("dve", lambda e: e.tensor_scalar(z[:, :, T + 1:T + 2], zh[:, 1, :, 0:1], cf[:, 2:3], None, ALU.mult),
             reads=["zh", "cf"], writes=["z%d" % i for i in range(4)])
        for c in range(4):
            for tb in range(NTB):
                t1 = rr("tmpf", 4)
                wc = lambda k, c_=c, g0_=g0: gt[:, g0_ + 40 + 4 * k + c_:g0_ + 40 + 4 * k + c_ + 1]
                zz = lambda k, c_=c, tb_=tb: z[:, c_, tb_ * TB + k:tb_ * TB + k + TB]
                S.op("dve", (lambda t_, zz_, wc_: lambda e: e.tensor_scalar(tmpf[t_][:], zz_(0), wc_(0), None, ALU.mult))(t1, zz, wc),
                     reads=["z%d" % c, "gt"], writes=["tmpf%d" % t1])
                S.op("dve", (lambda t_, zz_, wc_: lambda e: e.scalar_tensor_tensor(tmpf[t_][:], zz_(1), wc_(1), tmpf[t_][:], ALU.mult, ALU.add))(t1, zz, wc),
                     reads=["z%d" % c, "gt", "tmpf%d" % t1], writes=["tmpf%d" % t1])
                S.op("dve", (lambda t_, zz_, wc_: lambda e: e.scalar_tensor_tensor(tmpf[t_][:], zz_(2), wc_(2), tmpf[t_][:], ALU.mult, ALU.add))(t1, zz, wc),
                     reads=["z%d" % c, "gt", "tmpf%d" % t1], writes=["tmpf%d" % t1])
                S.op("pool", (lambda t_, c_, tb_: lambda e: e.tensor_tensor(hm[:, 4 + c_, tb_ * TB:(tb_ + 1) * TB], tmpf[t_][:], bc[:, c_, tb_ * TB:(tb_ + 1) * TB], ALU.mult))(t1, c, tb),
                     reads=["tmpf%d" % t1, "bc%d" % c], writes=["hmB%d" % tb])
        barrier()
        if STOP == "P1":
            return finish()

        S.op("pool", lambda e: e.memset(vt[:], 1.0), writes=["vt"])
        blk = 0
        for hp in range(4):
            S.op("sp", (lambda hp_: lambda e: e.dma_start(out=q_sb[:], in_=q_scr[hp_ * 128:(hp_ + 1) * 128, :]))(hp),
                 reads=["qs%d" % hp], writes=["q_sb"], dma=True, tag="ldq")
            for (buf, sn, rc, rres, soff, res, tg) in ((k_ext, snd_k, rcv_k, "rcvk", 0, "k_ext", "ldk"), (v_ext, snd_v, rcv_v, "rcvv", 4, "v_ext", "ldv")):
                r0 = hp * 128
                S.op("sp", (lambda b_, rc_, r_: lambda e: e.dma_start(out=b_[:, 0:1024], in_=rc_[r_:r_ + 128, 1024:2048]))(buf, rc, r0),
                     reads=[rres], writes=[res], dma=True, tag=tg)
                S.op("sp", (lambda b_, sn_, r_: lambda e: e.dma_start(out=b_[:, 1024:3072], in_=sn_[r_:r_ + 128, :]))(buf, sn, r0),
                     reads=["snd%d" % (soff + hp)], writes=[res], dma=True, tag=tg)
                S.op("sp", (lambda b_, rc_, r_: lambda e: e.dma_start(out=b_[:, 3072:4096], in_=rc_[512 + r_:512 + r_ + 128, 0:1024]))(buf, rc, r0),
                     reads=[rres], writes=[res], dma=True, tag=tg)
            S.op("pool", lambda e: e.tensor_copy(q16[:].rearrange("p (r m) -> p r m", r=16), q_sb[:].rearrange("p (m r) -> p r m", r=16)),
                 reads=["q_sb"], writes=["q16"])
            for d in (1, 4, 16):
                nblk = 16 // d
                ntile = d * (nblk + 1)
                for t0 in range(0, ntile, 8):
                    nt = min(8, ntile - t0)
                    tbk = rr("trb", 2)

                    def trfn(e, t0=t0, nt=nt, d=d, nblk=nblk, tbk=tbk):
                        ins = None
                        for i in range(nt):
                            rho, j = divmod(t0 + i, nblk + 1)
                            e0 = 1024 + rho + d * (128 * j - 64)
                            ins = e.transpose(psb[tbk][:, i * 128:(i + 1) * 128], v_ext[:, sl(e0, 128, d)], ident)
                        return ins
                    S.op("pe", trfn, reads=["v_ext", "cbf"], writes=["ps%d" % (6 + tbk)])
                    pv = lambda nt=nt, tbk=tbk: psb[tbk][:, 0:nt * 128].rearrange("p (t c) -> p t c", c=128)
                    S.op("act", (lambda t0_, nt_, pv_: lambda e: e.activation(vt[:, t0_:t0_ + nt_, 0:64], pv_()[:, :, 0:64], AF.Copy))(t0, nt, pv),
                         reads=["ps%d" % (6 + tbk)], writes=["vt"])
                    S.op("dve", (lambda t0_, nt_, pv_: lambda e: e.tensor_copy(vt[:, t0_:t0_ + nt_, 96:160], pv_()[:, :, 64:128]))(t0, nt, pv),
                         reads=["ps%d" % (6 + tbk)], writes=["vt"])
                blocks = [(rho, j) for rho in range(d) for j in range(nblk)]
                pars = []
                for bi in range(len(blocks)):
                    pars.append(blk % 2)
                    blk += 1

                def emit_scores(bi, d=d, nblk=nblk, blocks=blocks, pars=pars):
                    rho, j = blocks[bi]
                    par = pars[bi]
                    variant = (1 if j == 0 else 0) + (2 if j == nblk - 1 else 0)
                    if d == 16:
                        qsrc = lambda p0, rho=rho: q16[p0:p0 + 64, rho * 128:(rho + 1) * 128]
                    else:
                        qsrc = lambda p0, rho=rho, j=j, d=d: q_sb[p0:p0 + 64, sl(rho + d * 128 * j, 128, d)]
                    eA = 1024 + rho + d * (128 * j - 64)
                    eB = eA + d * 128
                    for hpar, p0 in ((0, 0), (1, 64)):
                        sbank = 2 * hpar + par

                        def sfn(e, sbank=sbank, p0=p0, eA=eA, eB=eB, d=d, qsrc=qsrc, variant=variant):
                            e.matmul(ps[sbank][:, 0:256], ident, maskv(variant), start=True, stop=False)
                            e.matmul(ps[sbank][:, 0:128], k_ext[p0:p0 + 64, sl(eA, 128, d)], qsrc(p0), start=False, stop=False)
                            return e.matmul(ps[sbank][:, 128:256], k_ext[p0:p0 + 64, sl(eB, 128, d)], qsrc(p0), start=False, stop=True)
                        S.op("pe", sfn, reads=["k_ext", "q16" if d == 16 else "q_sb", "cbf"], writes=["ps%d" % sbank])
                        pb = 2 * hpar + par
                        S.op("act", (lambda pb_, sb_: lambda e: e.activation(pbuf[pb_][:], ps[sb_][:, 0:256], AF.Exp, scale=0.125))(pb, sbank),
                             reads=["ps%d" % sbank], writes=["pb%d" % pb])

                def emit_pv(bi, d=d, nblk=nblk, blocks=blocks, pars=pars):
                    rho, j = blocks[bi]
                    par = pars[bi]
                    tA = rho * (nblk + 1) + j
                    obank = 4 + (bi // 2) % 2
                    osl = bi % 2
                    for hpar in (0, 1):
                        pb = 2 * hpar + par

                        def ofn(e, hpar=hpar, pb=pb, obank=obank, osl=osl, tA=tA):
                            if hpar == 0:
                                o_ap = ps[obank][0:65, osl * 128:(osl + 1) * 128]
                                la, lb = vt[:, tA, 0:65], vt[:, tA + 1, 0:65]
                            else:
                                o_ap = ps[obank][:, 256 + osl * 128:256 + (osl + 1) * 128]
                                la, lb = vt[:, tA, 32:160], vt[:, tA + 1, 32:160]
                            e.matmul(o_ap, la, pbuf[pb][:, 0:128], start=True, stop=False)
                            return e.matmul(o_ap, lb, pbuf[pb][:, 128:256], start=False, stop=True)
                        S.op("pe", ofn, reads=["vt", "pb%d" % pb], writes=["ps%d" % obank])
                    if bi % 2 == 1 or bi == len(blocks) - 1:
                        nb = 2 if bi % 2 == 1 else 1
                        b0 = bi - (nb - 1)
                        rho0, j0 = blocks[b0]
                        if d == 16:
                            dst = lambda t_, p_, rho0=rho0, nb=nb: t_[p_, :].rearrange("p (m r) -> p r m", r=16)[:, rho0:rho0 + nb, :]
                            src = lambda ap_: ap_.rearrange("p (r m) -> p r m", m=128)
                        else:
                            dst = lambda t_, p_, rho0=rho0, j0=j0, d=d, nb=nb: t_[p_, sl(rho0 + d * 128 * j0, 128 * nb, d)]
                            src = lambda ap_: ap_
                        for hpar in (0, 1):
                            if hpar == 0:
                                pp = slice(0, 65); o_t = oacc_e; c0 = 0; res = "oacc_e"
                            else:
                                pp = slice(0, 128); o_t = oacc_o; c0 = 256; res = "oacc_o"
                            if d == 1:
                                S.op("dve", (lambda o_t_, pp_, c0_, nb_, ob_, dst_, src_: lambda e: e.tensor_copy(dst_(o_t_, pp_), src_(ps[ob_][pp_, c0_:c0_ + 128 * nb_])))(o_t, pp, c0, nb, obank, dst, src),
                                     reads=["ps%d" % obank], writes=[res])
                            else:
                                S.op("dve", (lambda o_t_, pp_, c0_, nb_, ob_, dst_, src_: lambda e: e.tensor_tensor(dst_(o_t_, pp_), dst_(o_t_, pp_), src_(ps[ob_][pp_, c0_:c0_ + 128 * nb_]), ALU.add))(o_t, pp, c0, nb, obank, dst, src),
                                     reads=["ps%d" % obank, res], writes=[res])

                emit_scores(0)
                for bi in range(len(blocks)):
                    if bi + 1 < len(blocks):
                        emit_scores(bi + 1)
                    emit_pv(bi)
            S.op("dve", lambda e: e.reciprocal(oacc_e[64:65, :], oacc_e[64:65, :]), reads=["oacc_e"], writes=["oacc_e"])
            S.op("dve", lambda e: e.reciprocal(oacc_o[32:33, :], oacc_o[32:33, :]), reads=["oacc_o"], writes=["oacc_o"])
            for tb in range(NTB):
                cs_ = slice(tb * TB, (tb + 1) * TB)
                S.op("pe", (lambda c_: lambda e: e.matmul(ps[1][0:64, :], onesf[64:65, 0:64], oacc_e[64:65, c_], start=True, stop=True))(cs_),
                     reads=["oacc_e", "onesf"], writes=["ps1"])
                S.op("pe", (lambda c_: lambda e: e.matmul(ps[0][:, :], selo[32:33, :], oacc_o[32:33, c_], start=True, stop=True))(cs_),
                     reads=["oacc_o", "selo"], writes=["ps0"])
                S.op("dve", (lambda c_, hp_: lambda e: e.tensor_tensor(hm[0:64, hp_, c_], oacc_e[0:64, c_], ps[1][0:64, :], ALU.mult))(cs_, hp),
                     reads=["oacc_e", "ps1"], writes=["hmA%d" % tb])
                S.op("dve", (lambda c_, hp_: lambda e: e.tensor_tensor(hm[64:128, hp_, c_], oacc_o[64:128, c_], ps[0][64:128, :], ALU.mult))(cs_, hp),
                     reads=["oacc_o", "ps0"], writes=["hmA%d" % tb])
        barrier()
        if STOP == "P2":
            return finish()

        i0 = wcur["i"]
        w_issue(i0 + 1)
        ws0, ws1 = i0 % NW, (i0 + 1) % NW
        slot0, slot1 = wsl[ws0], wsl[ws1]
        wo = [slot0[:].rearrange("p (kc n) -> p kc n", kc=8), slot1[:].rearrange("p (kc n) -> p kc n", kc=8)]
        wres = ["w%d" % ws0, "w%d" % ws1]
        for tb in range(NTB):
            cs_ = slice(tb * TB, (tb + 1) * TB)
            for grp, gcol in ((0, g0 + 8), (1, g0 + 12)):
                ssb = 4 + rr("ss", 2)
                sq_ones((lambda g_, c_: lambda: hm[:, 4 * g_:4 * g_ + 4, c_])(grp, cs_), 4, ["hmA%d" % tb, "hmB%d" % tb], ssb, True, True)
                r = rstd_from(ssb, 512)
                for c in range(4):
                    cc = 4 * grp + c
                    S.op("dve", (lambda cc_, c_, r_, gc_: lambda e: e.scalar_tensor_tensor(hm[:, cc_, c_], hm[:, cc_, c_], gt[:, gc_:gc_ + 1], rs[r_][:], ALU.mult, ALU.mult))(cc, cs_, r, gcol + c),
                         reads=["hmA%d" % tb, "hmB%d" % tb, "gt", "rs%d" % r], writes=["hmA%d" % tb if grp == 0 else "hmB%d" % tb])
        for tb in range(NTB):
            cs_ = slice(tb * TB, (tb + 1) * TB)
            mbi = rr("mixb", 2)
            mb = mixbufs[mbi]
            ssb = 4 + rr("ss", 2)
            for m in range(8):
                bank = rr("mm", 4)
                mm_group(bank, [(wo[m // 4][:, kc, (m % 4) * 128:(m % 4 + 1) * 128], hm[:, kc, cs_]) for kc in range(8)],
                         [wres[m // 4], "hmA%d" % tb, "hmB%d" % tb])
                S.op("act", (lambda b_, m_, mb_: lambda e: e.activation(mb_[:, m_, :], ps[b_][:, :], AF.Copy))(bank, m, mb),
                     reads=["ps%d" % bank], writes=["mixbuf%d" % mbi])
                sqs = 0 if (cnt["sq"] % 2 == 0) else 4
                cnt["sq"] += 1
                S.op("act", (lambda b_, s_: lambda e: e.activation(sq[:, s_, :], ps[b_][:, :], AF.Square))(bank, sqs),
                     reads=["ps%d" % bank], writes=["sq%d" % sqs])
                S.op("pe", (lambda s_, m_, sb_: lambda e: e.matmul(ps[sb_][:, :], ones_bf, sq[:, s_, :], start=(m_ == 0), stop=(m_ == 7)))(sqs, m, ssb),
                     reads=["sq%d" % sqs, "cbf"], writes=["ps%d" % ssb])
            r = rstd_from(ssb, D)
            for m in range(8):
                t1 = rr("tmpf", 4)
                S.op("dve", (lambda t_, m_, r_, gc_, mb_: lambda e: e.scalar_tensor_tensor(tmpf[t_][:], mb_[:, m_, :], gt[:, gc_:gc_ + 1], rs[r_][:], ALU.mult, ALU.mult))(t1, m, r, g0 + 16 + m, mb),
                     reads=["mixbuf%d" % mbi, "gt", "rs%d" % r], writes=["tmpf%d" % t1])
                S.op("pool", (lambda t_, m_, c_: lambda e: e.tensor_tensor(x_sb[:, m_, c_], x_sb[:, m_, c_], tmpf[t_][:], ALU.add))(t1, m, cs_),
                     reads=["tmpf%d" % t1, "x%d" % tb], writes=["x%d" % tb])
        w_done(); w_done()
        barrier()
        if STOP == "P3":
            return finish()

        for hf in range(2):
            for tb2 in range(2):
                tbg = hf * 2 + tb2
                norm_x_to((lambda tb2_: lambda c_: h2[:, c_, tb2_ * TB:(tb2_ + 1) * TB])(tb2), "h2_%d" % tb2, tbg, g0 + 24)
            for f in range(11):
                ws, slot = w_get()
                wv = slot[:].rearrange("p (kc n) -> p kc n", kc=8)
                for tb2 in range(2):
                    for mi in range(2):
                        bg = rr("mm", 4)
                        mm_group(bg, [(wv[:, kc, mi * 128:(mi + 1) * 128], h2[:, kc, tb2 * TB:(tb2 + 1) * TB]) for kc in range(8)],
                                 ["w%d" % ws, "h2_%d" % tb2])
                        bu = rr("mm", 4)
                        mm_group(bu, [(wv[:, kc, 256 + mi * 128:256 + (mi + 1) * 128], h2[:, kc, tb2 * TB:(tb2 + 1) * TB]) for kc in range(8)],
                                 ["w%d" % ws, "h2_%d" % tb2])
                        t1 = rr("tmpf", 4)
                        S.op("act", (lambda t_, b_: lambda e: e.activation(tmpf[t_][:], ps[b_][:, :], AF.Silu))(t1, bg),
                             reads=["ps%d" % bg], writes=["tmpf%d" % t1])
                        S.op("dve", (lambda t_, b_, fc_, tb2_: lambda e: e.tensor_tensor(hid[:, fc_, tb2_ * TB:(tb2_ + 1) * TB], tmpf[t_][:], ps[b_][:, :], ALU.mult))(t1, bu, 2 * f + mi, tb2),
                             reads=["tmpf%d" % t1, "ps%d" % bu], writes=["hid%d" % tb2])
                w_done()
            ssbs = [4, 5]
            for mp in range(4):
                slots = []
                for kh in range(2):
                    ws, slot = w_get()
                    wv = slot[:, 0:2816].rearrange("p (kc n) -> p kc n", kc=11)
                    for mi in range(2):
                        for tb2 in range(2):
                            bank = mi * 2 + tb2

                            def dfn(e, wv=wv, mi=mi, tb2=tb2, bank=bank, kh=kh):
                                ins = None
                                for kk in range(11):
                                    ins = e.matmul(ps[bank][:, :], wv[:, kk, mi * 128:(mi + 1) * 128], hid[:, kh * 11 + kk, tb2 * TB:(tb2 + 1) * TB],
                                                   start=(kh == 0 and kk == 0), stop=(kh == 1 and kk == 10))
                                return ins
                            S.op("pe", dfn, reads=["w%d" % ws, "hid%d" % tb2], writes=["ps%d" % bank])
                    w_done()
                for mi in range(2):
                    for tb2 in range(2):
                        bank = mi * 2 + tb2
                        m = mp * 2 + mi
                        S.op("act", (lambda b_, m_, tb2_: lambda e: e.activation(f_bf[:, m_, tb2_ * TB:(tb2_ + 1) * TB], ps[b_][:, :], AF.Copy))(bank, m, tb2),
                             reads=["ps%d" % bank], writes=["f_bf%d" % tb2])
                        sqs = 0 if (cnt["sq"] % 2 == 0) else 4
                        cnt["sq"] += 1
                        S.op("act", (lambda b_, s_: lambda e: e.activation(sq[:, s_, :], ps[b_][:, :], AF.Square))(bank, sqs),
                             reads=["ps%d" % bank], writes=["sq%d" % sqs])
                        S.op("pe", (lambda s_, m_, sb_: lambda e: e.matmul(ps[sb_][:, :], ones_bf, sq[:, s_, :], start=(m_ == 0), stop=(m_ == 7)))(sqs, m, ssbs[tb2]),
                             reads=["sq%d" % sqs, "cbf"], writes=["ps%d" % ssbs[tb2]])
            for tb2 in range(2):
                tbg = hf * 2 + tb2
                r = rstd_from(ssbs[tb2], D)
                for m in range(8):
                    t1 = rr("tmpf", 4)
                    S.op("dve", (lambda t_, m_, r_, tb2_, gc_: lambda e: e.scalar_tensor_tensor(tmpf[t_][:], f_bf[:, m_, tb2_ * TB:(tb2_ + 1) * TB], gt[:, gc_:gc_ + 1], rs[r_][:], ALU.mult, ALU.mult))(t1, m, r, tb2, g0 + 32 + m),
                         reads=["f_bf%d" % tb2, "gt", "rs%d" % r], writes=["tmpf%d" % t1])
                    S.op("pool", (lambda t_, m_, tbg_: lambda e: e.tensor_tensor(x_sb[:, m_, tbg_ * TB:(tbg_ + 1) * TB], x_sb[:, m_, tbg_ * TB:(tbg_ + 1) * TB], tmpf[t_][:], ALU.add))(t1, m, tbg),
                         reads=["tmpf%d" % t1, "x%d" % tbg], writes=["x%d" % tbg])
        barrier()

    return finish()


def _const_tables(validP, validN):
    cbf = np.zeros((128, 384 + 1024), np.float32)
    cbf[:, 0:128] = np.eye(128)
    cbf[:, 128:256] = 1.0
    perm = np.zeros((128, 128), np.float32)
    for par in range(2):
        for dd in range(8):
            p = par * 64 + dd
            perm[p + 8, p] = -1.0
            perm[p, p + 8] = 1.0
    cbf[:, 256:384] = perm
    a = np.arange(128)[:, None]
    b = np.arange(128)[None, :]
    mA = np.where(b <= a, 0.0, NEG)
    mB = np.where(b >= a, 0.0, NEG)
    for v in range(4):
        ma = mA.copy(); mb = mB.copy()
        if (v & 1) and not validP:
            ma[0:64, :] = NEG
        if (v & 2) and not validN:
            mb[64:128, :] = NEG
        cbf[:, 384 + 256 * v:384 + 256 * v + 128] = ma
        cbf[:, 384 + 256 * v + 128:384 + 256 * (v + 1)] = mb
    cf = np.zeros((128, 4), np.float32)
    inv = np.float32(500000.0) ** (-(np.arange(0, 16, 2, dtype=np.float32)) / np.float32(16.0))
    for p in range(128):
        dd = p % 64
        if dd < 16:
            cf[p, 0] = inv[dd % 8]
    cf[:, 1] = 1.0 if validP else 0.0
    cf[:, 2] = 1.0 if validN else 0.0
    return cbf.astype(NPBF), cf


def _gain_table(layers, pre_mix_norm, conv_w, attn_out_norm, conv_out_norm, post_mix_norm, pre_ffn_norm, post_ffn_norm):
    gtab = np.zeros((128, len(layers) * GT_PER_LAYER), np.float32)

    def put(col, vec):
        n = vec.shape[0] // 128
        gtab[:, col:col + n] = vec.reshape(n, 128).T
    for i, l in enumerate(layers):
        b = i * GT_PER_LAYER
        put(b + 0, pre_mix_norm[l]); put(b + 8, attn_out_norm[l]); put(b + 12, conv_out_norm[l])
        put(b + 16, post_mix_norm[l]); put(b + 24, pre_ffn_norm[l]); put(b + 32, post_ffn_norm[l])
        for k in range(3):
            put(b + 40 + 4 * k, conv_w[l, k])
    return gtab


_PROG = {}


def _run(L, layers, xTs, positions, P):
    if L not in _PROG:
        _PROG[L] = build_program(L)
    nc = _PROG[L]
    gtab = _gain_table(layers, P["pre_mix_norm"], P["conv_w"], P["attn_out_norm"], P["conv_out_norm"],
                       P["post_mix_norm"], P["pre_ffn_norm"], P["post_ffn_norm"])
    lsel = slice(layers[0], layers[-1] + 1)
    wi = np.ascontiguousarray(P["w_in"][lsel]); wo = np.ascontiguousarray(P["w_out"][lsel])
    wg = np.ascontiguousarray(P["w_gate_up"][lsel]); wd = np.ascontiguousarray(P["w_down"][lsel])
    in_maps = []
    for c in range(8):
        b, r = divmod(c, 2)
        cbf, cf = _const_tables(validP=(r == 1), validN=(r == 0))
        in_maps.append({
            "xT": xTs[c],
            "pos": np.ascontiguousarray(positions[b:b + 1, r * T:(r + 1) * T]).astype(np.int32),
            "w_in": wi, "w_out": wo, "w_gu": wg, "w_dn": wd,
            "gt": gtab, "cbf": cbf, "cf": cf,
        })
    res = run_bass_kernel_spmd(nc, in_maps, core_ids=list(range(8)))
    return [np.asarray(res.results[c]["yT"]) for c in range(8)]


def kernel(x, positions, pre_mix_norm, w_in, conv_w, attn_out_norm, conv_out_norm, w_out,
           post_mix_norm, pre_ffn_norm, w_gate_up, w_down, post_ffn_norm):
    P = dict(pre_mix_norm=np.asarray(pre_mix_norm, np.float32), w_in=np.asarray(w_in, np.float32),
             conv_w=np.asarray(conv_w, np.float32), attn_out_norm=np.asarray(attn_out_norm, np.float32),
             conv_out_norm=np.asarray(conv_out_norm, np.float32), w_out=np.asarray(w_out, np.float32),
             post_mix_norm=np.asarray(post_mix_norm, np.float32), pre_ffn_norm=np.asarray(pre_ffn_norm, np.float32),
             w_gate_up=np.asarray(w_gate_up, np.float32), w_down=np.asarray(w_down, np.float32),
             post_ffn_norm=np.asarray(post_ffn_norm, np.float32))
    x = np.asarray(x, np.float32)
    positions = np.asarray(positions)
    xTs = []
    for c in range(8):
        b, r = divmod(c, 2)
        xTs.append(np.ascontiguousarray(x[b, r * T:(r + 1) * T, :].T))
    if FUSED:
        xTs = _run(DEPTH, list(range(DEPTH)), xTs, positions, P)
    else:
        for l in range(DEPTH):
            xTs = _run(1, [l], xTs, positions, P)
    out = np.empty((4, 4096, D), np.float32)
    for c in range(8):
        b, r = divmod(c, 2)
        out[b, r * T:(r + 1) * T, :] = xTs[c].T
    return out
```

```python
import collections
import contextlib
import math

import numpy as np
import ml_dtypes

import concourse.bass as bass
import concourse.mybir as mybir
from concourse.bass_utils import run_bass_kernel_spmd

F32 = mybir.dt.float32
BF16 = mybir.dt.bfloat16
I32 = mybir.dt.int32
ALU = mybir.AluOpType
AF = mybir.ActivationFunctionType
NPBF = ml_dtypes.bfloat16

FUSED = True
DEPTH = 4
D = 1024
T = 2048
TB = 512
NTB = 4
FFN = 2816
NEG = -30000.0
EPS = 1e-6
GT_PER_LAYER = 52
CBW = 2560
REPLICA_GROUPS = [[0, 1], [2, 3], [4, 5], [6, 7]]
STOP = None
ENG_NAMES = ("pe", "act", "dve", "pool", "sp")


def sl(start, n, step=1):
    return slice(start, start + (n - 1) * step + 1, step)


class Op:
    __slots__ = ("idx", "eng", "fn", "reads", "writes", "dma", "tag", "inc", "count",
                 "waits", "marked", "known", "kdma")

    def __init__(self, idx, eng, fn, reads, writes, dma, tag, inc):
        self.idx = idx; self.eng = eng; self.fn = fn
        self.reads = tuple(reads); self.writes = tuple(writes)
        self.dma = dma; self.tag = tag; self.inc = inc
        self.count = 0; self.waits = []; self.marked = False
        self.known = None; self.kdma = None


class Sched:
    def __init__(self):
        self.ops = []
        self.last_write = {}
        self.readers = collections.defaultdict(dict)
        self.tag_count = collections.defaultdict(int)
        self.tag_inc = {}
        self.eng_known = {e: {} for e in ENG_NAMES}
        self.eng_kdma = {e: {} for e in ENG_NAMES}
        self.eng_ops = {e: [] for e in ENG_NAMES}

    def op(self, eng, fn, reads=(), writes=(), dma=False, tag=None, inc=16, nophase=False):
        i = len(self.ops)
        reads = list(reads); writes = list(writes)
        if not nophase:
            reads.append("PHASE")
        for r in list(reads):
            if r.startswith("ps"):
                reads.remove(r)
                if r not in writes:
                    writes.append(r)
        o = Op(i, eng, fn, reads, writes, dma, tag, inc)
        self.ops.append(o)
        deps = {}
        for r in o.reads:
            j = self.last_write.get(r)
            if j is not None:
                deps[j] = "RAW"
        for r in o.writes:
            j = self.last_write.get(r)
            if j is not None and j not in deps:
                deps[j] = "WAW"
            for j in self.readers[r].values():
                if j not in deps:
                    deps[j] = "WAR"
        known = self.eng_known[eng]
        kdma = self.eng_kdma[eng]
        waits = {}
        for j, kind in sorted(deps.items()):
            d = self.ops[j]
            if d.dma:
                if kdma.get(d.tag, 0) >= d.count:
                    continue
                c = self.tag_count[d.tag]
                waits[("dma", d.tag)] = c
                kdma[d.tag] = c
            else:
                if d.eng == eng:
                    if eng in ("pe", "sp") or kind != "RAW":
                        continue
                if known.get(d.eng, -1) >= j:
                    continue
                waits[("eng", d.eng)] = max(waits.get(("eng", d.eng), -1), j)
        for key, v in waits.items():
            if key[0] == "eng":
                d = self.ops[v]
                d.marked = True
                known[key[1]] = max(known.get(key[1], -1), v)
                for e2, k2 in d.known.items():
                    if known.get(e2, -1) < k2:
                        known[e2] = k2
                for t2, c2 in d.kdma.items():
                    if kdma.get(t2, 0) < c2:
                        kdma[t2] = c2
        o.waits = list(waits.items())
        if dma:
            assert self.tag_inc.setdefault(tag, inc) == inc
            self.tag_count[tag] += 1
            o.count = self.tag_count[tag]
        o.known = dict(known)
        o.kdma = dict(kdma)
        for r in o.reads:
            self.readers[r][("dma", i) if dma else eng] = i
        for r in o.writes:
            self.last_write[r] = i
            self.readers[r] = {}
        self.eng_ops[eng].append(o)
        return o

    def emit(self, nc, final_wait_tags=()):
        rank = {}
        for e in ENG_NAMES:
            c = 0
            for o in self.eng_ops[e]:
                if o.marked and not o.dma:
                    c += 1
                    rank[o.idx] = c
        with contextlib.ExitStack() as st:
            esem = {e: st.enter_context(nc.semaphore("s_" + e)) for e in ("pe", "act", "dve", "pool")}
            tsem = {t: st.enter_context(nc.semaphore("t_" + str(t))) for t in sorted(self.tag_count)}
            block = st.enter_context(nc.Block())
            handles = {"pe": block.tensor, "act": block.scalar, "dve": block.vector,
                       "pool": block.gpsimd, "sp": block.sync}

            def body(ename):
                def _f(eng):
                    for o in self.eng_ops[ename]:
                        for key, v in o.waits:
                            if key[0] == "eng":
                                eng.wait_ge(esem[key[1]], rank[v])
                            else:
                                eng.wait_ge(tsem[key[1]], v * self.tag_inc[key[1]])
                        ins = o.fn(eng)
                        if o.dma:
                            ins.then_inc(tsem[o.tag], o.inc)
                        elif o.marked:
                            ins.then_inc(esem[ename], 1)
                    if ename == "sp":
                        for t in final_wait_tags:
                            eng.wait_ge(tsem[t], self.tag_count[t] * self.tag_inc[t])
                return _f

            for e in ENG_NAMES:
                handles[e](body(e))


class SbAlloc:
    BASE = 16512
    LIMIT = 229344

    def __init__(self, nc):
        self.nc = nc
        self.off = self.BASE
        self.n = 0

    def at(self, name, shape, dtype, offset):
        nbytes = int(np.prod(shape[1:])) * (2 if dtype == BF16 else 4)
        assert offset % 32 == 0 and offset + nbytes <= self.LIMIT, (name, offset, nbytes)
        self.n += 1
        return self.nc.alloc_sbuf_tensor_at("%s_%d" % (name, self.n), list(shape), dtype, offset=offset), nbytes

    def new(self, name, shape, dtype):
        t, nbytes = self.at(name, shape, dtype, self.off)
        self.off += (nbytes + 31) // 32 * 32
        return t


def build_program(L):
    nc = bass.Bass("TRN2", target_bir_lowering=False)
    S = Sched()

    def dram(name, shape, dt, kind):
        return nc.dram_tensor(name, list(shape), dt, kind=kind)

    xT = dram("xT", [D, T], F32, "ExternalInput").ap()
    pos = dram("pos", [1, T], I32, "ExternalInput").ap()
    w_in = dram("w_in", [L, D, 3072], F32, "ExternalInput").ap()
    w_out = dram("w_out", [L, D, D], F32, "ExternalInput").ap()
    w_gu = dram("w_gu", [L, D, 2 * FFN], F32, "ExternalInput").ap()
    w_dn = dram("w_dn", [L, FFN, D], F32, "ExternalInput").ap()
    gt_d = dram("gt", [128, L * GT_PER_LAYER], F32, "ExternalInput").ap()
    cbf_d = dram("cbf", [128, CBW], BF16, "ExternalInput").ap()
    cf_d = dram("cf", [128, 4], F32, "ExternalInput").ap()
    yT = dram("yT", [D, T], F32, "ExternalOutput").ap()
    q_scr_t = dram("q_scr", [512, T], BF16, "Internal")
    sndk_t = dram("snd_k", [512, T], BF16, "Internal")
    sndv_t = dram("snd_v", [512, T], BF16, "Internal")
    sndz_t = dram("snd_z", [8, T], BF16, "Internal")
    rcvk_t = dram("rcv_k", [1024, T], BF16, "Internal")
    rcvv_t = dram("rcv_v", [1024, T], BF16, "Internal")
    rcvz_t = dram("rcv_z", [16, T], BF16, "Internal")
    q_scr = q_scr_t.ap()
    snd_k, snd_v, snd_z = sndk_t.ap(), sndv_t.ap(), sndz_t.ap()
    rcv_k, rcv_v, rcv_z = rcvk_t.ap(), rcvv_t.ap(), rcvz_t.ap()

    def allgather(src, dst, reads, wres, tag):
        S.op("pool", lambda e: e.collective_compute("AllGather", ALU.bypass, replica_groups=REPLICA_GROUPS, ins=[src], outs=[dst]),
             reads=reads, writes=[wres], dma=True, tag=tag, inc=1)

    A = SbAlloc(nc)
    x_sb = A.new("x", [128, 8, T], F32)
    cs = A.new("cs", [128, 2, T], BF16)
    cbf = A.new("cbf", [128, CBW], BF16)
    cf = A.new("cf", [128, 4], F32)
    gt = A.new("gt", [128, L * GT_PER_LAYER], F32)
    onesf = A.new("onesf", [128, 128], F32)
    selo = A.new("selo", [128, 128], F32)
    bar = A.new("bar", [128, 8], F32)
    NW = 3
    wsl = [A.new("w%d" % i, [128, 4096], BF16) for i in range(NW)]
    sq = A.new("sq", [128, 8, TB], BF16)
    rs = [A.new("rs%d" % i, [128, TB], F32) for i in range(2)]
    rq = [A.new("rq%d" % i, [128, TB], BF16) for i in range(2)]
    stg = [A.new("stg%d" % i, [128, TB], BF16) for i in range(4)]
    zh = A.new("zh", [128, 2, 4, 16], BF16)
    regB = A.off
    tmpf = [A.new("tmpf%d" % i, [128, TB], F32) for i in range(4)]
    oacc_e, _ = A.at("oacc_e", [128, T], F32, regB)
    arena = A.off
    hm, _ = A.at("hm", [128, 8, T], BF16, arena)
    regA = arena + 32768
    z, _ = A.at("z", [128, 4, T + 2], BF16, regA)
    bc, _ = A.at("bc", [128, 4, T], BF16, regA + 16416)
    k_ext, _ = A.at("k_ext", [128, 4096], BF16, regA)
    v_ext, _ = A.at("v_ext", [128, 4096], BF16, regA + 8192)
    vt, _ = A.at("vt", [128, 32, 160], BF16, regA + 16384)
    q_sb, _ = A.at("q_sb", [128, T], BF16, regA + 16384 + 10240)
    o2 = regA + 32800
    q16, _ = A.at("q16", [128, T], BF16, o2)
    pbuf = [A.at("pb%d" % i, [128, 256], BF16, o2 + 4096 + 512 * i)[0] for i in range(4)]
    oacc_o, _ = A.at("oacc_o", [128, T], F32, o2 + 4096 + 2048)
    mixbufs = [A.at("mixbuf%d" % i, [128, 8, TB], BF16, regA + 8192 * i)[0] for i in range(2)]
    h2, _ = A.at("h2", [128, 8, 1024], BF16, arena)
    hid, _ = A.at("hid", [128, 22, 1024], BF16, arena + 16384)
    f_bf, _ = A.at("f_bf", [128, 8, 1024], BF16, arena + 16384 + 45056)
    assert arena + 16384 + 45056 + 16384 <= A.LIMIT
    su_i, _ = A.at("su_i", [128, T], I32, arena)
    su_ang, _ = A.at("su_ang", [128, T], F32, arena + 8192)
    su_a2, _ = A.at("su_a2", [128, T], F32, arena + 16384)
    su_kf, _ = A.at("su_kf", [128, T], F32, arena + 24576)
    su_ki, _ = A.at("su_ki", [128, T], I32, arena + 32768)
    su_r, _ = A.at("su_r", [128, T], F32, arena + 40960)

    ps = [nc.alloc_psum_tensor("ps%d" % i, [128, TB], F32) for i in range(6)]
    psb = [nc.alloc_psum_tensor("ps%d" % i, [128, 1024], BF16) for i in (6, 7)]

    ident = cbf[:, 0:128]
    ones_bf = cbf[:, 128:256]
    perm = cbf[:, 256:384]

    ident2 = cbf[:, 1408:1536]

    def maskv(v):
        return cbf[:, 384 + 256 * v: 384 + 256 * (v + 1)]

    def maskv2(v):
        return cbf[:, 1536 + 256 * v: 1536 + 256 * (v + 1)]

    def barrier():
        S.op("pool", lambda e: e.memset(bar[:], 0.0), writes=["PHASE"], nophase=True)

    def finish():
        for c in range(8):
            S.op("sp", (lambda c_: lambda e: e.dma_start(out=yT[c_ * 128:(c_ + 1) * 128, :], in_=x_sb[:, c_, :]))(c),
                 reads=["x%d" % tb for tb in range(NTB)], dma=True, tag="yout")
        S.emit(nc, final_wait_tags=["yout"])
        return nc

    S.op("sp", lambda e: e.dma_start(out=cbf[:], in_=cbf_d), writes=["cbf"], dma=True, tag="c0")
    S.op("sp", lambda e: e.dma_start(out=cf[:], in_=cf_d), writes=["cf"], dma=True, tag="c0")
    S.op("sp", lambda e: e.dma_start(out=gt[:], in_=gt_d), writes=["gt"], dma=True, tag="c0")
    S.op("sp", lambda e: e.dma_start(out=su_i[:], in_=pos.partition_broadcast(128)), writes=["su_i"], dma=True, tag="c1")
    for c in range(8):
        S.op("sp", (lambda c: lambda e: e.dma_start(out=x_sb[:, c, :], in_=xT[c * 128:(c + 1) * 128, :]))(c),
             writes=["x%d" % tb for tb in range(NTB)], dma=True, tag="xin")
    S.op("pool", lambda e: e.memset(onesf[:], 1.0), writes=["onesf"])
    S.op("pool", lambda e: e.memset(selo[:, 0:64], 0.0), writes=["selo"])
    S.op("pool", lambda e: e.memset(selo[:, 64:128], 1.0), writes=["selo"])
    TWO_PI = 2.0 * math.pi
    S.op("dve", lambda e: e.tensor_copy(su_ang[:], su_i[:]), reads=["su_i"], writes=["su_ang"])
    S.op("dve", lambda e: e.tensor_scalar(su_ang[:], su_ang[:], cf[:, 0:1], None, ALU.mult), reads=["su_ang", "cf"], writes=["su_ang"])
    for which, shift in ((1, 0.0), (0, 0.5 * math.pi)):
        S.op("dve", (lambda sh: lambda e: e.tensor_scalar(su_a2[:], su_ang[:], sh, None, ALU.add))(shift), reads=["su_ang"], writes=["su_a2"])
        S.op("dve", lambda e: e.tensor_scalar(su_kf[:], su_a2[:], 1.0 / TWO_PI, None, ALU.mult), reads=["su_a2"], writes=["su_kf"])
        S.op("dve", lambda e: e.tensor_copy(su_ki[:], su_kf[:]), reads=["su_kf"], writes=["su_ki"])
        S.op("dve", lambda e: e.tensor_copy(su_kf[:], su_ki[:]), reads=["su_ki"], writes=["su_kf"])
        S.op("dve", lambda e: e.scalar_tensor_tensor(su_r[:], su_kf[:], -TWO_PI, su_a2[:], ALU.mult, ALU.add), reads=["su_kf", "su_a2"], writes=["su_r"])
        S.op("dve", lambda e: e.tensor_scalar(su_kf[:], su_r[:], math.pi, TWO_PI, ALU.is_gt, ALU.mult), reads=["su_r"], writes=["su_kf"])
        S.op("dve", lambda e: e.tensor_tensor(su_r[:], su_r[:], su_kf[:], ALU.subtract), reads=["su_r", "su_kf"], writes=["su_r"])
        S.op("dve", lambda e: e.tensor_scalar(su_kf[:], su_r[:], -math.pi, TWO_PI, ALU.is_lt, ALU.mult), reads=["su_r"], writes=["su_kf"])
        S.op("dve", lambda e: e.tensor_tensor(su_r[:], su_r[:], su_kf[:], ALU.add), reads=["su_r", "su_kf"], writes=["su_r"])
        S.op("act", (lambda w: lambda e: e.activation(cs[:, w, :], su_r[:], AF.Sin))(which), reads=["su_r"], writes=["cs"])
    barrier()
    if STOP == "setup":
        return finish()

    wtiles = []
    for l in range(L):
        for c0 in (1536, 2560, 512, 1024, 0, 2048):
            wtiles.append(("in", l, c0))
        for c0 in (0, 512):
            wtiles.append(("out", l, c0))
        for hf in range(2):
            for f in range(11):
                wtiles.append(("gu", l, f))
            for mp in range(4):
                for kh in range(2):
                    wtiles.append(("dn", l, (mp, kh)))
    wstate = {"issued": 0}

    def w_issue(upto):
        while wstate["issued"] <= min(upto, len(wtiles) - 1):
            i = wstate["issued"]
            kind, l, a = wtiles[i]
            s = i % NW
            slot = wsl[s]
            if kind in ("in", "out"):
                src = (w_in if kind == "in" else w_out)[l].rearrange("(kc p) n -> p kc n", p=128)[:, :, a:a + 512]
                dst = slot[:].rearrange("p (kc n) -> p kc n", kc=8)
                S.op("pool", (lambda d_, s_: lambda e: e.dma_start(out=d_, in_=s_))(dst, src),
                     writes=["w%d" % s], dma=True, tag="w%d" % s, nophase=True)
            elif kind == "gu":
                v = w_gu[l].rearrange("(kc p) n -> p kc n", p=128)
                dst = slot[:].rearrange("p (kc n) -> p kc n", kc=8)
                S.op("pool", (lambda d_, s_: lambda e: e.dma_start(out=d_, in_=s_))(dst[:, :, 0:256], v[:, :, a * 256:(a + 1) * 256]),
                     writes=["w%d" % s], dma=True, tag="w%d" % s, nophase=True)
                S.op("pool", (lambda d_, s_: lambda e: e.dma_start(out=d_, in_=s_))(dst[:, :, 256:512], v[:, :, FFN + a * 256:FFN + (a + 1) * 256]),
                     writes=["w%d" % s], dma=True, tag="w%d" % s, nophase=True)
            else:
                mp, kh = a
                v = w_dn[l].rearrange("(kc p) n -> p kc n", p=128)
                dst = slot[:, 0:2816].rearrange("p (kc n) -> p kc n", kc=11)
                S.op("pool", (lambda d_, s_: lambda e: e.dma_start(out=d_, in_=s_))(dst, v[:, kh * 11:(kh + 1) * 11, mp * 256:(mp + 1) * 256]),
                     writes=["w%d" % s], dma=True, tag="w%d" % s, nophase=True)
            wstate["issued"] += 1

    wcur = {"i": 0}

    def w_get():
        i = wcur["i"]
        w_issue(i)
        return i % NW, wsl[i % NW]

    def w_done():
        wcur["i"] += 1
        w_issue(wcur["i"] - 1 + NW)

    w_issue(NW - 1)

    cnt = {"mm": 0, "ss": 0, "rs": 0, "sq": 0, "tmpf": 0, "stg": 0, "rq": 0, "trb": 0, "mixb": 0}

    def rr(key, n):
        v = cnt[key] % n
        cnt[key] += 1
        return v

    def mm_group(bank, pairs, reads):
        def fn(e):
            ins = None
            n = len(pairs)
            for i, (a_, b_) in enumerate(pairs):
                ins = e.matmul(ps[bank][:, :], a_, b_, start=(i == 0), stop=(i == n - 1))
            return ins
        S.op("pe", fn, reads=reads, writes=["ps%d" % bank])

    def rstd_from(ssbank, dim):
        r = rr("rs", 2)
        S.op("dve", lambda e: e.tensor_scalar(rs[r][:], ps[ssbank][:, :], 1.0 / dim, EPS, ALU.mult, ALU.add),
             reads=["ps%d" % ssbank], writes=["rs%d" % r])
        S.op("act", lambda e: e.activation(rs[r][:], rs[r][:], AF.Sqrt), reads=["rs%d" % r], writes=["rs%d" % r])
        S.op("dve", lambda e: e.reciprocal(rs[r][:], rs[r][:]), reads=["rs%d" % r], writes=["rs%d" % r])
        return r

    def sq_ones(src_ap_fn, nch, src_res, ssbank, first, last):
        base = 0 if (cnt["sq"] % 2 == 0) else 4
        cnt["sq"] += 1
        assert nch <= 4
        S.op("act", lambda e: e.activation(sq[:, base:base + nch, :], src_ap_fn(), AF.Square),
             reads=src_res, writes=["sq%d" % base])

        def fn(e):
            ins = None
            for c in range(nch):
                ins = e.matmul(ps[ssbank][:, :], ones_bf, sq[:, base + c, :], start=(first and c == 0), stop=(last and c == nch - 1))
            return ins
        S.op("pe", fn, reads=["sq%d" % base, "cbf"], writes=["ps%d" % ssbank])

    def norm_x_to(dst_fn, dst_res, tbg, gcol0):
        ssb = 4 + rr("ss", 2)
        for half in range(2):
            sq_ones((lambda h_: lambda: x_sb[:, 4 * h_:4 * h_ + 4, tbg * TB:(tbg + 1) * TB])(half), 4, ["x%d" % tbg], ssb, half == 0, half == 1)
        r = rstd_from(ssb, D)
        for c in range(8):
            S.op("dve", (lambda c_: lambda e: e.scalar_tensor_tensor(dst_fn(c_), x_sb[:, c_, tbg * TB:(tbg + 1) * TB], gt[:, gcol0 + c_:gcol0 + c_ + 1], rs[r][:], ALU.mult, ALU.mult))(c),
                 reads=["x%d" % tbg, "gt", "rs%d" % r], writes=[dst_res])

    for l in range(L):
        g0 = l * GT_PER_LAYER
        for tb in range(NTB):
            norm_x_to((lambda tb_: lambda c_: hm[:, c_, tb_ * TB:(tb_ + 1) * TB])(tb), "h%d" % tb, tb, g0 + 0)

        pending = []

        def rotary_store(bank, c, tb, dst_dram, dst_res):
            r = rr("rq", 2)
            S.op("act", lambda e: e.activation(rq[r][:], ps[bank][:, :], AF.Copy), reads=["ps%d" % bank], writes=["rq%d" % r])

            def rest():
                S.op("pe", lambda e: e.matmul(ps[3][:, :], perm, rq[r][:], start=True, stop=True), reads=["rq%d" % r, "cbf"], writes=["ps3"])
                t1 = rr("tmpf", 4); t2 = rr("tmpf", 4)
                S.op("dve", lambda e: e.tensor_tensor(tmpf[t1][:], ps[bank][:, :], cs[:, 0, tb * TB:(tb + 1) * TB], ALU.mult),
                     reads=["ps%d" % bank, "cs"], writes=["tmpf%d" % t1])
                S.op("dve", lambda e: e.tensor_tensor(tmpf[t2][:], ps[3][:, :], cs[:, 1, tb * TB:(tb + 1) * TB], ALU.mult),
                     reads=["ps3", "cs"], writes=["tmpf%d" % t2])
                s_ = rr("stg", 4)
                S.op("pool", lambda e: e.tensor_tensor(stg[s_][:], tmpf[t1][:], tmpf[t2][:], ALU.add),
                     reads=["tmpf%d" % t1, "tmpf%d" % t2], writes=["stg%d" % s_])
                S.op("sp", lambda e: e.dma_start(out=dst_dram[c * 128:(c + 1) * 128, tb * TB:(tb + 1) * TB], in_=stg[s_][:]),
                     reads=["stg%d" % s_], writes=[dst_res], dma=True, tag="stg%d" % s_)
            pending.append(rest)

        def flush_pending(keep=0):
            while len(pending) > keep:
                pending.pop(0)()

        if STOP == "P1a":
            return finish()
        for kind in ("u", "C", "k", "v", "q", "B"):
            if STOP == "P1u" and kind == "C":
                return finish()
            if STOP == "P1C" and kind == "k":
                return finish()
            if STOP == "P1k" and kind == "v":
                return finish()
            ws, slot = w_get()
            wv = slot[:].rearrange("p (kc n) -> p kc n", kc=8)
            for mi in range(4):
                for tb in range(NTB):
                    bank = rr("mm", 3)
                    mm_group(bank, [(wv[:, kc, mi * 128:(mi + 1) * 128], hm[:, kc, tb * TB:(tb + 1) * TB]) for kc in range(8)],
                             ["w%d" % ws, "h%d" % tb])
                    flush_pending()
                    zs = z[:, mi, 1 + tb * TB:1 + (tb + 1) * TB]
                    if kind == "u":
                        S.op("act", (lambda b_, o_: lambda e: e.activation(o_, ps[b_][:, :], AF.Copy))(bank, zs),
                             reads=["ps%d" % bank], writes=["z%d" % mi])
                    elif kind == "C":
                        S.op("dve", (lambda b_, o_: lambda e: e.tensor_tensor(o_, o_, ps[b_][:, :], ALU.mult))(bank, zs),
                             reads=["ps%d" % bank, "z%d" % mi], writes=["z%d" % mi])
                    elif kind == "B":
                        S.op("act", (lambda b_, o_: lambda e: e.activation(o_, ps[b_][:, :], AF.Copy))(bank, bc[:, mi, tb * TB:(tb + 1) * TB]),
                             reads=["ps%d" % bank], writes=["bc%d" % mi])
                    elif kind == "k":
                        rotary_store(bank, mi, tb, snd_k, "snd%d" % mi)
                    elif kind == "q":
                        rotary_store(bank, mi, tb, q_scr, "qs%d" % mi)
                    else:
                        s = rr("stg", 4)
                        S.op("act", (lambda b_, s_: lambda e: e.activation(stg[s_][:], ps[b_][:, :], AF.Copy))(bank, s),
                             reads=["ps%d" % bank], writes=["stg%d" % s])
                        S.op("sp", (lambda s_, mi_, tb_: lambda e: e.dma_start(out=snd_v[mi_ * 128:(mi_ + 1) * 128, tb_ * TB:(tb_ + 1) * TB], in_=stg[s_][:]))(s, mi, tb),
                             reads=["stg%d" % s], writes=["snd%d" % (4 + mi)], dma=True, tag="stg%d" % s)
            flush_pending()
            w_done()
            if kind == "C":
                for f, c0 in ((0, 1), (1, T - 15)):
                    dst = bass.AP(sndz_t, f * 8192, [[16, 128], [2048, 4], [1, 16]])
                    S.op("sp", (lambda d_, c0_: lambda e: e.dma_start(out=d_, in_=z[:, :, c0_:c0_ + 16]))(dst, c0),
                         reads=["z%d" % i for i in range(4)], writes=["sndz"], dma=True, tag="sndz")
                allgather(snd_z, rcv_z, ["sndz"], "rcvz", "ccz")
            if kind == "k":
                allgather(snd_k, rcv_k, ["snd%d" % i for i in range(4)], "rcvk", "cck")
            if kind == "v":
                allgather(snd_v, rcv_v, ["snd%d" % i for i in range(4, 8)], "rcvv", "ccv")
        if STOP == "P1g":
            return finish()
        for f, (rank, slab) in enumerate(((0, 1), (1, 0))):
            src = bass.AP(rcvz_t, rank * 16384 + slab * 8192, [[16, 128], [2048, 4], [1, 16]])
            S.op("sp", (lambda f_, s_: lambda e: e.dma_start(out=zh[:, f_, :, :], in_=s_))(f, src),
                 reads=["rcvz"], writes=["zh"], dma=True, tag="ldzh")
        S.op("dve", lambda e: e.tensor_scalar(z[:, :, 0:1], zh[:, 0, :, 15:16], cf[:, 1:2], None, ALU.mult),
             reads=["zh", "cf"], writes=["z%d" % i for i in range(4)])
        S.op("dve", lambda e: e.tensor_scalar(z[:, :, T + 1:T + 2], zh[:, 1, :, 0:1], cf[:, 2:3], None, ALU.mult),
             reads=["zh", "cf"], writes=["z%d" % i for i in range(4)])
        for c in range(4):
            for tb in range(NTB):
                t1 = rr("tmpf", 4)
                wc = lambda k, c_=c, g0_=g0: gt[:, g0_ + 40 + 4 * k + c_:g0_ + 40 + 4 * k + c_ + 1]
                zz = lambda k, c_=c, tb_=tb: z[:, c_, tb_ * TB + k:tb_ * TB + k + TB]
                S.op("dve", (lambda t_, zz_, wc_: lambda e: e.tensor_scalar(tmpf[t_][:], zz_(0), wc_(0), None, ALU.mult))(t1, zz, wc),
                     reads=["z%d" % c, "gt"], writes=["tmpf%d" % t1])
                S.op("dve", (lambda t_, zz_, wc_: lambda e: e.scalar_tensor_tensor(tmpf[t_][:], zz_(1), wc_(1), tmpf[t_][:], ALU.mult, ALU.add))(t1, zz, wc),
                     reads=["z%d" % c, "gt", "tmpf%d" % t1], writes=["tmpf%d" % t1])
                S.op("dve", (lambda t_, zz_, wc_: lambda e: e.scalar_tensor_tensor(tmpf[t_][:], zz_(2), wc_(2), tmpf[t_][:], ALU.mult, ALU.add))(t1, zz, wc),
                     reads=["z%d" % c, "gt", "tmpf%d" % t1], writes=["tmpf%d" % t1])
                S.op("pool", (lambda t_, c_, tb_: lambda e: e.tensor_tensor(hm[:, 4 + c_, tb_ * TB:(tb_ + 1) * TB], tmpf[t_][:], bc[:, c_, tb_ * TB:(tb_ + 1) * TB], ALU.mult))(t1, c, tb),
                     reads=["tmpf%d" % t1, "bc%d" % c], writes=["hmB%d" % tb])
        barrier()
        if STOP == "P1":
            return finish()

        S.op("pool", lambda e: e.memset(vt[:], 1.0), writes=["vt"])
        blk = 0
        for hp in range(4):
            S.op("sp", (lambda hp_: lambda e: e.dma_start(out=q_sb[:], in_=q_scr[hp_ * 128:(hp_ + 1) * 128, :]))(hp),
                 reads=["qs%d" % hp], writes=["q_sb"], dma=True, tag="ldq")
            for (buf, sn, rc, rres, soff, res, tg) in ((k_ext, snd_k, rcv_k, "rcvk", 0, "k_ext", "ldk"), (v_ext, snd_v, rcv_v, "rcvv", 4, "v_ext", "ldv")):
                r0 = hp * 128
                S.op("sp", (lambda b_, rc_, r_: lambda e: e.dma_start(out=b_[:, 0:1024], in_=rc_[r_:r_ + 128, 1024:2048]))(buf, rc, r0),
                     reads=[rres], writes=[res], dma=True, tag=tg)
                S.op("sp", (lambda b_, sn_, r_: lambda e: e.dma_start(out=b_[:, 1024:3072], in_=sn_[r_:r_ + 128, :]))(buf, sn, r0),
                     reads=["snd%d" % (soff + hp)], writes=[res], dma=True, tag=tg)
                S.op("sp", (lambda b_, rc_, r_: lambda e: e.dma_start(out=b_[:, 3072:4096], in_=rc_[512 + r_:512 + r_ + 128, 0:1024]))(buf, rc, r0),
                     reads=[rres], writes=[res], dma=True, tag=tg)
            S.op("pool", lambda e: e.tensor_copy(q16[:].rearrange("p (r m) -> p r m", r=16), q_sb[:].rearrange("p (m r) -> p r m", r=16)),
                 reads=["q_sb"], writes=["q16"])
            for d in (1, 4, 16):
                nblk = 16 // d
                ntile = d * (nblk + 1)
                for t0 in range(0, ntile, 8):
                    nt = min(8, ntile - t0)
                    tbk = rr("trb", 2)

                    def trfn(e, t0=t0, nt=nt, d=d, nblk=nblk, tbk=tbk):
                        ins = None
                        for i in range(nt):
                            rho, j = divmod(t0 + i, nblk + 1)
                            e0 = 1024 + rho + d * (128 * j - 64)
                            ins = e.transpose(psb[tbk][:, i * 128:(i + 1) * 128], v_ext[:, sl(e0, 128, d)], ident)
                        return ins
                    S.op("pe", trfn, reads=["v_ext", "cbf"], writes=["ps%d" % (6 + tbk)])
                    pv = lambda nt=nt, tbk=tbk: psb[tbk][:, 0:nt * 128].rearrange("p (t c) -> p t c", c=128)
                    S.op("act", (lambda t0_, nt_, pv_: lambda e: e.activation(vt[:, t0_:t0_ + nt_, 0:64], pv_()[:, :, 0:64], AF.Copy))(t0, nt, pv),
                         reads=["ps%d" % (6 + tbk)], writes=["vt"])
                    S.op("dve", (lambda t0_, nt_, pv_: lambda e: e.tensor_copy(vt[:, t0_:t0_ + nt_, 96:160], pv_()[:, :, 64:128]))(t0, nt, pv),
                         reads=["ps%d" % (6 + tbk)], writes=["vt"])
                blocks = [(rho, j) for rho in range(d) for j in range(nblk)]
                pars = []
                for bi in range(len(blocks)):
                    pars.append(blk % 2)
                    blk += 1

                def emit_scores(bi, d=d, nblk=nblk, blocks=blocks, pars=pars):
                    rho, j = blocks[bi]
                    par = pars[bi]
                    variant = (1 if j == 0 else 0) + (2 if j == nblk - 1 else 0)
                    if d == 16:
                        qsrc = lambda p0, rho=rho: q16[p0:p0 + 64, rho * 128:(rho + 1) * 128]
                    else:
                        qsrc = lambda p0, rho=rho, j=j, d=d: q_sb[p0:p0 + 64, sl(rho + d * 128 * j, 128, d)]
                    eA = 1024 + rho + d * (128 * j - 64)
                    eB = eA + d * 128
                    for hpar, p0 in ((0, 0), (1, 64)):
                        sbank = 2 * hpar + par

                        def sfn(e, sbank=sbank, p0=p0, eA=eA, eB=eB, d=d, qsrc=qsrc, variant=variant):
                            e.matmul(ps[sbank][:, 0:256], ident[p0:p0 + 64, :], maskv(variant)[p0:p0 + 64, :], start=True, stop=False)
                            e.matmul(ps[sbank][:, 0:256], ident2[p0:p0 + 64, :], maskv2(variant)[p0:p0 + 64, :], start=False, stop=False)
                            e.matmul(ps[sbank][:, 0:128], k_ext[p0:p0 + 64, sl(eA, 128, d)], qsrc(p0), start=False, stop=False)
                            return e.matmul(ps[sbank][:, 128:256], k_ext[p0:p0 + 64, sl(eB, 128, d)], qsrc(p0), start=False, stop=True)
                        S.op("pe", sfn, reads=["k_ext", "q16" if d == 16 else "q_sb", "cbf"], writes=["ps%d" % sbank])
                        pb = 2 * hpar + par
                        S.op("act", (lambda pb_, sb_: lambda e: e.activation(pbuf[pb_][:], ps[sb_][:, 0:256], AF.Exp, scale=0.125))(pb, sbank),
                             reads=["ps%d" % sbank], writes=["pb%d" % pb])

                def emit_pv(bi, d=d, nblk=nblk, blocks=blocks, pars=pars):
                    rho, j = blocks[bi]
                    par = pars[bi]
                    tA = rho * (nblk + 1) + j
                    obank = 4 + (bi // 2) % 2
                    osl = bi % 2
                    for hpar in (0, 1):
                        pb = 2 * hpar + par

                        def ofn(e, hpar=hpar, pb=pb, obank=obank, osl=osl, tA=tA):
                            if hpar == 0:
                                o_ap = ps[obank][0:65, osl * 128:(osl + 1) * 128]
                                la, lb = vt[:, tA, 0:65], vt[:, tA + 1, 0:65]
                            else:
                                o_ap = ps[obank][:, 256 + osl * 128:256 + (osl + 1) * 128]
                                la, lb = vt[:, tA, 32:160], vt[:, tA + 1, 32:160]
                            e.matmul(o_ap, la, pbuf[pb][:, 0:128], start=True, stop=False)
                            return e.matmul(o_ap, lb, pbuf[pb][:, 128:256], start=False, stop=True)
                        S.op("pe", ofn, reads=["vt", "pb%d" % pb], writes=["ps%d" % obank])
                    if bi % 2 == 1 or bi == len(blocks) - 1:
                        nb = 2 if bi % 2 == 1 else 1
                        b0 = bi - (nb - 1)
                        rho0, j0 = blocks[b0]
                        if d == 16:
                            dst = lambda t_, p_, rho0=rho0, nb=nb: t_[p_, :].rearrange("p (m r) -> p r m", r=16)[:, rho0:rho0 + nb, :]
                            src = lambda ap_: ap_.rearrange("p (r m) -> p r m", m=128)
                        else:
                            dst = lambda t_, p_, rho0=rho0, j0=j0, d=d, nb=nb: t_[p_, sl(rho0 + d * 128 * j0, 128 * nb, d)]
                            src = lambda ap_: ap_
                        for hpar in (0, 1):
                            if hpar == 0:
                                pp = slice(0, 65); o_t = oacc_e; c0 = 0; res = "oacc_e"
                            else:
                                pp = slice(0, 128); o_t = oacc_o; c0 = 256; res = "oacc_o"
                            if d == 1:
                                S.op("dve", (lambda o_t_, pp_, c0_, nb_, ob_, dst_, src_: lambda e: e.tensor_copy(dst_(o_t_, pp_), src_(ps[ob_][pp_, c0_:c0_ + 128 * nb_])))(o_t, pp, c0, nb, obank, dst, src),
                                     reads=["ps%d" % obank], writes=[res])
                            else:
                                S.op("dve", (lambda o_t_, pp_, c0_, nb_, ob_, dst_, src_: lambda e: e.tensor_tensor(dst_(o_t_, pp_), dst_(o_t_, pp_), src_(ps[ob_][pp_, c0_:c0_ + 128 * nb_]), ALU.add))(o_t, pp, c0, nb, obank, dst, src),
                                     reads=["ps%d" % obank, res], writes=[res])

                emit_scores(0)
                for bi in range(len(blocks)):
                    if bi + 1 < len(blocks):
                        emit_scores(bi + 1)
                    emit_pv(bi)
            S.op("dve", lambda e: e.reciprocal(oacc_e[64:65, :], oacc_e[64:65, :]), reads=["oacc_e"], writes=["oacc_e"])
            S.op("dve", lambda e: e.reciprocal(oacc_o[32:33, :], oacc_o[32:33, :]), reads=["oacc_o"], writes=["oacc_o"])
            for tb in range(NTB):
                cs_ = slice(tb * TB, (tb + 1) * TB)
                S.op("pe", (lambda c_: lambda e: e.matmul(ps[1][0:64, :], onesf[64:65, 0:64], oacc_e[64:65, c_], start=True, stop=True))(cs_),
                     reads=["oacc_e", "onesf"], writes=["ps1"])
                S.op("pe", (lambda c_: lambda e: e.matmul(ps[0][:, :], selo[32:33, :], oacc_o[32:33, c_], start=True, stop=True))(cs_),
                     reads=["oacc_o", "selo"], writes=["ps0"])
                S.op("dve", (lambda c_, hp_: lambda e: e.tensor_tensor(hm[0:64, hp_, c_], oacc_e[0:64, c_], ps[1][0:64, :], ALU.mult))(cs_, hp),
                     reads=["oacc_e", "ps1"], writes=["hmA%d" % tb])
                S.op("dve", (lambda c_, hp_: lambda e: e.tensor_tensor(hm[64:128, hp_, c_], oacc_o[64:128, c_], ps[0][64:128, :], ALU.mult))(cs_, hp),
                     reads=["oacc_o", "ps0"], writes=["hmA%d" % tb])
        barrier()
        if STOP == "P2":
            return finish()

        i0 = wcur["i"]
        w_issue(i0 + 1)
        ws0, ws1 = i0 % NW, (i0 + 1) % NW
        slot0, slot1 = wsl[ws0], wsl[ws1]
        wo = [slot0[:].rearrange("p (kc n) -> p kc n", kc=8), slot1[:].rearrange("p (kc n) -> p kc n", kc=8)]
        wres = ["w%d" % ws0, "w%d" % ws1]
        for tb in range(NTB):
            cs_ = slice(tb * TB, (tb + 1) * TB)
            for grp, gcol in ((0, g0 + 8), (1, g0 + 12)):
                ssb = 4 + rr("ss", 2)
                sq_ones((lambda g_, c_: lambda: hm[:, 4 * g_:4 * g_ + 4, c_])(grp, cs_), 4, ["hmA%d" % tb, "hmB%d" % tb], ssb, True, True)
                r = rstd_from(ssb, 512)
                for c in range(4):
                    cc = 4 * grp + c
                    S.op("dve", (lambda cc_, c_, r_, gc_: lambda e: e.scalar_tensor_tensor(hm[:, cc_, c_], hm[:, cc_, c_], gt[:, gc_:gc_ + 1], rs[r_][:], ALU.mult, ALU.mult))(cc, cs_, r, gcol + c),
                         reads=["hmA%d" % tb, "hmB%d" % tb, "gt", "rs%d" % r], writes=["hmA%d" % tb if grp == 0 else "hmB%d" % tb])
        for tb in range(NTB):
            cs_ = slice(tb * TB, (tb + 1) * TB)
            mbi = rr("mixb", 2)
            mb = mixbufs[mbi]
            ssb = 4 + rr("ss", 2)
            for m in range(8):
                bank = rr("mm", 4)
                mm_group(bank, [(wo[m // 4][:, kc, (m % 4) * 128:(m % 4 + 1) * 128], hm[:, kc, cs_]) for kc in range(8)],
                         [wres[m // 4], "hmA%d" % tb, "hmB%d" % tb])
                S.op("act", (lambda b_, m_, mb_: lambda e: e.activation(mb_[:, m_, :], ps[b_][:, :], AF.Copy))(bank, m, mb),
                     reads=["ps%d" % bank], writes=["mixbuf%d" % mbi])
                sqs = 0 if (cnt["sq"] % 2 == 0) else 4
                cnt["sq"] += 1
                S.op("act", (lambda b_, s_: lambda e: e.activation(sq[:, s_, :], ps[b_][:, :], AF.Square))(bank, sqs),
                     reads=["ps%d" % bank], writes=["sq%d" % sqs])
                S.op("pe", (lambda s_, m_, sb_: lambda e: e.matmul(ps[sb_][:, :], ones_bf, sq[:, s_, :], start=(m_ == 0), stop=(m_ == 7)))(sqs, m, ssb),
                     reads=["sq%d" % sqs, "cbf"], writes=["ps%d" % ssb])
            r = rstd_from(ssb, D)
            for m in range(8):
                t1 = rr("tmpf", 4)
                S.op("dve", (lambda t_, m_, r_, gc_, mb_: lambda e: e.scalar_tensor_tensor(tmpf[t_][:], mb_[:, m_, :], gt[:, gc_:gc_ + 1], rs[r_][:], ALU.mult, ALU.mult))(t1, m, r, g0 + 16 + m, mb),
                     reads=["mixbuf%d" % mbi, "gt", "rs%d" % r], writes=["tmpf%d" % t1])
                S.op("pool", (lambda t_, m_, c_: lambda e: e.tensor_tensor(x_sb[:, m_, c_], x_sb[:, m_, c_], tmpf[t_][:], ALU.add))(t1, m, cs_),
                     reads=["tmpf%d" % t1, "x%d" % tb], writes=["x%d" % tb])
        w_done(); w_done()
        barrier()
        if STOP == "P3":
            return finish()

        for hf in range(2):
            for tb2 in range(2):
                tbg = hf * 2 + tb2
                norm_x_to((lambda tb2_: lambda c_: h2[:, c_, tb2_ * TB:(tb2_ + 1) * TB])(tb2), "h2_%d" % tb2, tbg, g0 + 24)
            for f in range(11):
                ws, slot = w_get()
                wv = slot[:].rearrange("p (kc n) -> p kc n", kc=8)
                for tb2 in range(2):
                    for mi in range(2):
                        bg = rr("mm", 4)
                        mm_group(bg, [(wv[:, kc, mi * 128:(mi + 1) * 128], h2[:, kc, tb2 * TB:(tb2 + 1) * TB]) for kc in range(8)],
                                 ["w%d" % ws, "h2_%d" % tb2])
                        bu = rr("mm", 4)
                        mm_group(bu, [(wv[:, kc, 256 + mi * 128:256 + (mi + 1) * 128], h2[:, kc, tb2 * TB:(tb2 + 1) * TB]) for kc in range(8)],
                                 ["w%d" % ws, "h2_%d" % tb2])
                        t1 = rr("tmpf", 4)
                        S.op("act", (lambda t_, b_: lambda e: e.activation(tmpf[t_][:], ps[b_][:, :], AF.Silu))(t1, bg),
                             reads=["ps%d" % bg], writes=["tmpf%d" % t1])
                        S.op("dve", (lambda t_, b_, fc_, tb2_: lambda e: e.tensor_tensor(hid[:, fc_, tb2_ * TB:(tb2_ + 1) * TB], tmpf[t_][:], ps[b_][:, :], ALU.mult))(t1, bu, 2 * f + mi, tb2),
                             reads=["tmpf%d" % t1, "ps%d" % bu], writes=["hid%d" % tb2])
                w_done()
            ssbs = [4, 5]
            for mp in range(4):
                slots = []
                for kh in range(2):
                    ws, slot = w_get()
                    wv = slot[:, 0:2816].rearrange("p (kc n) -> p kc n", kc=11)
                    for mi in range(2):
                        for tb2 in range(2):
                            bank = mi * 2 + tb2

                            def dfn(e, wv=wv, mi=mi, tb2=tb2, bank=bank, kh=kh):
                                ins = None
                                for kk in range(11):
                                    ins = e.matmul(ps[bank][:, :], wv[:, kk, mi * 128:(mi + 1) * 128], hid[:, kh * 11 + kk, tb2 * TB:(tb2 + 1) * TB],
                                                   start=(kh == 0 and kk == 0), stop=(kh == 1 and kk == 10))
                                return ins
                            S.op("pe", dfn, reads=["w%d" % ws, "hid%d" % tb2], writes=["ps%d" % bank])
                    w_done()
                for mi in range(2):
                    for tb2 in range(2):
                        bank = mi * 2 + tb2
                        m = mp * 2 + mi
                        S.op("act", (lambda b_, m_, tb2_: lambda e: e.activation(f_bf[:, m_, tb2_ * TB:(tb2_ + 1) * TB], ps[b_][:, :], AF.Copy))(bank, m, tb2),
                             reads=["ps%d" % bank], writes=["f_bf%d" % tb2])
                        sqs = 0 if (cnt["sq"] % 2 == 0) else 4
                        cnt["sq"] += 1
                        S.op("act", (lambda b_, s_: lambda e: e.activation(sq[:, s_, :], ps[b_][:, :], AF.Square))(bank, sqs),
                             reads=["ps%d" % bank], writes=["sq%d" % sqs])
                        S.op("pe", (lambda s_, m_, sb_: lambda e: e.matmul(ps[sb_][:, :], ones_bf, sq[:, s_, :], start=(m_ == 0), stop=(m_ == 7)))(sqs, m, ssbs[tb2]),
                             reads=["sq%d" % sqs, "cbf"], writes=["ps%d" % ssbs[tb2]])
            for tb2 in range(2):
                tbg = hf * 2 + tb2
                r = rstd_from(ssbs[tb2], D)
                for m in range(8):
                    t1 = rr("tmpf", 4)
                    S.op("dve", (lambda t_, m_, r_, tb2_, gc_: lambda e: e.scalar_tensor_tensor(tmpf[t_][:], f_bf[:, m_, tb2_ * TB:(tb2_ + 1) * TB], gt[:, gc_:gc_ + 1], rs[r_][:], ALU.mult, ALU.mult))(t1, m, r, tb2, g0 + 32 + m),
                         reads=["f_bf%d" % tb2, "gt", "rs%d" % r], writes=["tmpf%d" % t1])
                    S.op("pool", (lambda t_, m_, tbg_: lambda e: e.tensor_tensor(x_sb[:, m_, tbg_ * TB:(tbg_ + 1) * TB], x_sb[:, m_, tbg_ * TB:(tbg_ + 1) * TB], tmpf[t_][:], ALU.add))(t1, m, tbg),
                         reads=["tmpf%d" % t1, "x%d" % tbg], writes=["x%d" % tbg])
        barrier()

    return finish()


def _const_tables(validP, validN):
    cbf = np.zeros((128, CBW), np.float32)
    cbf[:, 0:128] = np.eye(128)
    cbf[:, 128:256] = 1.0
    perm = np.zeros((128, 128), np.float32)
    for par in range(2):
        for dd in range(8):
            p = par * 64 + dd
            perm[p + 8, p] = -1.0
            perm[p, p + 8] = 1.0
    cbf[:, 256:384] = perm
    a = np.arange(128)[:, None]
    b = np.arange(128)[None, :]
    mA = np.where(b <= a, 0.0, NEG)
    mB = np.where(b >= a, 0.0, NEG)
    for v in range(4):
        ma = mA.copy(); mb = mB.copy()
        if (v & 1) and not validP:
            ma[0:64, :] = NEG
        if (v & 2) and not validN:
            mb[64:128, :] = NEG
        cbf[:, 384 + 256 * v:384 + 256 * v + 128] = ma
        cbf[:, 384 + 256 * v + 128:384 + 256 * (v + 1)] = mb
    cbf[:, 1408:1536] = np.roll(cbf[:, 0:128], -64, axis=0)
    cbf[:, 1536:2560] = np.roll(cbf[:, 384:1408], -64, axis=0)
    cf = np.zeros((128, 4), np.float32)
    inv = np.float32(500000.0) ** (-(np.arange(0, 16, 2, dtype=np.float32)) / np.float32(16.0))
    for p in range(128):
        dd = p % 64
        if dd < 16:
            cf[p, 0] = inv[dd % 8]
    cf[:, 1] = 1.0 if validP else 0.0
    cf[:, 2] = 1.0 if validN else 0.0
    return cbf.astype(NPBF), cf


def _gain_table(layers, pre_mix_norm, conv_w, attn_out_norm, conv_out_norm, post_mix_norm, pre_ffn_norm, post_ffn_norm):
    gtab = np.zeros((128, len(layers) * GT_PER_LAYER), np.float32)

    def put(col, vec):
        n = vec.shape[0] // 128
        gtab[:, col:col + n] = vec.reshape(n, 128).T
    for i, l in enumerate(layers):
        b = i * GT_PER_LAYER
        put(b + 0, pre_mix_norm[l]); put(b + 8, attn_out_norm[l]); put(b + 12, conv_out_norm[l])
        put(b + 16, post_mix_norm[l]); put(b + 24, pre_ffn_norm[l]); put(b + 32, post_ffn_norm[l])
        for k in range(3):
            put(b + 40 + 4 * k, conv_w[l, k])
    return gtab


_PROG = {}


def _run(L, layers, xTs, positions, P):
    if L not in _PROG:
        _PROG[L] = build_program(L)
    nc = _PROG[L]
    gtab = _gain_table(layers, P["pre_mix_norm"], P["conv_w"], P["attn_out_norm"], P["conv_out_norm"],
                       P["post_mix_norm"], P["pre_ffn_norm"], P["post_ffn_norm"])
    lsel = slice(layers[0], layers[-1] + 1)
    wi = np.ascontiguousarray(P["w_in"][lsel]); wo = np.ascontiguousarray(P["w_out"][lsel])
    wg = np.ascontiguousarray(P["w_gate_up"][lsel]); wd = np.ascontiguousarray(P["w_down"][lsel])
    in_maps = []
    for c in range(8):
        b, r = divmod(c, 2)
        cbf, cf = _const_tables(validP=(r == 1), validN=(r == 0))
        in_maps.append({
            "xT": xTs[c],
            "pos": np.ascontiguousarray(positions[b:b + 1, r * T:(r + 1) * T]).astype(np.int32),
            "w_in": wi, "w_out": wo, "w_gu": wg, "w_dn": wd,
            "gt": gtab, "cbf": cbf, "cf": cf,
        })
    res = run_bass_kernel_spmd(nc, in_maps, core_ids=list(range(8)))
    return [np.asarray(res.results[c]["yT"]) for c in range(8)]


def kernel(x, positions, pre_mix_norm, w_in, conv_w, attn_out_norm, conv_out_norm, w_out,
           post_mix_norm, pre_ffn_norm, w_gate_up, w_down, post_ffn_norm):
    P = dict(pre_mix_norm=np.asarray(pre_mix_norm, np.float32), w_in=np.asarray(w_in, np.float32),
             conv_w=np.asarray(conv_w, np.float32), attn_out_norm=np.asarray(attn_out_norm, np.float32),
             conv_out_norm=np.asarray(conv_out_norm, np.float32), w_out=np.asarray(w_out, np.float32),
             post_mix_norm=np.asarray(post_mix_norm, np.float32), pre_ffn_norm=np.asarray(pre_ffn_norm, np.float32),
             w_gate_up=np.asarray(w_gate_up, np.float32), w_down=np.asarray(w_down, np.float32),
             post_ffn_norm=np.asarray(post_ffn_norm, np.float32))
    x = np.asarray(x, np.float32)
    positions = np.asarray(positions)
    xTs = []
    for c in range(8):
        b, r = divmod(c, 2)
        xTs.append(np.ascontiguousarray(x[b, r * T:(r + 1) * T, :].T))
    if FUSED:
        xTs = _run(DEPTH, list(range(DEPTH)), xTs, positions, P)
    else:
        for l in range(DEPTH):
            xTs = _run(1, [l], xTs, positions, P)
    out = np.empty((4, 4096, D), np.float32)
    for c in range(8):
        b, r = divmod(c, 2)
        out[b, r * T:(r + 1) * T, :] = xTs[c].T
    return out
```

```python
import collections
import contextlib
import math

import numpy as np
import ml_dtypes

import concourse.bass as bass
import concourse.mybir as mybir
from concourse.bass_utils import run_bass_kernel_spmd

F32 = mybir.dt.float32
BF16 = mybir.dt.bfloat16
I32 = mybir.dt.int32
ALU = mybir.AluOpType
AF = mybir.ActivationFunctionType
NPBF = ml_dtypes.bfloat16

FUSED = True
DEPTH = 4
D = 1024
T = 2048
TB = 512
NTB = 4
FFN = 2816
NEG = -30000.0
EPS = 1e-6
GT_PER_LAYER = 52
CBW = 2560
REPLICA_GROUPS = [[0, 1], [2, 3], [4, 5], [6, 7]]
STOP = None
ENG_NAMES = ("pe", "act", "dve", "pool", "sp")


def sl(start, n, step=1):
    return slice(start, start + (n - 1) * step + 1, step)


class Op:
    __slots__ = ("idx", "eng", "fn", "reads", "writes", "dma", "tag", "inc", "count",
                 "waits", "marked", "known", "kdma")

    def __init__(self, idx, eng, fn, reads, writes, dma, tag, inc):
        self.idx = idx; self.eng = eng; self.fn = fn
        self.reads = tuple(reads); self.writes = tuple(writes)
        self.dma = dma; self.tag = tag; self.inc = inc
        self.count = 0; self.waits = []; self.marked = False
        self.known = None; self.kdma = None


class Sched:
    def __init__(self):
        self.ops = []
        self.last_write = {}
        self.readers = collections.defaultdict(dict)
        self.tag_count = collections.defaultdict(int)
        self.tag_inc = {}
        self.eng_known = {e: {} for e in ENG_NAMES}
        self.eng_kdma = {e: {} for e in ENG_NAMES}
        self.eng_ops = {e: [] for e in ENG_NAMES}

    def op(self, eng, fn, reads=(), writes=(), dma=False, tag=None, inc=16, nophase=False):
        i = len(self.ops)
        reads = list(reads); writes = list(writes)
        if not nophase:
            reads.append("PHASE")
        for r in list(reads):
            if r.startswith("ps"):
                reads.remove(r)
                if r not in writes:
                    writes.append(r)
        o = Op(i, eng, fn, reads, writes, dma, tag, inc)
        self.ops.append(o)
        deps = {}
        for r in o.reads:
            j = self.last_write.get(r)
            if j is not None:
                deps[j] = "RAW"
        for r in o.writes:
            j = self.last_write.get(r)
            if j is not None and j not in deps:
                deps[j] = "WAW"
            for j in self.readers[r].values():
                if j not in deps:
                    deps[j] = "WAR"
        known = self.eng_known[eng]
        kdma = self.eng_kdma[eng]
        waits = {}
        for j, kind in sorted(deps.items()):
            d = self.ops[j]
            if d.dma:
                if kdma.get(d.tag, 0) >= d.count:
                    continue
                c = self.tag_count[d.tag]
                waits[("dma", d.tag)] = c
                kdma[d.tag] = c
            else:
                if d.eng == eng:
                    if eng in ("pe", "sp") or kind != "RAW":
                        continue
                if known.get(d.eng, -1) >= j:
                    continue
                waits[("eng", d.eng)] = max(waits.get(("eng", d.eng), -1), j)
        for key, v in waits.items():
            if key[0] == "eng":
                d = self.ops[v]
                d.marked = True
                known[key[1]] = max(known.get(key[1], -1), v)
                for e2, k2 in d.known.items():
                    if known.get(e2, -1) < k2:
                        known[e2] = k2
                for t2, c2 in d.kdma.items():
                    if kdma.get(t2, 0) < c2:
                        kdma[t2] = c2
        o.waits = list(waits.items())
        if dma:
            assert self.tag_inc.setdefault(tag, inc) == inc
            self.tag_count[tag] += 1
            o.count = self.tag_count[tag]
        o.known = dict(known)
        o.kdma = dict(kdma)
        for r in o.reads:
            self.readers[r][("dma", i) if dma else eng] = i
        for r in o.writes:
            self.last_write[r] = i
            self.readers[r] = {}
        self.eng_ops[eng].append(o)
        return o

    def emit(self, nc, final_wait_tags=()):
        rank = {}
        for e in ENG_NAMES:
            c = 0
            for o in self.eng_ops[e]:
                if o.marked and not o.dma:
                    c += 1
                    rank[o.idx] = c
        with contextlib.ExitStack() as st:
            esem = {e: st.enter_context(nc.semaphore("s_" + e)) for e in ("pe", "act", "dve", "pool")}
            tsem = {t: st.enter_context(nc.semaphore("t_" + str(t))) for t in sorted(self.tag_count)}
            block = st.enter_context(nc.Block())
            handles = {"pe": block.tensor, "act": block.scalar, "dve": block.vector,
                       "pool": block.gpsimd, "sp": block.sync}

            def body(ename):
                def _f(eng):
                    for o in self.eng_ops[ename]:
                        for key, v in o.waits:
                            if key[0] == "eng":
                                eng.wait_ge(esem[key[1]], rank[v])
                            else:
                                eng.wait_ge(tsem[key[1]], v * self.tag_inc[key[1]])
                        ins = o.fn(eng)
                        if o.dma:
                            ins.then_inc(tsem[o.tag], o.inc)
                        elif o.marked:
                            ins.then_inc(esem[ename], 1)
                    if ename == "sp":
                        for t in final_wait_tags:
                            eng.wait_ge(tsem[t], self.tag_count[t] * self.tag_inc[t])
                return _f

            for e in ENG_NAMES:
                handles[e](body(e))


class SbAlloc:
    BASE = 16512
    LIMIT = 229344

    def __init__(self, nc):
        self.nc = nc
        self.off = self.BASE
        self.n = 0

    def at(self, name, shape, dtype, offset):
        nbytes = int(np.prod(shape[1:])) * (2 if dtype == BF16 else 4)
        assert offset % 32 == 0 and offset + nbytes <= self.LIMIT, (name, offset, nbytes)
        self.n += 1
        return self.nc.alloc_sbuf_tensor_at("%s_%d" % (name, self.n), list(shape), dtype, offset=offset), nbytes

    def new(self, name, shape, dtype):
        t, nbytes = self.at(name, shape, dtype, self.off)
        self.off += (nbytes + 31) // 32 * 32
        return t


def build_program(L):
    nc = bass.Bass("TRN2", target_bir_lowering=False)
    S = Sched()

    def dram(name, shape, dt, kind):
        return nc.dram_tensor(name, list(shape), dt, kind=kind)

    xT = dram("xT", [D, T], F32, "ExternalInput").ap()
    pos = dram("pos", [1, T], I32, "ExternalInput").ap()
    w_in = dram("w_in", [L, D, 3072], F32, "ExternalInput").ap()
    w_out = dram("w_out", [L, D, D], F32, "ExternalInput").ap()
    w_gu = dram("w_gu", [L, D, 2 * FFN], F32, "ExternalInput").ap()
    w_dn = dram("w_dn", [L, FFN, D], F32, "ExternalInput").ap()
    gt_d = dram("gt", [128, L * GT_PER_LAYER], F32, "ExternalInput").ap()
    cbf_d = dram("cbf", [128, CBW], BF16, "ExternalInput").ap()
    cf_d = dram("cf", [128, 4], F32, "ExternalInput").ap()
    yT = dram("yT", [D, T], F32, "ExternalOutput").ap()
    q_scr_t = dram("q_scr", [512, T], BF16, "Internal")
    sndk_t = dram("snd_k", [512, T], BF16, "Internal")
    sndv_t = dram("snd_v", [512, T], BF16, "Internal")
    sndz_t = dram("snd_z", [8, T], BF16, "Internal")
    rcvk_t = dram("rcv_k", [1024, T], BF16, "Internal")
    rcvv_t = dram("rcv_v", [1024, T], BF16, "Internal")
    rcvz_t = dram("rcv_z", [16, T], BF16, "Internal")
    q_scr = q_scr_t.ap()
    snd_k, snd_v, snd_z = sndk_t.ap(), sndv_t.ap(), sndz_t.ap()
    rcv_k, rcv_v, rcv_z = rcvk_t.ap(), rcvv_t.ap(), rcvz_t.ap()

    def allgather(src, dst, reads, wres, tag):
        S.op("pool", lambda e: e.collective_compute("AllGather", ALU.bypass, replica_groups=REPLICA_GROUPS, ins=[src], outs=[dst]),
             reads=reads, writes=[wres], dma=True, tag=tag, inc=1)

    A = SbAlloc(nc)
    x_sb = A.new("x", [128, 8, T], F32)
    cs = A.new("cs", [128, 2, T], BF16)
    cbf = A.new("cbf", [128, CBW], BF16)
    cf = A.new("cf", [128, 4], F32)
    gt = A.new("gt", [128, L * GT_PER_LAYER], F32)
    onesf = A.new("onesf", [128, 128], F32)
    selo = A.new("selo", [128, 128], F32)
    bar = A.new("bar", [128, 8], F32)
    NW = 3
    wsl = [A.new("w%d" % i, [128, 4096], BF16) for i in range(NW)]
    sq = A.new("sq", [128, 8, TB], BF16)
    rs = [A.new("rs%d" % i, [128, TB], F32) for i in range(2)]
    rq = [A.new("rq%d" % i, [128, TB], BF16) for i in range(2)]
    stg = [A.new("stg%d" % i, [128, TB], BF16) for i in range(4)]
    zh = A.new("zh", [128, 2, 4, 16], BF16)
    regB = A.off
    tmpf = [A.new("tmpf%d" % i, [128, TB], F32) for i in range(4)]
    oacc_e, _ = A.at("oacc_e", [128, T], F32, regB)
    arena = A.off
    hm, _ = A.at("hm", [128, 8, T], BF16, arena)
    regA = arena + 32768
    z, _ = A.at("z", [128, 4, T + 2], BF16, regA)
    bc, _ = A.at("bc", [128, 4, T], BF16, regA + 16416)
    k_ext, _ = A.at("k_ext", [128, 4096], BF16, regA)
    v_ext, _ = A.at("v_ext", [128, 4096], BF16, regA + 8192)
    vt, _ = A.at("vt", [128, 32, 160], BF16, regA + 16384)
    q_sb, _ = A.at("q_sb", [128, T], BF16, regA + 16384 + 10240)
    o2 = regA + 32800
    q16, _ = A.at("q16", [128, T], BF16, o2)
    pbuf = [A.at("pb%d" % i, [128, 256], BF16, o2 + 4096 + 512 * i)[0] for i in range(4)]
    oacc_o, _ = A.at("oacc_o", [128, T], F32, o2 + 4096 + 2048)
    mixbufs = [A.at("mixbuf%d" % i, [128, 8, TB], BF16, regA + 8192 * i)[0] for i in range(2)]
    h2, _ = A.at("h2", [128, 8, 1024], BF16, arena)
    hid, _ = A.at("hid", [128, 22, 1024], BF16, arena + 16384)
    f_bf, _ = A.at("f_bf", [128, 8, 1024], BF16, arena + 16384 + 45056)
    assert arena + 16384 + 45056 + 16384 <= A.LIMIT
    su_i, _ = A.at("su_i", [128, T], I32, arena)
    su_ang, _ = A.at("su_ang", [128, T], F32, arena + 8192)
    su_a2, _ = A.at("su_a2", [128, T], F32, arena + 16384)
    su_kf, _ = A.at("su_kf", [128, T], F32, arena + 24576)
    su_ki, _ = A.at("su_ki", [128, T], I32, arena + 32768)
    su_r, _ = A.at("su_r", [128, T], F32, arena + 40960)

    ps = [nc.alloc_psum_tensor("ps%d" % i, [128, TB], F32) for i in range(6)]
    psb = [nc.alloc_psum_tensor("ps%d" % i, [128, 1024], BF16) for i in (6, 7)]

    ident = cbf[:, 0:128]
    ones_bf = cbf[:, 128:256]
    perm = cbf[:, 256:384]

    ident2 = cbf[:, 1408:1536]

    def maskv(v):
        return cbf[:, 384 + 256 * v: 384 + 256 * (v + 1)]

    def maskv2(v):
        return cbf[:, 1536 + 256 * v: 1536 + 256 * (v + 1)]

    def barrier():
        S.op("pool", lambda e: e.memset(bar[:], 0.0), writes=["PHASE"], nophase=True)

    def finish():
        for c in range(8):
            S.op("sp", (lambda c_: lambda e: e.dma_start(out=yT[c_ * 128:(c_ + 1) * 128, :], in_=x_sb[:, c_, :]))(c),
                 reads=["x%d" % tb for tb in range(NTB)], dma=True, tag="yout")
        S.emit(nc, final_wait_tags=["yout"])
        return nc

    S.op("sp", lambda e: e.dma_start(out=cbf[:], in_=cbf_d), writes=["cbf"], dma=True, tag="c0")
    S.op("sp", lambda e: e.dma_start(out=cf[:], in_=cf_d), writes=["cf"], dma=True, tag="c0")
    S.op("sp", lambda e: e.dma_start(out=gt[:], in_=gt_d), writes=["gt"], dma=True, tag="c0")
    S.op("sp", lambda e: e.dma_start(out=su_i[:], in_=pos.partition_broadcast(128)), writes=["su_i"], dma=True, tag="c1")
    for c in range(8):
        S.op("sp", (lambda c: lambda e: e.dma_start(out=x_sb[:, c, :], in_=xT[c * 128:(c + 1) * 128, :]))(c),
             writes=["x%d" % tb for tb in range(NTB)], dma=True, tag="xin")
    S.op("pool", lambda e: e.memset(onesf[:], 1.0), writes=["onesf"])
    S.op("pool", lambda e: e.memset(selo[:, 0:64], 0.0), writes=["selo"])
    S.op("pool", lambda e: e.memset(selo[:, 64:128], 1.0), writes=["selo"])
    TWO_PI = 2.0 * math.pi
    S.op("dve", lambda e: e.tensor_copy(su_ang[:], su_i[:]), reads=["su_i"], writes=["su_ang"])
    S.op("dve", lambda e: e.tensor_scalar(su_ang[:], su_ang[:], cf[:, 0:1], None, ALU.mult), reads=["su_ang", "cf"], writes=["su_ang"])
    for which, shift in ((1, 0.0), (0, 0.5 * math.pi)):
        S.op("dve", (lambda sh: lambda e: e.tensor_scalar(su_a2[:], su_ang[:], sh, None, ALU.add))(shift), reads=["su_ang"], writes=["su_a2"])
        S.op("dve", lambda e: e.tensor_scalar(su_kf[:], su_a2[:], 1.0 / TWO_PI, None, ALU.mult), reads=["su_a2"], writes=["su_kf"])
        S.op("dve", lambda e: e.tensor_copy(su_ki[:], su_kf[:]), reads=["su_kf"], writes=["su_ki"])
        S.op("dve", lambda e: e.tensor_copy(su_kf[:], su_ki[:]), reads=["su_ki"], writes=["su_kf"])
        S.op("dve", lambda e: e.scalar_tensor_tensor(su_r[:], su_kf[:], -TWO_PI, su_a2[:], ALU.mult, ALU.add), reads=["su_kf", "su_a2"], writes=["su_r"])
        S.op("dve", lambda e: e.tensor_scalar(su_kf[:], su_r[:], math.pi, TWO_PI, ALU.is_gt, ALU.mult), reads=["su_r"], writes=["su_kf"])
        S.op("dve", lambda e: e.tensor_tensor(su_r[:], su_r[:], su_kf[:], ALU.subtract), reads=["su_r", "su_kf"], writes=["su_r"])
        S.op("dve", lambda e: e.tensor_scalar(su_kf[:], su_r[:], -math.pi, TWO_PI, ALU.is_lt, ALU.mult), reads=["su_r"], writes=["su_kf"])
        S.op("dve", lambda e: e.tensor_tensor(su_r[:], su_r[:], su_kf[:], ALU.add), reads=["su_r", "su_kf"], writes=["su_r"])
        S.op("act", (lambda w: lambda e: e.activation(cs[:, w, :], su_r[:], AF.Sin))(which), reads=["su_r"], writes=["cs"])
    barrier()
    if STOP == "setup":
        return finish()

    wtiles = []
    for l in range(L):
        for c0 in (1536, 2560, 512, 1024, 0, 2048):
            wtiles.append(("in", l, c0))
        for c0 in (0, 512):
            wtiles.append(("out", l, c0))
        for hf in range(2):
            for f in range(11):
                wtiles.append(("gu", l, f))
            for mp in range(4):
                for kh in range(2):
                    wtiles.append(("dn", l, (mp, kh)))
    wstate = {"issued": 0}

    def w_issue(upto):
        while wstate["issued"] <= min(upto, len(wtiles) - 1):
            i = wstate["issued"]
            kind, l, a = wtiles[i]
            s = i % NW
            slot = wsl[s]
            if kind in ("in", "out"):
                src = (w_in if kind == "in" else w_out)[l].rearrange("(kc p) n -> p kc n", p=128)[:, :, a:a + 512]
                dst = slot[:].rearrange("p (kc n) -> p kc n", kc=8)
                S.op("pool", (lambda d_, s_: lambda e: e.dma_start(out=d_, in_=s_))(dst, src),
                     writes=["w%d" % s], dma=True, tag="w%d" % s, nophase=True)
            elif kind == "gu":
                v = w_gu[l].rearrange("(kc p) n -> p kc n", p=128)
                dst = slot[:].rearrange("p (kc n) -> p kc n", kc=8)
                S.op("pool", (lambda d_, s_: lambda e: e.dma_start(out=d_, in_=s_))(dst[:, :, 0:256], v[:, :, a * 256:(a + 1) * 256]),
                     writes=["w%d" % s], dma=True, tag="w%d" % s, nophase=True)
                S.op("pool", (lambda d_, s_: lambda e: e.dma_start(out=d_, in_=s_))(dst[:, :, 256:512], v[:, :, FFN + a * 256:FFN + (a + 1) * 256]),
                     writes=["w%d" % s], dma=True, tag="w%d" % s, nophase=True)
            else:
                mp, kh = a
                v = w_dn[l].rearrange("(kc p) n -> p kc n", p=128)
                dst = slot[:, 0:2816].rearrange("p (kc n) -> p kc n", kc=11)
                S.op("pool", (lambda d_, s_: lambda e: e.dma_start(out=d_, in_=s_))(dst, v[:, kh * 11:(kh + 1) * 11, mp * 256:(mp + 1) * 256]),
                     writes=["w%d" % s], dma=True, tag="w%d" % s, nophase=True)
            wstate["issued"] += 1

    wcur = {"i": 0}

    def w_get():
        i = wcur["i"]
        w_issue(i)
        return i % NW, wsl[i % NW]

    def w_done():
        wcur["i"] += 1
        w_issue(wcur["i"] - 1 + NW)

    w_issue(NW - 1)

    cnt = {"mm": 0, "ss": 0, "rs": 0, "sq": 0, "tmpf": 0, "stg": 0, "rq": 0, "trb": 0, "mixb": 0}

    def rr(key, n):
        v = cnt[key] % n
        cnt[key] += 1
        return v

    def mm_group(bank, pairs, reads):
        def fn(e):
            ins = None
            n = len(pairs)
            for i, (a_, b_) in enumerate(pairs):
                ins = e.matmul(ps[bank][:, :], a_, b_, start=(i == 0), stop=(i == n - 1))
            return ins
        S.op("pe", fn, reads=reads, writes=["ps%d" % bank])

    def rstd_from(ssbank, dim):
        r = rr("rs", 2)
        S.op("dve", lambda e: e.tensor_scalar(rs[r][:], ps[ssbank][:, :], 1.0 / dim, EPS, ALU.mult, ALU.add),
             reads=["ps%d" % ssbank], writes=["rs%d" % r])
        S.op("act", lambda e: e.activation(rs[r][:], rs[r][:], AF.Sqrt), reads=["rs%d" % r], writes=["rs%d" % r])
        S.op("dve", lambda e: e.reciprocal(rs[r][:], rs[r][:]), reads=["rs%d" % r], writes=["rs%d" % r])
        return r

    def sq_ones(src_ap_fn, nch, src_res, ssbank, first, last):
        base = 0 if (cnt["sq"] % 2 == 0) else 4
        cnt["sq"] += 1
        assert nch <= 4
        S.op("act", lambda e: e.activation(sq[:, base:base + nch, :], src_ap_fn(), AF.Square),
             reads=src_res, writes=["sq%d" % base])

        def fn(e):
            ins = None
            for c in range(nch):
                ins = e.matmul(ps[ssbank][:, :], ones_bf, sq[:, base + c, :], start=(first and c == 0), stop=(last and c == nch - 1))
            return ins
        S.op("pe", fn, reads=["sq%d" % base, "cbf"], writes=["ps%d" % ssbank])

    def norm_x_to(dst_fn, dst_res, tbg, gcol0):
        ssb = 4 + rr("ss", 2)
        for half in range(2):
            sq_ones((lambda h_: lambda: x_sb[:, 4 * h_:4 * h_ + 4, tbg * TB:(tbg + 1) * TB])(half), 4, ["x%d" % tbg], ssb, half == 0, half == 1)
        r = rstd_from(ssb, D)
        for c in range(8):
            S.op("dve", (lambda c_: lambda e: e.scalar_tensor_tensor(dst_fn(c_), x_sb[:, c_, tbg * TB:(tbg + 1) * TB], gt[:, gcol0 + c_:gcol0 + c_ + 1], rs[r][:], ALU.mult, ALU.mult))(c),
                 reads=["x%d" % tbg, "gt", "rs%d" % r], writes=[dst_res])

    for l in range(L):
        g0 = l * GT_PER_LAYER
        for tb in range(NTB):
            norm_x_to((lambda tb_: lambda c_: hm[:, c_, tb_ * TB:(tb_ + 1) * TB])(tb), "h%d" % tb, tb, g0 + 0)

        pending = []

        def rotary_store(bank, c, tb, dst_dram, dst_res):
            r = rr("rq", 2)
            S.op("act", lambda e: e.activation(rq[r][:], ps[bank][:, :], AF.Copy), reads=["ps%d" % bank], writes=["rq%d" % r])

            def rest():
                S.op("pe", lambda e: e.matmul(ps[5][:, :], perm, rq[r][:], start=True, stop=True), reads=["rq%d" % r, "cbf"], writes=["ps5"])
                t1 = rr("tmpf", 4); t2 = rr("tmpf", 4)
                S.op("dve", lambda e: e.tensor_tensor(tmpf[t1][:], ps[bank][:, :], cs[:, 0, tb * TB:(tb + 1) * TB], ALU.mult),
                     reads=["ps%d" % bank, "cs"], writes=["tmpf%d" % t1])
                S.op("dve", lambda e: e.tensor_tensor(tmpf[t2][:], ps[5][:, :], cs[:, 1, tb * TB:(tb + 1) * TB], ALU.mult),
                     reads=["ps5", "cs"], writes=["tmpf%d" % t2])
                s_ = rr("stg", 4)
                S.op("pool", lambda e: e.tensor_tensor(stg[s_][:], tmpf[t1][:], tmpf[t2][:], ALU.add),
                     reads=["tmpf%d" % t1, "tmpf%d" % t2], writes=["stg%d" % s_])
                S.op("sp", lambda e: e.dma_start(out=dst_dram[c * 128:(c + 1) * 128, tb * TB:(tb + 1) * TB], in_=stg[s_][:]),
                     reads=["stg%d" % s_], writes=[dst_res], dma=True, tag="stg%d" % s_)
            pending.append(rest)

        def flush_pending(keep=0):
            while len(pending) > keep:
                pending.pop(0)()

        if STOP == "P1a":
            return finish()
        for kind in ("u", "C", "k", "v", "q", "B"):
            if STOP == "P1u" and kind == "C":
                return finish()
            if STOP == "P1C" and kind == "k":
                return finish()
            if STOP == "P1k" and kind == "v":
                return finish()
            ws, slot = w_get()
            wv = slot[:].rearrange("p (kc n) -> p kc n", kc=8)
            for mi in range(4):
                for tb in range(NTB):
                    bank = rr("mm", 4)
                    mm_group(bank, [(wv[:, kc, mi * 128:(mi + 1) * 128], hm[:, kc, tb * TB:(tb + 1) * TB]) for kc in range(8)],
                             ["w%d" % ws, "h%d" % tb])
                    flush_pending()
                    zs = z[:, mi, 1 + tb * TB:1 + (tb + 1) * TB]
                    if kind == "u":
                        S.op("act", (lambda b_, o_: lambda e: e.activation(o_, ps[b_][:, :], AF.Copy))(bank, zs),
                             reads=["ps%d" % bank], writes=["z%d" % mi])
                    elif kind == "C":
                        S.op("dve", (lambda b_, o_: lambda e: e.tensor_tensor(o_, o_, ps[b_][:, :], ALU.mult))(bank, zs),
                             reads=["ps%d" % bank, "z%d" % mi], writes=["z%d" % mi])
                    elif kind == "B":
                        S.op("act", (lambda b_, o_: lambda e: e.activation(o_, ps[b_][:, :], AF.Copy))(bank, bc[:, mi, tb * TB:(tb + 1) * TB]),
                             reads=["ps%d" % bank], writes=["bc%d" % mi])
                    elif kind == "k":
                        rotary_store(bank, mi, tb, snd_k, "snd%d" % mi)
                    elif kind == "q":
                        rotary_store(bank, mi, tb, q_scr, "qs%d" % mi)
                    else:
                        s = rr("stg", 4)
                        S.op("act", (lambda b_, s_: lambda e: e.activation(stg[s_][:], ps[b_][:, :], AF.Copy))(bank, s),
                             reads=["ps%d" % bank], writes=["stg%d" % s])
                        S.op("sp", (lambda s_, mi_, tb_: lambda e: e.dma_start(out=snd_v[mi_ * 128:(mi_ + 1) * 128, tb_ * TB:(tb_ + 1) * TB], in_=stg[s_][:]))(s, mi, tb),
                             reads=["stg%d" % s], writes=["snd%d" % (4 + mi)], dma=True, tag="stg%d" % s)
            flush_pending()
            w_done()
            if kind == "C":
                for f, c0 in ((0, 1), (1, T - 15)):
                    dst = bass.AP(sndz_t, f * 8192, [[16, 128], [2048, 4], [1, 16]])
                    S.op("sp", (lambda d_, c0_: lambda e: e.dma_start(out=d_, in_=z[:, :, c0_:c0_ + 16]))(dst, c0),
                         reads=["z%d" % i for i in range(4)], writes=["sndz"], dma=True, tag="sndz")
                allgather(snd_z, rcv_z, ["sndz"], "rcvz", "ccz")
            if kind == "k":
                allgather(snd_k, rcv_k, ["snd%d" % i for i in range(4)], "rcvk", "cck")
            if kind == "v":
                allgather(snd_v, rcv_v, ["snd%d" % i for i in range(4, 8)], "rcvv", "ccv")
        if STOP == "P1g":
            return finish()
        for f, (rank, slab) in enumerate(((0, 1), (1, 0))):
            src = bass.AP(rcvz_t, rank * 16384 + slab * 8192, [[16, 128], [2048, 4], [1, 16]])
            S.op("sp", (lambda f_, s_: lambda e: e.dma_start(out=zh[:, f_, :, :], in_=s_))(f, src),
                 reads=["rcvz"], writes=["zh"], dma=True, tag="ldzh")
        S.op("dve", lambda e: e.tensor_scalar(z[:, :, 0:1], zh[:, 0, :, 15:16], cf[:, 1:2], None, ALU.mult),
             reads=["zh", "cf"], writes=["z%d" % i for i in range(4)])
        S.op("dve", lambda e: e.tensor_scalar(z[:, :, T + 1:T + 2], zh[:, 1, :, 0:1], cf[:, 2:3], None, ALU.mult),
             reads=["zh", "cf"], writes=["z%d" % i for i in range(4)])
        for c in range(4):
            for tb in range(NTB):
                t1 = rr("tmpf", 4)
                wc = lambda k, c_=c, g0_=g0: gt[:, g0_ + 40 + 4 * k + c_:g0_ + 40 + 4 * k + c_ + 1]
                zz = lambda k, c_=c, tb_=tb: z[:, c_, tb_ * TB + k:tb_ * TB + k + TB]
                S.op("dve", (lambda t_, zz_, wc_: lambda e: e.tensor_scalar(tmpf[t_][:], zz_(0), wc_(0), None, ALU.mult))(t1, zz, wc),
                     reads=["z%d" % c, "gt"], writes=["tmpf%d" % t1])
                S.op("dve", (lambda t_, zz_, wc_: lambda e: e.scalar_tensor_tensor(tmpf[t_][:], zz_(1), wc_(1), tmpf[t_][:], ALU.mult, ALU.add))(t1, zz, wc),
                     reads=["z%d" % c, "gt", "tmpf%d" % t1], writes=["tmpf%d" % t1])
                S.op("dve", (lambda t_, zz_, wc_: lambda e: e.scalar_tensor_tensor(tmpf[t_][:], zz_(2), wc_(2), tmpf[t_][:], ALU.mult, ALU.add))(t1, zz, wc),
                     reads=["z%d" % c, "gt", "tmpf%d" % t1], writes=["tmpf%d" % t1])
                S.op("pool", (lambda t_, c_, tb_: lambda e: e.tensor_tensor(hm[:, 4 + c_, tb_ * TB:(tb_ + 1) * TB], tmpf[t_][:], bc[:, c_, tb_ * TB:(tb_ + 1) * TB], ALU.mult))(t1, c, tb),
                     reads=["tmpf%d" % t1, "bc%d" % c], writes=["hmB%d" % tb])
        barrier()
        if STOP == "P1":
            return finish()

        S.op("pool", lambda e: e.memset(vt[:], 1.0), writes=["vt"])
        blk = 0
        for hp in range(4):
            S.op("sp", (lambda hp_: lambda e: e.dma_start(out=q_sb[:], in_=q_scr[hp_ * 128:(hp_ + 1) * 128, :]))(hp),
                 reads=["qs%d" % hp], writes=["q_sb"], dma=True, tag="ldq")
            for (buf, sn, rc, rres, soff, res, tg) in ((k_ext, snd_k, rcv_k, "rcvk", 0, "k_ext", "ldk"), (v_ext, snd_v, rcv_v, "rcvv", 4, "v_ext", "ldv")):
                r0 = hp * 128
                S.op("sp", (lambda b_, rc_, r_: lambda e: e.dma_start(out=b_[:, 0:1024], in_=rc_[r_:r_ + 128, 1024:2048]))(buf, rc, r0),
                     reads=[rres], writes=[res], dma=True, tag=tg)
                S.op("sp", (lambda b_, sn_, r_: lambda e: e.dma_start(out=b_[:, 1024:3072], in_=sn_[r_:r_ + 128, :]))(buf, sn, r0),
                     reads=["snd%d" % (soff + hp)], writes=[res], dma=True, tag=tg)
                S.op("sp", (lambda b_, rc_, r_: lambda e: e.dma_start(out=b_[:, 3072:4096], in_=rc_[512 + r_:512 + r_ + 128, 0:1024]))(buf, rc, r0),
                     reads=[rres], writes=[res], dma=True, tag=tg)
            S.op("pool", lambda e: e.tensor_copy(q16[:].rearrange("p (r m) -> p r m", r=16), q_sb[:].rearrange("p (m r) -> p r m", r=16)),
                 reads=["q_sb"], writes=["q16"])
            for d in (1, 4, 16):
                nblk = 16 // d
                ntile = d * (nblk + 1)
                for t0 in range(0, ntile, 8):
                    nt = min(8, ntile - t0)
                    tbk = rr("trb", 2)

                    def trfn(e, t0=t0, nt=nt, d=d, nblk=nblk, tbk=tbk):
                        ins = None
                        for i in range(nt):
                            rho, j = divmod(t0 + i, nblk + 1)
                            e0 = 1024 + rho + d * (128 * j - 64)
                            ins = e.transpose(psb[tbk][:, i * 128:(i + 1) * 128], v_ext[:, sl(e0, 128, d)], ident)
                        return ins
                    S.op("pe", trfn, reads=["v_ext", "cbf"], writes=["ps%d" % (6 + tbk)])
                    pv = lambda nt=nt, tbk=tbk: psb[tbk][:, 0:nt * 128].rearrange("p (t c) -> p t c", c=128)
                    S.op("act", (lambda t0_, nt_, pv_: lambda e: e.activation(vt[:, t0_:t0_ + nt_, 0:64], pv_()[:, :, 0:64], AF.Copy))(t0, nt, pv),
                         reads=["ps%d" % (6 + tbk)], writes=["vt"])
                    S.op("dve", (lambda t0_, nt_, pv_: lambda e: e.tensor_copy(vt[:, t0_:t0_ + nt_, 96:160], pv_()[:, :, 64:128]))(t0, nt, pv),
                         reads=["ps%d" % (6 + tbk)], writes=["vt"])
                blocks = [(rho, j) for rho in range(d) for j in range(nblk)]
                pars = []
                for bi in range(len(blocks)):
                    pars.append(blk % 2)
                    blk += 1

                def emit_scores(bi, d=d, nblk=nblk, blocks=blocks, pars=pars):
                    rho, j = blocks[bi]
                    par = pars[bi]
                    variant = (1 if j == 0 else 0) + (2 if j == nblk - 1 else 0)
                    if d == 16:
                        qsrc = lambda p0, rho=rho: q16[p0:p0 + 64, rho * 128:(rho + 1) * 128]
                    else:
                        qsrc = lambda p0, rho=rho, j=j, d=d: q_sb[p0:p0 + 64, sl(rho + d * 128 * j, 128, d)]
                    eA = 1024 + rho + d * (128 * j - 64)
                    eB = eA + d * 128
                    for hpar, p0 in ((0, 0), (1, 64)):
                        sbank = 2 * hpar + par

                        def sfn(e, sbank=sbank, p0=p0, eA=eA, eB=eB, d=d, qsrc=qsrc, variant=variant):
                            e.matmul(ps[sbank][:, 0:256], ident[p0:p0 + 64, :], maskv(variant)[p0:p0 + 64, :], start=True, stop=False)
                            e.matmul(ps[sbank][:, 0:256], ident2[p0:p0 + 64, :], maskv2(variant)[p0:p0 + 64, :], start=False, stop=False)
                            e.matmul(ps[sbank][:, 0:128], k_ext[p0:p0 + 64, sl(eA, 128, d)], qsrc(p0), start=False, stop=False)
                            return e.matmul(ps[sbank][:, 128:256], k_ext[p0:p0 + 64, sl(eB, 128, d)], qsrc(p0), start=False, stop=True)
                        S.op("pe", sfn, reads=["k_ext", "q16" if d == 16 else "q_sb", "cbf"], writes=["ps%d" % sbank])
                        pb = 2 * hpar + par
                        S.op("act", (lambda pb_, sb_: lambda e: e.activation(pbuf[pb_][:], ps[sb_][:, 0:256], AF.Exp, scale=0.125))(pb, sbank),
                             reads=["ps%d" % sbank], writes=["pb%d" % pb])

                def emit_pv(bi, d=d, nblk=nblk, blocks=blocks, pars=pars):
                    rho, j = blocks[bi]
                    par = pars[bi]
                    tA = rho * (nblk + 1) + j
                    obank = 4 + (bi // 2) % 2
                    osl = bi % 2
                    for hpar in (0, 1):
                        pb = 2 * hpar + par

                        def ofn(e, hpar=hpar, pb=pb, obank=obank, osl=osl, tA=tA):
                            if hpar == 0:
                                o_ap = ps[obank][0:65, osl * 128:(osl + 1) * 128]
                                la, lb = vt[:, tA, 0:65], vt[:, tA + 1, 0:65]
                            else:
                                o_ap = ps[obank][:, 256 + osl * 128:256 + (osl + 1) * 128]
                                la, lb = vt[:, tA, 32:160], vt[:, tA + 1, 32:160]
                            e.matmul(o_ap, la, pbuf[pb][:, 0:128], start=True, stop=False)
                            return e.matmul(o_ap, lb, pbuf[pb][:, 128:256], start=False, stop=True)
                        S.op("pe", ofn, reads=["vt", "pb%d" % pb], writes=["ps%d" % obank])
                    if bi % 2 == 1 or bi == len(blocks) - 1:
                        nb = 2 if bi % 2 == 1 else 1
                        b0 = bi - (nb - 1)
                        rho0, j0 = blocks[b0]
                        if d == 16:
                            dst = lambda t_, p_, rho0=rho0, nb=nb: t_[p_, :].rearrange("p (m r) -> p r m", r=16)[:, rho0:rho0 + nb, :]
                            src = lambda ap_: ap_.rearrange("p (r m) -> p r m", m=128)
                        else:
                            dst = lambda t_, p_, rho0=rho0, j0=j0, d=d, nb=nb: t_[p_, sl(rho0 + d * 128 * j0, 128 * nb, d)]
                            src = lambda ap_: ap_
                        for hpar in (0, 1):
                            if hpar == 0:
                                pp = slice(0, 65); o_t = oacc_e; c0 = 0; res = "oacc_e"
                            else:
                                pp = slice(0, 128); o_t = oacc_o; c0 = 256; res = "oacc_o"
                            if d == 1:
                                S.op("dve", (lambda o_t_, pp_, c0_, nb_, ob_, dst_, src_: lambda e: e.tensor_copy(dst_(o_t_, pp_), src_(ps[ob_][pp_, c0_:c0_ + 128 * nb_])))(o_t, pp, c0, nb, obank, dst, src),
                                     reads=["ps%d" % obank], writes=[res])
                            else:
                                S.op("dve", (lambda o_t_, pp_, c0_, nb_, ob_, dst_, src_: lambda e: e.tensor_tensor(dst_(o_t_, pp_), dst_(o_t_, pp_), src_(ps[ob_][pp_, c0_:c0_ + 128 * nb_]), ALU.add))(o_t, pp, c0, nb, obank, dst, src),
                                     reads=["ps%d" % obank, res], writes=[res])

                emit_scores(0)
                for bi in range(len(blocks)):
                    if bi + 1 < len(blocks):
                        emit_scores(bi + 1)
                    emit_pv(bi)
            S.op("dve", lambda e: e.reciprocal(oacc_e[64:65, :], oacc_e[64:65, :]), reads=["oacc_e"], writes=["oacc_e"])
            S.op("dve", lambda e: e.reciprocal(oacc_o[32:33, :], oacc_o[32:33, :]), reads=["oacc_o"], writes=["oacc_o"])
            for tb in range(NTB):
                cs_ = slice(tb * TB, (tb + 1) * TB)
                S.op("pe", (lambda c_: lambda e: e.matmul(ps[1][0:64, :], onesf[64:65, 0:64], oacc_e[64:65, c_], start=True, stop=True))(cs_),
                     reads=["oacc_e", "onesf"], writes=["ps1"])
                S.op("pe", (lambda c_: lambda e: e.matmul(ps[0][:, :], selo[32:33, :], oacc_o[32:33, c_], start=True, stop=True))(cs_),
                     reads=["oacc_o", "selo"], writes=["ps0"])
                S.op("dve", (lambda c_, hp_: lambda e: e.tensor_tensor(hm[0:64, hp_, c_], oacc_e[0:64, c_], ps[1][0:64, :], ALU.mult))(cs_, hp),
                     reads=["oacc_e", "ps1"], writes=["hmA%d" % tb])
                S.op("dve", (lambda c_, hp_: lambda e: e.tensor_tensor(hm[64:128, hp_, c_], oacc_o[64:128, c_], ps[0][64:128, :], ALU.mult))(cs_, hp),
                     reads=["oacc_o", "ps0"], writes=["hmA%d" % tb])
        barrier()
        if STOP == "P2":
            return finish()

        i0 = wcur["i"]
        w_issue(i0 + 1)
        ws0, ws1 = i0 % NW, (i0 + 1) % NW
        slot0, slot1 = wsl[ws0], wsl[ws1]
        wo = [slot0[:].rearrange("p (kc n) -> p kc n", kc=8), slot1[:].rearrange("p (kc n) -> p kc n", kc=8)]
        wres = ["w%d" % ws0, "w%d" % ws1]
        for tb in range(NTB):
            cs_ = slice(tb * TB, (tb + 1) * TB)
            for grp, gcol in ((0, g0 + 8), (1, g0 + 12)):
                ssb = 4 + rr("ss", 2)
                sq_ones((lambda g_, c_: lambda: hm[:, 4 * g_:4 * g_ + 4, c_])(grp, cs_), 4, ["hmA%d" % tb, "hmB%d" % tb], ssb, True, True)
                r = rstd_from(ssb, 512)
                for c in range(4):
                    cc = 4 * grp + c
                    S.op("dve", (lambda cc_, c_, r_, gc_: lambda e: e.scalar_tensor_tensor(hm[:, cc_, c_], hm[:, cc_, c_], gt[:, gc_:gc_ + 1], rs[r_][:], ALU.mult, ALU.mult))(cc, cs_, r, gcol + c),
                         reads=["hmA%d" % tb, "hmB%d" % tb, "gt", "rs%d" % r], writes=["hmA%d" % tb if grp == 0 else "hmB%d" % tb])
        for tb in range(NTB):
            cs_ = slice(tb * TB, (tb + 1) * TB)
            mbi = rr("mixb", 2)
            mb = mixbufs[mbi]
            ssb = 4 + rr("ss", 2)
            for m in range(8):
                bank = rr("mm", 4)
                mm_group(bank, [(wo[m // 4][:, kc, (m % 4) * 128:(m % 4 + 1) * 128], hm[:, kc, cs_]) for kc in range(8)],
                         [wres[m // 4], "hmA%d" % tb, "hmB%d" % tb])
                S.op("act", (lambda b_, m_, mb_: lambda e: e.activation(mb_[:, m_, :], ps[b_][:, :], AF.Copy))(bank, m, mb),
                     reads=["ps%d" % bank], writes=["mixbuf%d" % mbi])
                sqs = 0 if (cnt["sq"] % 2 == 0) else 4
                cnt["sq"] += 1
                S.op("act", (lambda b_, s_: lambda e: e.activation(sq[:, s_, :], ps[b_][:, :], AF.Square))(bank, sqs),
                     reads=["ps%d" % bank], writes=["sq%d" % sqs])
                S.op("pe", (lambda s_, m_, sb_: lambda e: e.matmul(ps[sb_][:, :], ones_bf, sq[:, s_, :], start=(m_ == 0), stop=(m_ == 7)))(sqs, m, ssb),
                     reads=["sq%d" % sqs, "cbf"], writes=["ps%d" % ssb])
            r = rstd_from(ssb, D)
            for m in range(8):
                t1 = rr("tmpf", 4)
                S.op("dve", (lambda t_, m_, r_, gc_, mb_: lambda e: e.scalar_tensor_tensor(tmpf[t_][:], mb_[:, m_, :], gt[:, gc_:gc_ + 1], rs[r_][:], ALU.mult, ALU.mult))(t1, m, r, g0 + 16 + m, mb),
                     reads=["mixbuf%d" % mbi, "gt", "rs%d" % r], writes=["tmpf%d" % t1])
                S.op("pool", (lambda t_, m_, c_: lambda e: e.tensor_tensor(x_sb[:, m_, c_], x_sb[:, m_, c_], tmpf[t_][:], ALU.add))(t1, m, cs_),
                     reads=["tmpf%d" % t1, "x%d" % tb], writes=["x%d" % tb])
        w_done(); w_done()
        barrier()
        if STOP == "P3":
            return finish()

        for hf in range(2):
            for tb2 in range(2):
                tbg = hf * 2 + tb2
                norm_x_to((lambda tb2_: lambda c_: h2[:, c_, tb2_ * TB:(tb2_ + 1) * TB])(tb2), "h2_%d" % tb2, tbg, g0 + 24)
            for f in range(11):
                ws, slot = w_get()
                wv = slot[:].rearrange("p (kc n) -> p kc n", kc=8)
                for tb2 in range(2):
                    for mi in range(2):
                        bg = rr("mm", 4)
                        mm_group(bg, [(wv[:, kc, mi * 128:(mi + 1) * 128], h2[:, kc, tb2 * TB:(tb2 + 1) * TB]) for kc in range(8)],
                                 ["w%d" % ws, "h2_%d" % tb2])
                        bu = rr("mm", 4)
                        mm_group(bu, [(wv[:, kc, 256 + mi * 128:256 + (mi + 1) * 128], h2[:, kc, tb2 * TB:(tb2 + 1) * TB]) for kc in range(8)],
                                 ["w%d" % ws, "h2_%d" % tb2])
                        t1 = rr("tmpf", 4)
                        S.op("act", (lambda t_, b_: lambda e: e.activation(tmpf[t_][:], ps[b_][:, :], AF.Silu))(t1, bg),
                             reads=["ps%d" % bg], writes=["tmpf%d" % t1])
                        S.op("dve", (lambda t_, b_, fc_, tb2_: lambda e: e.tensor_tensor(hid[:, fc_, tb2_ * TB:(tb2_ + 1) * TB], tmpf[t_][:], ps[b_][:, :], ALU.mult))(t1, bu, 2 * f + mi, tb2),
                             reads=["tmpf%d" % t1, "ps%d" % bu], writes=["hid%d" % tb2])
                w_done()
            ssbs = [4, 5]
            for mp in range(4):
                slots = []
                for kh in range(2):
                    ws, slot = w_get()
                    wv = slot[:, 0:2816].rearrange("p (kc n) -> p kc n", kc=11)
                    for mi in range(2):
                        for tb2 in range(2):
                            bank = mi * 2 + tb2

                            def dfn(e, wv=wv, mi=mi, tb2=tb2, bank=bank, kh=kh):
                                ins = None
                                for kk in range(11):
                                    ins = e.matmul(ps[bank][:, :], wv[:, kk, mi * 128:(mi + 1) * 128], hid[:, kh * 11 + kk, tb2 * TB:(tb2 + 1) * TB],
                                                   start=(kh == 0 and kk == 0), stop=(kh == 1 and kk == 10))
                                return ins
                            S.op("pe", dfn, reads=["w%d" % ws, "hid%d" % tb2], writes=["ps%d" % bank])
                    w_done()
                for mi in range(2):
                    for tb2 in range(2):
                        bank = mi * 2 + tb2
                        m = mp * 2 + mi
                        S.op("dve", (lambda b_, m_, tb2_: lambda e: e.tensor_copy(f_bf[:, m_, tb2_ * TB:(tb2_ + 1) * TB], ps[b_][:, :]))(bank, m, tb2),
                             reads=["ps%d" % bank], writes=["f_bf%d" % tb2])
                        sqs = 0 if (cnt["sq"] % 2 == 0) else 4
                        cnt["sq"] += 1
                        S.op("act", (lambda b_, s_: lambda e: e.activation(sq[:, s_, :], ps[b_][:, :], AF.Square))(bank, sqs),
                             reads=["ps%d" % bank], writes=["sq%d" % sqs])
                        S.op("pe", (lambda s_, m_, sb_: lambda e: e.matmul(ps[sb_][:, :], ones_bf, sq[:, s_, :], start=(m_ == 0), stop=(m_ == 7)))(sqs, m, ssbs[tb2]),
                             reads=["sq%d" % sqs, "cbf"], writes=["ps%d" % ssbs[tb2]])
            for tb2 in range(2):
                tbg = hf * 2 + tb2
                r = rstd_from(ssbs[tb2], D)
                for m in range(8):
                    t1 = rr("tmpf", 4)
                    S.op("dve", (lambda t_, m_, r_, tb2_, gc_: lambda e: e.scalar_tensor_tensor(tmpf[t_][:], f_bf[:, m_, tb2_ * TB:(tb2_ + 1) * TB], gt[:, gc_:gc_ + 1], rs[r_][:], ALU.mult, ALU.mult))(t1, m, r, tb2, g0 + 32 + m),
                         reads=["f_bf%d" % tb2, "gt", "rs%d" % r], writes=["tmpf%d" % t1])
                    S.op("pool", (lambda t_, m_, tbg_: lambda e: e.tensor_tensor(x_sb[:, m_, tbg_ * TB:(tbg_ + 1) * TB], x_sb[:, m_, tbg_ * TB:(tbg_ + 1) * TB], tmpf[t_][:], ALU.add))(t1, m, tbg),
                         reads=["tmpf%d" % t1, "x%d" % tbg], writes=["x%d" % tbg])
        barrier()

    return finish()


def _const_tables(validP, validN):
    cbf = np.zeros((128, CBW), np.float32)
    cbf[:, 0:128] = np.eye(128)
    cbf[:, 128:256] = 1.0
    perm = np.zeros((128, 128), np.float32)
    for par in range(2):
        for dd in range(8):
            p = par * 64 + dd
            perm[p + 8, p] = -1.0
            perm[p, p + 8] = 1.0
    cbf[:, 256:384] = perm
    a = np.arange(128)[:, None]
    b = np.arange(128)[None, :]
    mA = np.where(b <= a, 0.0, NEG)
    mB = np.where(b >= a, 0.0, NEG)
    for v in range(4):
        ma = mA.copy(); mb = mB.copy()
        if (v & 1) and not validP:
            ma[0:64, :] = NEG
        if (v & 2) and not validN:
            mb[64:128, :] = NEG
        cbf[:, 384 + 256 * v:384 + 256 * v + 128] = ma
        cbf[:, 384 + 256 * v + 128:384 + 256 * (v + 1)] = mb
    cbf[:, 1408:1536] = np.roll(cbf[:, 0:128], -64, axis=0)
    cbf[:, 1536:2560] = np.roll(cbf[:, 384:1408], -64, axis=0)
    cf = np.zeros((128, 4), np.float32)
    inv = np.float32(500000.0) ** (-(np.arange(0, 16, 2, dtype=np.float32)) / np.float32(16.0))
    for p in range(128):
        dd = p % 64
        if dd < 16:
            cf[p, 0] = inv[dd % 8]
    cf[:, 1] = 1.0 if validP else 0.0
    cf[:, 2] = 1.0 if validN else 0.0
    return cbf.astype(NPBF), cf


def _gain_table(layers, pre_mix_norm, conv_w, attn_out_norm, conv_out_norm, post_mix_norm, pre_ffn_norm, post_ffn_norm):
    gtab = np.zeros((128, len(layers) * GT_PER_LAYER), np.float32)

    def put(col, vec):
        n = vec.shape[0] // 128
        gtab[:, col:col + n] = vec.reshape(n, 128).T
    for i, l in enumerate(layers):
        b = i * GT_PER_LAYER
        put(b + 0, pre_mix_norm[l]); put(b + 8, attn_out_norm[l]); put(b + 12, conv_out_norm[l])
        put(b + 16, post_mix_norm[l]); put(b + 24, pre_ffn_norm[l]); put(b + 32, post_ffn_norm[l])
        for k in range(3):
            put(b + 40 + 4 * k, conv_w[l, k])
    return gtab


_PROG = {}


def _run(L, layers, xTs, positions, P):
    if L not in _PROG:
        _PROG[L] = build_program(L)
    nc = _PROG[L]
    gtab = _gain_table(layers, P["pre_mix_norm"], P["conv_w"], P["attn_out_norm"], P["conv_out_norm"],
                       P["post_mix_norm"], P["pre_ffn_norm"], P["post_ffn_norm"])
    lsel = slice(layers[0], layers[-1] + 1)
    wi = np.ascontiguousarray(P["w_in"][lsel]); wo = np.ascontiguousarray(P["w_out"][lsel])
    wg = np.ascontiguousarray(P["w_gate_up"][lsel]); wd = np.ascontiguousarray(P["w_down"][lsel])
    in_maps = []
    for c in range(8):
        b, r = divmod(c, 2)
        cbf, cf = _const_tables(validP=(r == 1), validN=(r == 0))
        in_maps.append({
            "xT": xTs[c],
            "pos": np.ascontiguousarray(positions[b:b + 1, r * T:(r + 1) * T]).astype(np.int32),
            "w_in": wi, "w_out": wo, "w_gu": wg, "w_dn": wd,
            "gt": gtab, "cbf": cbf, "cf": cf,
        })
    res = run_bass_kernel_spmd(nc, in_maps, core_ids=list(range(8)))
    return [np.asarray(res.results[c]["yT"]) for c in range(8)]


def kernel(x, positions, pre_mix_norm, w_in, conv_w, attn_out_norm, conv_out_norm, w_out,
           post_mix_norm, pre_ffn_norm, w_gate_up, w_down, post_ffn_norm):
    P = dict(pre_mix_norm=np.asarray(pre_mix_norm, np.float32), w_in=np.asarray(w_in, np.float32),
             conv_w=np.asarray(conv_w, np.float32), attn_out_norm=np.asarray(attn_out_norm, np.float32),
             conv_out_norm=np.asarray(conv_out_norm, np.float32), w_out=np.asarray(w_out, np.float32),
             post_mix_norm=np.asarray(post_mix_norm, np.float32), pre_ffn_norm=np.asarray(pre_ffn_norm, np.float32),
             w_gate_up=np.asarray(w_gate_up, np.float32), w_down=np.asarray(w_down, np.float32),
             post_ffn_norm=np.asarray(post_ffn_norm, np.float32))
    x = np.asarray(x, np.float32)
    positions = np.asarray(positions)
    xTs = []
    for c in range(8):
        b, r = divmod(c, 2)
        xTs.append(np.ascontiguousarray(x[b, r * T:(r + 1) * T, :].T))
    if FUSED:
        xTs = _run(DEPTH, list(range(DEPTH)), xTs, positions, P)
    else:
        for l in range(DEPTH):
            xTs = _run(1, [l], xTs, positions, P)
    out = np.empty((4, 4096, D), np.float32)
    for c in range(8):
        b, r = divmod(c, 2)
        out[b, r * T:(r + 1) * T, :] = xTs[c].T
    return out
```

```python
import collections
import contextlib
import math

import numpy as np
import ml_dtypes

import concourse.bass as bass
import concourse.mybir as mybir
from concourse.bass_utils import run_bass_kernel_spmd

F32 = mybir.dt.float32
BF16 = mybir.dt.bfloat16
I32 = mybir.dt.int32
ALU = mybir.AluOpType
AF = mybir.ActivationFunctionType
NPBF = ml_dtypes.bfloat16

FUSED = True
DEPTH = 4
D = 1024
T = 2048
TB = 512
NTB = 4
FFN = 2816
NEG = -30000.0
EPS = 1e-6
GT_PER_LAYER = 52
CBW = 2560
REPLICA_GROUPS = [[0, 1], [2, 3], [4, 5], [6, 7]]
STOP = None
ENG_NAMES = ("pe", "act", "dve", "pool", "sp")


def sl(start, n, step=1):
    return slice(start, start + (n - 1) * step + 1, step)


class Op:
    __slots__ = ("idx", "eng", "fn", "reads", "writes", "dma", "tag", "inc", "count",
                 "waits", "marked", "known", "kdma")

    def __init__(self, idx, eng, fn, reads, writes, dma, tag, inc):
        self.idx = idx; self.eng = eng; self.fn = fn
        self.reads = tuple(reads); self.writes = tuple(writes)
        self.dma = dma; self.tag = tag; self.inc = inc
        self.count = 0; self.waits = []; self.marked = False
        self.known = None; self.kdma = None


class Sched:
    def __init__(self):
        self.ops = []
        self.last_write = {}
        self.readers = collections.defaultdict(dict)
        self.tag_count = collections.defaultdict(int)
        self.tag_inc = {}
        self.eng_known = {e: {} for e in ENG_NAMES}
        self.eng_kdma = {e: {} for e in ENG_NAMES}
        self.eng_ops = {e: [] for e in ENG_NAMES}

    def op(self, eng, fn, reads=(), writes=(), dma=False, tag=None, inc=16, nophase=False):
        i = len(self.ops)
        reads = list(reads); writes = list(writes)
        if not nophase:
            reads.append("PHASE")
        for r in list(reads):
            if r.startswith("ps"):
                reads.remove(r)
                if r not in writes:
                    writes.append(r)
        o = Op(i, eng, fn, reads, writes, dma, tag, inc)
        self.ops.append(o)
        deps = {}
        for r in o.reads:
            j = self.last_write.get(r)
            if j is not None:
                deps[j] = "RAW"
        for r in o.writes:
            j = self.last_write.get(r)
            if j is not None and j not in deps:
                deps[j] = "WAW"
            for j in self.readers[r].values():
                if j not in deps:
                    deps[j] = "WAR"
        known = self.eng_known[eng]
        kdma = self.eng_kdma[eng]
        waits = {}
        for j, kind in sorted(deps.items()):
            d = self.ops[j]
            if d.dma:
                if kdma.get(d.tag, 0) >= d.count:
                    continue
                c = self.tag_count[d.tag]
                waits[("dma", d.tag)] = c
                kdma[d.tag] = c
            else:
                if d.eng == eng:
                    if eng in ("pe", "sp") or kind != "RAW":
                        continue
                if known.get(d.eng, -1) >= j:
                    continue
                waits[("eng", d.eng)] = max(waits.get(("eng", d.eng), -1), j)
        for key, v in waits.items():
            if key[0] == "eng":
                d = self.ops[v]
                d.marked = True
                known[key[1]] = max(known.get(key[1], -1), v)
                for e2, k2 in d.known.items():
                    if known.get(e2, -1) < k2:
                        known[e2] = k2
                for t2, c2 in d.kdma.items():
                    if kdma.get(t2, 0) < c2:
                        kdma[t2] = c2
        o.waits = list(waits.items())
        if dma:
            assert self.tag_inc.setdefault(tag, inc) == inc
            self.tag_count[tag] += 1
            o.count = self.tag_count[tag]
        o.known = dict(known)
        o.kdma = dict(kdma)
        for r in o.reads:
            self.readers[r][("dma", i) if dma else eng] = i
        for r in o.writes:
            self.last_write[r] = i
            self.readers[r] = {}
        self.eng_ops[eng].append(o)
        return o

    def emit(self, nc, final_wait_tags=()):
        rank = {}
        for e in ENG_NAMES:
            c = 0
            for o in self.eng_ops[e]:
                if o.marked and not o.dma:
                    c += 1
                    rank[o.idx] = c
        with contextlib.ExitStack() as st:
            esem = {e: st.enter_context(nc.semaphore("s_" + e)) for e in ("pe", "act", "dve", "pool")}
            tsem = {t: st.enter_context(nc.semaphore("t_" + str(t))) for t in sorted(self.tag_count)}
            block = st.enter_context(nc.Block())
            handles = {"pe": block.tensor, "act": block.scalar, "dve": block.vector,
                       "pool": block.gpsimd, "sp": block.sync}

            def body(ename):
                def _f(eng):
                    for o in self.eng_ops[ename]:
                        for key, v in o.waits:
                            if key[0] == "eng":
                                eng.wait_ge(esem[key[1]], rank[v])
                            else:
                                eng.wait_ge(tsem[key[1]], v * self.tag_inc[key[1]])
                        ins = o.fn(eng)
                        if o.dma:
                            ins.then_inc(tsem[o.tag], o.inc)
                        elif o.marked:
                            ins.then_inc(esem[ename], 1)
                    if ename == "sp":
                        for t in final_wait_tags:
                            eng.wait_ge(tsem[t], self.tag_count[t] * self.tag_inc[t])
                return _f

            for e in ENG_NAMES:
                handles[e](body(e))


class SbAlloc:
    BASE = 16512
    LIMIT = 229344

    def __init__(self, nc):
        self.nc = nc
        self.off = self.BASE
        self.n = 0

    def at(self, name, shape, dtype, offset):
        nbytes = int(np.prod(shape[1:])) * (2 if dtype == BF16 else 4)
        assert offset % 32 == 0 and offset + nbytes <= self.LIMIT, (name, offset, nbytes)
        self.n += 1
        return self.nc.alloc_sbuf_tensor_at("%s_%d" % (name, self.n), list(shape), dtype, offset=offset), nbytes

    def new(self, name, shape, dtype):
        t, nbytes = self.at(name, shape, dtype, self.off)
        self.off += (nbytes + 31) // 32 * 32
        return t


def build_program(L):
    nc = bass.Bass("TRN2", target_bir_lowering=False)
    S = Sched()

    def dram(name, shape, dt, kind):
        return nc.dram_tensor(name, list(shape), dt, kind=kind)

    xT = dram("xT", [D, T], F32, "ExternalInput").ap()
    pos = dram("pos", [1, T], I32, "ExternalInput").ap()
    w_in = dram("w_in", [L, D, 3072], F32, "ExternalInput").ap()
    w_out = dram("w_out", [L, D, D], F32, "ExternalInput").ap()
    w_gu = dram("w_gu", [L, D, 2 * FFN], F32, "ExternalInput").ap()
    w_dn = dram("w_dn", [L, FFN, D], F32, "ExternalInput").ap()
    gt_d = dram("gt", [128, L * GT_PER_LAYER], F32, "ExternalInput").ap()
    cbf_d = dram("cbf", [128, CBW], BF16, "ExternalInput").ap()
    cf_d = dram("cf", [128, 4], F32, "ExternalInput").ap()
    yT = dram("yT", [D, T], F32, "ExternalOutput").ap()
    q_scr_t = dram("q_scr", [512, T], BF16, "Internal")
    sndk_t = dram("snd_k", [512, T], BF16, "Internal")
    sndv_t = dram("snd_v", [512, T], BF16, "Internal")
    sndz_t = dram("snd_z", [8, T], BF16, "Internal")
    rcvk_t = dram("rcv_k", [1024, T], BF16, "Internal")
    rcvv_t = dram("rcv_v", [1024, T], BF16, "Internal")
    rcvz_t = dram("rcv_z", [16, T], BF16, "Internal")
    q_scr = q_scr_t.ap()
    snd_k, snd_v, snd_z = sndk_t.ap(), sndv_t.ap(), sndz_t.ap()
    rcv_k, rcv_v, rcv_z = rcvk_t.ap(), rcvv_t.ap(), rcvz_t.ap()

    def allgather(src, dst, reads, wres, tag):
        S.op("pool", lambda e: e.collective_compute("AllGather", ALU.bypass, replica_groups=REPLICA_GROUPS, ins=[src], outs=[dst]),
             reads=reads, writes=[wres], dma=True, tag=tag, inc=1)

    A = SbAlloc(nc)
    x_sb = A.new("x", [128, 8, T], F32)
    cs = A.new("cs", [128, 2, T], BF16)
    cbf = A.new("cbf", [128, CBW], BF16)
    cf = A.new("cf", [128, 4], F32)
    gt = A.new("gt", [128, L * GT_PER_LAYER], F32)
    onesf = A.new("onesf", [128, 128], F32)
    selo = A.new("selo", [128, 128], F32)
    bar = A.new("bar", [128, 8], F32)
    NW = 3
    wsl = [A.new("w%d" % i, [128, 4096], BF16) for i in range(NW)]
    sq = A.new("sq", [128, 8, TB], BF16)
    rs = [A.new("rs%d" % i, [128, TB], F32) for i in range(2)]
    rq = [A.new("rq%d" % i, [128, TB], BF16) for i in range(2)]
    stg = [A.new("stg%d" % i, [128, TB], BF16) for i in range(4)]
    zh = A.new("zh", [128, 2, 4, 16], BF16)
    regB = A.off
    tmpf = [A.new("tmpf%d" % i, [128, TB], F32) for i in range(4)]
    oacc_e, _ = A.at("oacc_e", [128, T], F32, regB)
    arena = A.off
    hm, _ = A.at("hm", [128, 8, T], BF16, arena)
    regA = arena + 32768
    z, _ = A.at("z", [128, 4, T + 2], BF16, regA)
    bc, _ = A.at("bc", [128, 4, T], BF16, regA + 16416)
    k_ext, _ = A.at("k_ext", [128, 4096], BF16, regA)
    v_ext, _ = A.at("v_ext", [128, 4096], BF16, regA + 8192)
    vt, _ = A.at("vt", [128, 32, 160], BF16, regA + 16384)
    q_sb, _ = A.at("q_sb", [128, T], BF16, regA + 16384 + 10240)
    o2 = regA + 32800
    q16, _ = A.at("q16", [128, T], BF16, o2)
    pbuf = [A.at("pb%d" % i, [128, 256], BF16, o2 + 4096 + 512 * i)[0] for i in range(4)]
    oacc_o, _ = A.at("oacc_o", [128, T], F32, o2 + 4096 + 2048)
    mixbufs = [A.at("mixbuf%d" % i, [128, 8, TB], BF16, regA + 8192 * i)[0] for i in range(2)]
    h2, _ = A.at("h2", [128, 8, 1024], BF16, arena)
    hid, _ = A.at("hid", [128, 22, 1024], BF16, arena + 16384)
    f_bf, _ = A.at("f_bf", [128, 8, 1024], BF16, arena + 16384 + 45056)
    assert arena + 16384 + 45056 + 16384 <= A.LIMIT
    su_i, _ = A.at("su_i", [128, T], I32, arena)
    su_ang, _ = A.at("su_ang", [128, T], F32, arena + 8192)
    su_a2, _ = A.at("su_a2", [128, T], F32, arena + 16384)
    su_kf, _ = A.at("su_kf", [128, T], F32, arena + 24576)
    su_ki, _ = A.at("su_ki", [128, T], I32, arena + 32768)
    su_r, _ = A.at("su_r", [128, T], F32, arena + 40960)

    ps = [nc.alloc_psum_tensor("ps%d" % i, [128, TB], F32) for i in range(6)]
    psb = [nc.alloc_psum_tensor("ps%d" % i, [128, 1024], BF16) for i in (6, 7)]

    ident = cbf[:, 0:128]
    ones_bf = cbf[:, 128:256]
    perm = cbf[:, 256:384]

    ident2 = cbf[:, 1408:1536]

    def maskv(v):
        return cbf[:, 384 + 256 * v: 384 + 256 * (v + 1)]

    def maskv2(v):
        return cbf[:, 1536 + 256 * v: 1536 + 256 * (v + 1)]

    def barrier():
        S.op("pool", lambda e: e.memset(bar[:], 0.0), writes=["PHASE"], nophase=True)

    def finish():
        for c in range(8):
            S.op("sp", (lambda c_: lambda e: e.dma_start(out=yT[c_ * 128:(c_ + 1) * 128, :], in_=x_sb[:, c_, :]))(c),
                 reads=["x%d" % tb for tb in range(NTB)], dma=True, tag="yout")
        S.emit(nc, final_wait_tags=["yout"])
        return nc

    S.op("sp", lambda e: e.dma_start(out=cbf[:], in_=cbf_d), writes=["cbf"], dma=True, tag="c0")
    S.op("sp", lambda e: e.dma_start(out=cf[:], in_=cf_d), writes=["cf"], dma=True, tag="c0")
    S.op("sp", lambda e: e.dma_start(out=gt[:], in_=gt_d), writes=["gt"], dma=True, tag="c0")
    S.op("sp", lambda e: e.dma_start(out=su_i[:], in_=pos.partition_broadcast(128)), writes=["su_i"], dma=True, tag="c1")
    for c in range(8):
        S.op("sp", (lambda c: lambda e: e.dma_start(out=x_sb[:, c, :], in_=xT[c * 128:(c + 1) * 128, :]))(c),
             writes=["x%d" % tb for tb in range(NTB)], dma=True, tag="xin")
    S.op("pool", lambda e: e.memset(onesf[:], 1.0), writes=["onesf"])
    S.op("pool", lambda e: e.memset(selo[:, 0:64], 0.0), writes=["selo"])
    S.op("pool", lambda e: e.memset(selo[:, 64:128], 1.0), writes=["selo"])
    TWO_PI = 2.0 * math.pi
    S.op("dve", lambda e: e.tensor_copy(su_ang[:], su_i[:]), reads=["su_i"], writes=["su_ang"])
    S.op("dve", lambda e: e.tensor_scalar(su_ang[:], su_ang[:], cf[:, 0:1], None, ALU.mult), reads=["su_ang", "cf"], writes=["su_ang"])
    for which, shift in ((1, 0.0), (0, 0.5 * math.pi)):
        S.op("dve", (lambda sh: lambda e: e.tensor_scalar(su_a2[:], su_ang[:], sh, None, ALU.add))(shift), reads=["su_ang"], writes=["su_a2"])
        S.op("dve", lambda e: e.tensor_scalar(su_kf[:], su_a2[:], 1.0 / TWO_PI, None, ALU.mult), reads=["su_a2"], writes=["su_kf"])
        S.op("dve", lambda e: e.tensor_copy(su_ki[:], su_kf[:]), reads=["su_kf"], writes=["su_ki"])
        S.op("dve", lambda e: e.tensor_copy(su_kf[:], su_ki[:]), reads=["su_ki"], writes=["su_kf"])
        S.op("dve", lambda e: e.scalar_tensor_tensor(su_r[:], su_kf[:], -TWO_PI, su_a2[:], ALU.mult, ALU.add), reads=["su_kf", "su_a2"], writes=["su_r"])
        S.op("dve", lambda e: e.tensor_scalar(su_kf[:], su_r[:], math.pi, TWO_PI, ALU.is_gt, ALU.mult), reads=["su_r"], writes=["su_kf"])
        S.op("dve", lambda e: e.tensor_tensor(su_r[:], su_r[:], su_kf[:], ALU.subtract), reads=["su_r", "su_kf"], writes=["su_r"])
        S.op("dve", lambda e: e.tensor_scalar(su_kf[:], su_r[:], -math.pi, TWO_PI, ALU.is_lt, ALU.mult), reads=["su_r"], writes=["su_kf"])
        S.op("dve", lambda e: e.tensor_tensor(su_r[:], su_r[:], su_kf[:], ALU.add), reads=["su_r", "su_kf"], writes=["su_r"])
        S.op("act", (lambda w: lambda e: e.activation(cs[:, w, :], su_r[:], AF.Sin))(which), reads=["su_r"], writes=["cs"])
    barrier()
    if STOP == "setup":
        return finish()

    wtiles = []
    for l in range(L):
        for c0 in (1536, 2560, 512, 1024, 0, 2048):
            wtiles.append(("in", l, c0))
        for c0 in (0, 512):
            wtiles.append(("out", l, c0))
        for hf in range(2):
            for f in range(11):
                wtiles.append(("gu", l, f))
            for mp in range(4):
                for kh in range(2):
                    wtiles.append(("dn", l, (mp, kh)))
    wstate = {"issued": 0}

    def w_issue(upto):
        while wstate["issued"] <= min(upto, len(wtiles) - 1):
            i = wstate["issued"]
            kind, l, a = wtiles[i]
            s = i % NW
            slot = wsl[s]
            if kind in ("in", "out"):
                src = (w_in if kind == "in" else w_out)[l].rearrange("(kc p) n -> p kc n", p=128)[:, :, a:a + 512]
                dst = slot[:].rearrange("p (kc n) -> p kc n", kc=8)
                S.op("pool", (lambda d_, s_: lambda e: e.dma_start(out=d_, in_=s_))(dst, src),
                     writes=["w%d" % s], dma=True, tag="w%d" % s, nophase=True)
            elif kind == "gu":
                v = w_gu[l].rearrange("(kc p) n -> p kc n", p=128)
                dst = slot[:].rearrange("p (kc n) -> p kc n", kc=8)
                S.op("pool", (lambda d_, s_: lambda e: e.dma_start(out=d_, in_=s_))(dst[:, :, 0:256], v[:, :, a * 256:(a + 1) * 256]),
                     writes=["w%d" % s], dma=True, tag="w%d" % s, nophase=True)
                S.op("pool", (lambda d_, s_: lambda e: e.dma_start(out=d_, in_=s_))(dst[:, :, 256:512], v[:, :, FFN + a * 256:FFN + (a + 1) * 256]),
                     writes=["w%d" % s], dma=True, tag="w%d" % s, nophase=True)
            else:
                mp, kh = a
                v = w_dn[l].rearrange("(kc p) n -> p kc n", p=128)
                dst = slot[:, 0:2816].rearrange("p (kc n) -> p kc n", kc=11)
                S.op("pool", (lambda d_, s_: lambda e: e.dma_start(out=d_, in_=s_))(dst, v[:, kh * 11:(kh + 1) * 11, mp * 256:(mp + 1) * 256]),
                     writes=["w%d" % s], dma=True, tag="w%d" % s, nophase=True)
            wstate["issued"] += 1

    wcur = {"i": 0}

    def w_get():
        i = wcur["i"]
        w_issue(i)
        return i % NW, wsl[i % NW]

    def w_done():
        wcur["i"] += 1
        w_issue(wcur["i"] - 1 + NW)

    w_issue(NW - 1)

    cnt = {"mm": 0, "ss": 0, "rs": 0, "sq": 0, "tmpf": 0, "stg": 0, "rq": 0, "trb": 0, "mixb": 0}

    def rr(key, n):
        v = cnt[key] % n
        cnt[key] += 1
        return v

    def mm_group(bank, pairs, reads):
        def fn(e):
            ins = None
            n = len(pairs)
            for i, (a_, b_) in enumerate(pairs):
                ins = e.matmul(ps[bank][:, :], a_, b_, start=(i == 0), stop=(i == n - 1))
            return ins
        S.op("pe", fn, reads=reads, writes=["ps%d" % bank])

    def rstd_from(ssbank, dim):
        r = rr("rs", 2)
        S.op("dve", lambda e: e.tensor_scalar(rs[r][:], ps[ssbank][:, :], 1.0 / dim, EPS, ALU.mult, ALU.add),
             reads=["ps%d" % ssbank], writes=["rs%d" % r])
        S.op("act", lambda e: e.activation(rs[r][:], rs[r][:], AF.Sqrt), reads=["rs%d" % r], writes=["rs%d" % r])
        S.op("dve", lambda e: e.reciprocal(rs[r][:], rs[r][:]), reads=["rs%d" % r], writes=["rs%d" % r])
        return r

    def sq_ones(src_ap_fn, nch, src_res, ssbank, first, last):
        base = 0 if (cnt["sq"] % 2 == 0) else 4
        cnt["sq"] += 1
        assert nch <= 4
        S.op("act", lambda e: e.activation(sq[:, base:base + nch, :], src_ap_fn(), AF.Square),
             reads=src_res, writes=["sq%d" % base])

        def fn(e):
            ins = None
            for c in range(nch):
                ins = e.matmul(ps[ssbank][:, :], ones_bf, sq[:, base + c, :], start=(first and c == 0), stop=(last and c == nch - 1))
            return ins
        S.op("pe", fn, reads=["sq%d" % base, "cbf"], writes=["ps%d" % ssbank])

    def norm_x_to(dst_fn, dst_res, tbg, gcol0):
        ssb = 4 + rr("ss", 2)
        for half in range(2):
            sq_ones((lambda h_: lambda: x_sb[:, 4 * h_:4 * h_ + 4, tbg * TB:(tbg + 1) * TB])(half), 4, ["x%d" % tbg], ssb, half == 0, half == 1)
        r = rstd_from(ssb, D)
        for c in range(8):
            S.op("dve", (lambda c_: lambda e: e.scalar_tensor_tensor(dst_fn(c_), x_sb[:, c_, tbg * TB:(tbg + 1) * TB], gt[:, gcol0 + c_:gcol0 + c_ + 1], rs[r][:], ALU.mult, ALU.mult))(c),
                 reads=["x%d" % tbg, "gt", "rs%d" % r], writes=[dst_res])

    for l in range(L):
        g0 = l * GT_PER_LAYER
        for tb in range(NTB):
            norm_x_to((lambda tb_: lambda c_: hm[:, c_, tb_ * TB:(tb_ + 1) * TB])(tb), "h%d" % tb, tb, g0 + 0)

        pending = []

        def rotary_store(bank, c, tb, dst_dram, dst_res):
            r = rr("rq", 2)
            S.op("act", lambda e: e.activation(rq[r][:], ps[bank][:, :], AF.Copy), reads=["ps%d" % bank], writes=["rq%d" % r])

            def rest():
                S.op("pe", lambda e: e.matmul(ps[5][:, :], perm, rq[r][:], start=True, stop=True), reads=["rq%d" % r, "cbf"], writes=["ps5"])
                t1 = rr("tmpf", 4); t2 = rr("tmpf", 4)
                S.op("dve", lambda e: e.tensor_tensor(tmpf[t1][:], ps[bank][:, :], cs[:, 0, tb * TB:(tb + 1) * TB], ALU.mult),
                     reads=["ps%d" % bank, "cs"], writes=["tmpf%d" % t1])
                S.op("dve", lambda e: e.tensor_tensor(tmpf[t2][:], ps[5][:, :], cs[:, 1, tb * TB:(tb + 1) * TB], ALU.mult),
                     reads=["ps5", "cs"], writes=["tmpf%d" % t2])
                s_ = rr("stg", 4)
                S.op("pool", lambda e: e.tensor_tensor(stg[s_][:], tmpf[t1][:], tmpf[t2][:], ALU.add),
                     reads=["tmpf%d" % t1, "tmpf%d" % t2], writes=["stg%d" % s_])
                S.op("sp", lambda e: e.dma_start(out=dst_dram[c * 128:(c + 1) * 128, tb * TB:(tb + 1) * TB], in_=stg[s_][:]),
                     reads=["stg%d" % s_], writes=[dst_res], dma=True, tag="stg%d" % s_)
            pending.append(rest)

        def flush_pending(keep=0):
            while len(pending) > keep:
                pending.pop(0)()

        if STOP == "P1a":
            return finish()
        for kind in ("u", "C", "k", "v", "q", "B"):
            if STOP == "P1u" and kind == "C":
                return finish()
            if STOP == "P1C" and kind == "k":
                return finish()
            if STOP == "P1k" and kind == "v":
                return finish()
            ws, slot = w_get()
            wv = slot[:].rearrange("p (kc n) -> p kc n", kc=8)
            for mi in range(4):
                for tb in range(NTB):
                    bank = rr("mm", 4)
                    mm_group(bank, [(wv[:, kc, mi * 128:(mi + 1) * 128], hm[:, kc, tb * TB:(tb + 1) * TB]) for kc in range(8)],
                             ["w%d" % ws, "h%d" % tb])
                    flush_pending()
                    zs = z[:, mi, 1 + tb * TB:1 + (tb + 1) * TB]
                    if kind == "u":
                        S.op("act", (lambda b_, o_: lambda e: e.activation(o_, ps[b_][:, :], AF.Copy))(bank, zs),
                             reads=["ps%d" % bank], writes=["z%d" % mi])
                    elif kind == "C":
                        S.op("dve", (lambda b_, o_: lambda e: e.tensor_tensor(o_, o_, ps[b_][:, :], ALU.mult))(bank, zs),
                             reads=["ps%d" % bank, "z%d" % mi], writes=["z%d" % mi])
                    elif kind == "B":
                        S.op("act", (lambda b_, o_: lambda e: e.activation(o_, ps[b_][:, :], AF.Copy))(bank, bc[:, mi, tb * TB:(tb + 1) * TB]),
                             reads=["ps%d" % bank], writes=["bc%d" % mi])
                    elif kind == "k":
                        rotary_store(bank, mi, tb, snd_k, "snd%d" % mi)
                    elif kind == "q":
                        rotary_store(bank, mi, tb, q_scr, "qs%d" % mi)
                    else:
                        s = rr("stg", 4)
                        S.op("act", (lambda b_, s_: lambda e: e.activation(stg[s_][:], ps[b_][:, :], AF.Copy))(bank, s),
                             reads=["ps%d" % bank], writes=["stg%d" % s])
                        S.op("sp", (lambda s_, mi_, tb_: lambda e: e.dma_start(out=snd_v[mi_ * 128:(mi_ + 1) * 128, tb_ * TB:(tb_ + 1) * TB], in_=stg[s_][:]))(s, mi, tb),
                             reads=["stg%d" % s], writes=["snd%d" % (4 + mi)], dma=True, tag="stg%d" % s)
            flush_pending()
            w_done()
            if kind == "C":
                for f, c0 in ((0, 1), (1, T - 15)):
                    dst = bass.AP(sndz_t, f * 8192, [[16, 128], [2048, 4], [1, 16]])
                    S.op("sp", (lambda d_, c0_: lambda e: e.dma_start(out=d_, in_=z[:, :, c0_:c0_ + 16]))(dst, c0),
                         reads=["z%d" % i for i in range(4)], writes=["sndz"], dma=True, tag="sndz")
                allgather(snd_z, rcv_z, ["sndz"], "rcvz", "ccz")
            if kind == "k":
                allgather(snd_k, rcv_k, ["snd%d" % i for i in range(4)], "rcvk", "cck")
            if kind == "v":
                allgather(snd_v, rcv_v, ["snd%d" % i for i in range(4, 8)], "rcvv", "ccv")
        if STOP == "P1g":
            return finish()
        for f, (rank, slab) in enumerate(((0, 1), (1, 0))):
            src = bass.AP(rcvz_t, rank * 16384 + slab * 8192, [[16, 128], [2048, 4], [1, 16]])
            S.op("sp", (lambda f_, s_: lambda e: e.dma_start(out=zh[:, f_, :, :], in_=s_))(f, src),
                 reads=["rcvz"], writes=["zh"], dma=True, tag="ldzh")
        S.op("dve", lambda e: e.tensor_scalar(z[:, :, 0:1], zh[:, 0, :, 15:16], cf[:, 1:2], None, ALU.mult),
             reads=["zh", "cf"], writes=["z%d" % i for i in range(4)])
        S.op("dve", lambda e: e.tensor_scalar(z[:, :, T + 1:T + 2], zh[:, 1, :, 0:1], cf[:, 2:3], None, ALU.mult),
             reads=["zh", "cf"], writes=["z%d" % i for i in range(4)])
        for c in range(4):
            for tb in range(NTB):
                t1 = rr("tmpf", 4)
                wc = lambda k, c_=c, g0_=g0: gt[:, g0_ + 40 + 4 * k + c_:g0_ + 40 + 4 * k + c_ + 1]
                zz = lambda k, c_=c, tb_=tb: z[:, c_, tb_ * TB + k:tb_ * TB + k + TB]
                S.op("dve", (lambda t_, zz_, wc_: lambda e: e.tensor_scalar(tmpf[t_][:], zz_(0), wc_(0), None, ALU.mult))(t1, zz, wc),
                     reads=["z%d" % c, "gt"], writes=["tmpf%d" % t1])
                S.op("dve", (lambda t_, zz_, wc_: lambda e: e.scalar_tensor_tensor(tmpf[t_][:], zz_(1), wc_(1), tmpf[t_][:], ALU.mult, ALU.add))(t1, zz, wc),
                     reads=["z%d" % c, "gt", "tmpf%d" % t1], writes=["tmpf%d" % t1])
                S.op("dve", (lambda t_, zz_, wc_: lambda e: e.scalar_tensor_tensor(tmpf[t_][:], zz_(2), wc_(2), tmpf[t_][:], ALU.mult, ALU.add))(t1, zz, wc),
                     reads=["z%d" % c, "gt", "tmpf%d" % t1], writes=["tmpf%d" % t1])
                S.op("pool", (lambda t_, c_, tb_: lambda e: e.tensor_tensor(hm[:, 4 + c_, tb_ * TB:(tb_ + 1) * TB], tmpf[t_][:], bc[:, c_, tb_ * TB:(tb_ + 1) * TB], ALU.mult))(t1, c, tb),
                     reads=["tmpf%d" % t1, "bc%d" % c], writes=["hmB%d" % tb])
        barrier()
        if STOP == "P1":
            return finish()

        S.op("pool", lambda e: e.memset(vt[:], 1.0), writes=["vt"])
        blk = 0
        for hp in range(4):
            S.op("sp", (lambda hp_: lambda e: e.dma_start(out=q_sb[:], in_=q_scr[hp_ * 128:(hp_ + 1) * 128, :]))(hp),
                 reads=["qs%d" % hp], writes=["q_sb"], dma=True, tag="ldq")
            for (buf, sn, rc, rres, soff, res, tg) in ((k_ext, snd_k, rcv_k, "rcvk", 0, "k_ext", "ldk"), (v_ext, snd_v, rcv_v, "rcvv", 4, "v_ext", "ldv")):
                r0 = hp * 128
                S.op("sp", (lambda b_, rc_, r_: lambda e: e.dma_start(out=b_[:, 0:1024], in_=rc_[r_:r_ + 128, 1024:2048]))(buf, rc, r0),
                     reads=[rres], writes=[res], dma=True, tag=tg)
                S.op("sp", (lambda b_, sn_, r_: lambda e: e.dma_start(out=b_[:, 1024:3072], in_=sn_[r_:r_ + 128, :]))(buf, sn, r0),
                     reads=["snd%d" % (soff + hp)], writes=[res], dma=True, tag=tg)
                S.op("sp", (lambda b_, rc_, r_: lambda e: e.dma_start(out=b_[:, 3072:4096], in_=rc_[512 + r_:512 + r_ + 128, 0:1024]))(buf, rc, r0),
                     reads=[rres], writes=[res], dma=True, tag=tg)
            S.op("pool", lambda e: e.tensor_copy(q16[:].rearrange("p (r m) -> p r m", r=16), q_sb[:].rearrange("p (m r) -> p r m", r=16)),
                 reads=["q_sb"], writes=["q16"])
            for d in (1, 4, 16):
                nblk = 16 // d
                ntile = d * (nblk + 1)
                for t0 in range(0, ntile, 8):
                    nt = min(8, ntile - t0)
                    tbk = rr("trb", 2)

                    def trfn(e, t0=t0, nt=nt, d=d, nblk=nblk, tbk=tbk):
                        ins = None
                        for i in range(nt):
                            rho, j = divmod(t0 + i, nblk + 1)
                            e0 = 1024 + rho + d * (128 * j - 64)
                            ins = e.transpose(psb[tbk][:, i * 128:(i + 1) * 128], v_ext[:, sl(e0, 128, d)], ident)
                        return ins
                    S.op("pe", trfn, reads=["v_ext", "cbf"], writes=["ps%d" % (6 + tbk)])
                    pv = lambda nt=nt, tbk=tbk: psb[tbk][:, 0:nt * 128].rearrange("p (t c) -> p t c", c=128)
                    S.op("act", (lambda t0_, nt_, pv_: lambda e: e.activation(vt[:, t0_:t0_ + nt_, 0:64], pv_()[:, :, 0:64], AF.Copy))(t0, nt, pv),
                         reads=["ps%d" % (6 + tbk)], writes=["vt"])
                    S.op("dve", (lambda t0_, nt_, pv_: lambda e: e.tensor_copy(vt[:, t0_:t0_ + nt_, 96:160], pv_()[:, :, 64:128]))(t0, nt, pv),
                         reads=["ps%d" % (6 + tbk)], writes=["vt"])
                blocks = [(rho, j) for rho in range(d) for j in range(nblk)]
                pars = []
                for bi in range(len(blocks)):
                    pars.append(blk % 2)
                    blk += 1

                def emit_scores(bi, d=d, nblk=nblk, blocks=blocks, pars=pars):
                    rho, j = blocks[bi]
                    par = pars[bi]
                    variant = (1 if j == 0 else 0) + (2 if j == nblk - 1 else 0)
                    if d == 16:
                        qsrc = lambda p0, rho=rho: q16[p0:p0 + 64, rho * 128:(rho + 1) * 128]
                    else:
                        qsrc = lambda p0, rho=rho, j=j, d=d: q_sb[p0:p0 + 64, sl(rho + d * 128 * j, 128, d)]
                    eA = 1024 + rho + d * (128 * j - 64)
                    eB = eA + d * 128
                    for hpar, p0 in ((0, 0), (1, 64)):
                        sbank = 2 * hpar + par

                        def sfn(e, sbank=sbank, p0=p0, eA=eA, eB=eB, d=d, qsrc=qsrc, variant=variant):
                            e.matmul(ps[sbank][:, 0:256], ident[p0:p0 + 64, :], maskv(variant)[p0:p0 + 64, :], start=True, stop=False)
                            e.matmul(ps[sbank][:, 0:256], ident2[p0:p0 + 64, :], maskv2(variant)[p0:p0 + 64, :], start=False, stop=False)
                            e.matmul(ps[sbank][:, 0:128], k_ext[p0:p0 + 64, sl(eA, 128, d)], qsrc(p0), start=False, stop=False)
                            return e.matmul(ps[sbank][:, 128:256], k_ext[p0:p0 + 64, sl(eB, 128, d)], qsrc(p0), start=False, stop=True)
                        S.op("pe", sfn, reads=["k_ext", "q16" if d == 16 else "q_sb", "cbf"], writes=["ps%d" % sbank])
                        pb = 2 * hpar + par
                        S.op("act", (lambda pb_, sb_: lambda e: e.activation(pbuf[pb_][:], ps[sb_][:, 0:256], AF.Exp, scale=0.125))(pb, sbank),
                             reads=["ps%d" % sbank], writes=["pb%d" % pb])

                def emit_pv(bi, d=d, nblk=nblk, blocks=blocks, pars=pars):
                    rho, j = blocks[bi]
                    par = pars[bi]
                    tA = rho * (nblk + 1) + j
                    obank = 4 + (bi // 2) % 2
                    osl = bi % 2
                    for hpar in (0, 1):
                        pb = 2 * hpar + par

                        def ofn(e, hpar=hpar, pb=pb, obank=obank, osl=osl, tA=tA):
                            if hpar == 0:
                                o_ap = ps[obank][0:65, osl * 128:(osl + 1) * 128]
                                la, lb = vt[:, tA, 0:65], vt[:, tA + 1, 0:65]
                            else:
                                o_ap = ps[obank][:, 256 + osl * 128:256 + (osl + 1) * 128]
                                la, lb = vt[:, tA, 32:160], vt[:, tA + 1, 32:160]
                            e.matmul(o_ap, la, pbuf[pb][:, 0:128], start=True, stop=False)
                            return e.matmul(o_ap, lb, pbuf[pb][:, 128:256], start=False, stop=True)
                        S.op("pe", ofn, reads=["vt", "pb%d" % pb], writes=["ps%d" % obank])
                    if bi % 2 == 1 or bi == len(blocks) - 1:
                        nb = 2 if bi % 2 == 1 else 1
                        b0 = bi - (nb - 1)
                        rho0, j0 = blocks[b0]
                        if d == 16:
                            dst = lambda t_, p_, rho0=rho0, nb=nb: t_[p_, :].rearrange("p (m r) -> p r m", r=16)[:, rho0:rho0 + nb, :]
                            src = lambda ap_: ap_.rearrange("p (r m) -> p r m", m=128)
                        else:
                            dst = lambda t_, p_, rho0=rho0, j0=j0, d=d, nb=nb: t_[p_, sl(rho0 + d * 128 * j0, 128 * nb, d)]
                            src = lambda ap_: ap_
                        for hpar in (0, 1):
                            if hpar == 0:
                                pp = slice(0, 65); o_t = oacc_e; c0 = 0; res = "oacc_e"
                            else:
                                pp = slice(0, 128); o_t = oacc_o; c0 = 256; res = "oacc_o"
                            if d == 1:
                                S.op("dve", (lambda o_t_, pp_, c0_, nb_, ob_, dst_, src_: lambda e: e.tensor_copy(dst_(o_t_, pp_), src_(ps[ob_][pp_, c0_:c0_ + 128 * nb_])))(o_t, pp, c0, nb, obank, dst, src),
                                     reads=["ps%d" % obank], writes=[res])
                            else:
                                S.op("dve", (lambda o_t_, pp_, c0_, nb_, ob_, dst_, src_: lambda e: e.tensor_tensor(dst_(o_t_, pp_), dst_(o_t_, pp_), src_(ps[ob_][pp_, c0_:c0_ + 128 * nb_]), ALU.add))(o_t, pp, c0, nb, obank, dst, src),
                                     reads=["ps%d" % obank, res], writes=[res])

                emit_scores(0)
                for bi in range(len(blocks)):
                    if bi + 1 < len(blocks):
                        emit_scores(bi + 1)
                    emit_pv(bi)
            S.op("dve", lambda e: e.reciprocal(oacc_e[64:65, :], oacc_e[64:65, :]), reads=["oacc_e"], writes=["oacc_e"])
            S.op("dve", lambda e: e.reciprocal(oacc_o[32:33, :], oacc_o[32:33, :]), reads=["oacc_o"], writes=["oacc_o"])
            for tb in range(NTB):
                cs_ = slice(tb * TB, (tb + 1) * TB)
                S.op("pe", (lambda c_: lambda e: e.matmul(ps[1][0:64, :], onesf[64:65, 0:64], oacc_e[64:65, c_], start=True, stop=True))(cs_),
                     reads=["oacc_e", "onesf"], writes=["ps1"])
                S.op("pe", (lambda c_: lambda e: e.matmul(ps[0][:, :], selo[32:33, :], oacc_o[32:33, c_], start=True, stop=True))(cs_),
                     reads=["oacc_o", "selo"], writes=["ps0"])
                S.op("dve", (lambda c_, hp_: lambda e: e.tensor_tensor(hm[0:64, hp_, c_], oacc_e[0:64, c_], ps[1][0:64, :], ALU.mult))(cs_, hp),
                     reads=["oacc_e", "ps1"], writes=["hmA%d" % tb])
                S.op("dve", (lambda c_, hp_: lambda e: e.tensor_tensor(hm[64:128, hp_, c_], oacc_o[64:128, c_], ps[0][64:128, :], ALU.mult))(cs_, hp),
                     reads=["oacc_o", "ps0"], writes=["hmA%d" % tb])
        barrier()
        if STOP == "P2":
            return finish()

        i0 = wcur["i"]
        w_issue(i0 + 1)
        ws0, ws1 = i0 % NW, (i0 + 1) % NW
        slot0, slot1 = wsl[ws0], wsl[ws1]
        wo = [slot0[:].rearrange("p (kc n) -> p kc n", kc=8), slot1[:].rearrange("p (kc n) -> p kc n", kc=8)]
        wres = ["w%d" % ws0, "w%d" % ws1]
        for tb in range(NTB):
            cs_ = slice(tb * TB, (tb + 1) * TB)
            for grp, gcol in ((0, g0 + 8), (1, g0 + 12)):
                ssb = 4 + rr("ss", 2)
                sq_ones((lambda g_, c_: lambda: hm[:, 4 * g_:4 * g_ + 4, c_])(grp, cs_), 4, ["hmA%d" % tb, "hmB%d" % tb], ssb, True, True)
                r = rstd_from(ssb, 512)
                for c in range(4):
                    cc = 4 * grp + c
                    S.op("dve", (lambda cc_, c_, r_, gc_: lambda e: e.scalar_tensor_tensor(hm[:, cc_, c_], hm[:, cc_, c_], gt[:, gc_:gc_ + 1], rs[r_][:], ALU.mult, ALU.mult))(cc, cs_, r, gcol + c),
                         reads=["hmA%d" % tb, "hmB%d" % tb, "gt", "rs%d" % r], writes=["hmA%d" % tb if grp == 0 else "hmB%d" % tb])
        for tb in range(NTB):
            cs_ = slice(tb * TB, (tb + 1) * TB)
            mbi = rr("mixb", 2)
            mb = mixbufs[mbi]
            ssb = 4 + rr("ss", 2)
            pend3 = []
            for m in range(8):
                bank = rr("mm", 4)
                mm_group(bank, [(wo[m // 4][:, kc, (m % 4) * 128:(m % 4 + 1) * 128], hm[:, kc, cs_]) for kc in range(8)],
                         [wres[m // 4], "hmA%d" % tb, "hmB%d" % tb])
                while pend3:
                    pend3.pop(0)()
                S.op("act", (lambda b_, m_, mb_: lambda e: e.activation(mb_[:, m_, :], ps[b_][:, :], AF.Copy))(bank, m, mb),
                     reads=["ps%d" % bank], writes=["mixbuf%d" % mbi])
                sqs = 0 if (cnt["sq"] % 2 == 0) else 4
                cnt["sq"] += 1
                S.op("act", (lambda b_, s_: lambda e: e.activation(sq[:, s_, :], ps[b_][:, :], AF.Square))(bank, sqs),
                     reads=["ps%d" % bank], writes=["sq%d" % sqs])
                pend3.append((lambda s_, m_, sb_: lambda: S.op("pe", lambda e: e.matmul(ps[sb_][:, :], ones_bf, sq[:, s_, :], start=(m_ == 0), stop=(m_ == 7)),
                                                               reads=["sq%d" % s_, "cbf"], writes=["ps%d" % sb_]))(sqs, m, ssb))
            while pend3:
                pend3.pop(0)()
            r = rstd_from(ssb, D)
            for m in range(8):
                t1 = rr("tmpf", 4)
                S.op("dve", (lambda t_, m_, r_, gc_, mb_: lambda e: e.scalar_tensor_tensor(tmpf[t_][:], mb_[:, m_, :], gt[:, gc_:gc_ + 1], rs[r_][:], ALU.mult, ALU.mult))(t1, m, r, g0 + 16 + m, mb),
                     reads=["mixbuf%d" % mbi, "gt", "rs%d" % r], writes=["tmpf%d" % t1])
                S.op("pool", (lambda t_, m_, c_: lambda e: e.tensor_tensor(x_sb[:, m_, c_], x_sb[:, m_, c_], tmpf[t_][:], ALU.add))(t1, m, cs_),
                     reads=["tmpf%d" % t1, "x%d" % tb], writes=["x%d" % tb])
        w_done(); w_done()
        barrier()
        if STOP == "P3":
            return finish()

        for hf in range(2):
            for tb2 in range(2):
                tbg = hf * 2 + tb2
                norm_x_to((lambda tb2_: lambda c_: h2[:, c_, tb2_ * TB:(tb2_ + 1) * TB])(tb2), "h2_%d" % tb2, tbg, g0 + 24)
            for f in range(11):
                ws, slot = w_get()
                wv = slot[:].rearrange("p (kc n) -> p kc n", kc=8)
                for tb2 in range(2):
                    for mi in range(2):
                        bg = rr("mm", 4)
                        mm_group(bg, [(wv[:, kc, mi * 128:(mi + 1) * 128], h2[:, kc, tb2 * TB:(tb2 + 1) * TB]) for kc in range(8)],
                                 ["w%d" % ws, "h2_%d" % tb2])
                        bu = rr("mm", 4)
                        mm_group(bu, [(wv[:, kc, 256 + mi * 128:256 + (mi + 1) * 128], h2[:, kc, tb2 * TB:(tb2 + 1) * TB]) for kc in range(8)],
                                 ["w%d" % ws, "h2_%d" % tb2])
                        t1 = rr("tmpf", 4)
                        S.op("act", (lambda t_, b_: lambda e: e.activation(tmpf[t_][:], ps[b_][:, :], AF.Silu))(t1, bg),
                             reads=["ps%d" % bg], writes=["tmpf%d" % t1])
                        S.op("dve", (lambda t_, b_, fc_, tb2_: lambda e: e.tensor_tensor(hid[:, fc_, tb2_ * TB:(tb2_ + 1) * TB], tmpf[t_][:], ps[b_][:, :], ALU.mult))(t1, bu, 2 * f + mi, tb2),
                             reads=["tmpf%d" % t1, "ps%d" % bu], writes=["hid%d" % tb2])
                w_done()
            ssbs = [4, 5]
            pend4 = []
            for mp in range(4):
                slots = []
                for kh in range(2):
                    ws, slot = w_get()
                    wv = slot[:, 0:2816].rearrange("p (kc n) -> p kc n", kc=11)
                    for mi in range(2):
                        for tb2 in range(2):
                            bank = mi * 2 + tb2

                            def dfn(e, wv=wv, mi=mi, tb2=tb2, bank=bank, kh=kh):
                                ins = None
                                for kk in range(11):
                                    ins = e.matmul(ps[bank][:, :], wv[:, kk, mi * 128:(mi + 1) * 128], hid[:, kh * 11 + kk, tb2 * TB:(tb2 + 1) * TB],
                                                   start=(kh == 0 and kk == 0), stop=(kh == 1 and kk == 10))
                                return ins
                            S.op("pe", dfn, reads=["w%d" % ws, "hid%d" % tb2], writes=["ps%d" % bank])
                    if kh == 0:
                        while pend4:
                            pend4.pop(0)()
                    w_done()
                for mi in range(2):
                    for tb2 in range(2):
                        bank = mi * 2 + tb2
                        m = mp * 2 + mi
                        S.op("dve", (lambda b_, m_, tb2_: lambda e: e.tensor_copy(f_bf[:, m_, tb2_ * TB:(tb2_ + 1) * TB], ps[b_][:, :]))(bank, m, tb2),
                             reads=["ps%d" % bank], writes=["f_bf%d" % tb2])
                        sqg = 4 * (mp % 2)
                        sqs = sqg + bank
                        S.op("act", (lambda b_, s_: lambda e: e.activation(sq[:, s_, :], ps[b_][:, :], AF.Square))(bank, sqs),
                             reads=["ps%d" % bank], writes=["sq%d" % sqg])
                        pend4.append((lambda s_, g_, m_, sb_: lambda: S.op("pe", lambda e: e.matmul(ps[sb_][:, :], ones_bf, sq[:, s_, :], start=(m_ == 0), stop=(m_ == 7)),
                                                                           reads=["sq%d" % g_, "cbf"], writes=["ps%d" % sb_]))(sqs, sqg, m, ssbs[tb2]))
            while pend4:
                pend4.pop(0)()
            for tb2 in range(2):
                tbg = hf * 2 + tb2
                r = rstd_from(ssbs[tb2], D)
                for m in range(8):
                    t1 = rr("tmpf", 4)
                    S.op("dve", (lambda t_, m_, r_, tb2_, gc_: lambda e: e.scalar_tensor_tensor(tmpf[t_][:], f_bf[:, m_, tb2_ * TB:(tb2_ + 1) * TB], gt[:, gc_:gc_ + 1], rs[r_][:], ALU.mult, ALU.mult))(t1, m, r, tb2, g0 + 32 + m),
                         reads=["f_bf%d" % tb2, "gt", "rs%d" % r], writes=["tmpf%d" % t1])
                    S.op("pool", (lambda t_, m_, tbg_: lambda e: e.tensor_tensor(x_sb[:, m_, tbg_ * TB:(tbg_ + 1) * TB], x_sb[:, m_, tbg_ * TB:(tbg_ + 1) * TB], tmpf[t_][:], ALU.add))(t1, m, tbg),
                         reads=["tmpf%d" % t1, "x%d" % tbg], writes=["x%d" % tbg])
        barrier()

    return finish()


def _const_tables(validP, validN):
    cbf = np.zeros((128, CBW), np.float32)
    cbf[:, 0:128] = np.eye(128)
    cbf[:, 128:256] = 1.0
    perm = np.zeros((128, 128), np.float32)
    for par in range(2):
        for dd in range(8):
            p = par * 64 + dd
            perm[p + 8, p] = -1.0
            perm[p, p + 8] = 1.0
    cbf[:, 256:384] = perm
    a = np.arange(128)[:, None]
    b = np.arange(128)[None, :]
    mA = np.where(b <= a, 0.0, NEG)
    mB = np.where(b >= a, 0.0, NEG)
    for v in range(4):
        ma = mA.copy(); mb = mB.copy()
        if (v & 1) and not validP:
            ma[0:64, :] = NEG
        if (v & 2) and not validN:
            mb[64:128, :] = NEG
        cbf[:, 384 + 256 * v:384 + 256 * v + 128] = ma
        cbf[:, 384 + 256 * v + 128:384 + 256 * (v + 1)] = mb
    cbf[:, 1408:1536] = np.roll(cbf[:, 0:128], -64, axis=0)
    cbf[:, 1536:2560] = np.roll(cbf[:, 384:1408], -64, axis=0)
    cf = np.zeros((128, 4), np.float32)
    inv = np.float32(500000.0) ** (-(np.arange(0, 16, 2, dtype=np.float32)) / np.float32(16.0))
    for p in range(128):
        dd = p % 64
        if dd < 16:
            cf[p, 0] = inv[dd % 8]
    cf[:, 1] = 1.0 if validP else 0.0
    cf[:, 2] = 1.0 if validN else 0.0
    return cbf.astype(NPBF), cf


def _gain_table(layers, pre_mix_norm, conv_w, attn_out_norm, conv_out_norm, post_mix_norm, pre_ffn_norm, post_ffn_norm):
    gtab = np.zeros((128, len(layers) * GT_PER_LAYER), np.float32)

    def put(col, vec):
        n = vec.shape[0] // 128
        gtab[:, col:col + n] = vec.reshape(n, 128).T
    for i, l in enumerate(layers):
        b = i * GT_PER_LAYER
        put(b + 0, pre_mix_norm[l]); put(b + 8, attn_out_norm[l]); put(b + 12, conv_out_norm[l])
        put(b + 16, post_mix_norm[l]); put(b + 24, pre_ffn_norm[l]); put(b + 32, post_ffn_norm[l])
        for k in range(3):
            put(b + 40 + 4 * k, conv_w[l, k])
    return gtab


_PROG = {}


def _run(L, layers, xTs, positions, P):
    if L not in _PROG:
        _PROG[L] = build_program(L)
    nc = _PROG[L]
    gtab = _gain_table(layers, P["pre_mix_norm"], P["conv_w"], P["attn_out_norm"], P["conv_out_norm"],
                       P["post_mix_norm"], P["pre_ffn_norm"], P["post_ffn_norm"])
    lsel = slice(layers[0], layers[-1] + 1)
    wi = np.ascontiguousarray(P["w_in"][lsel]); wo = np.ascontiguousarray(P["w_out"][lsel])
    wg = np.ascontiguousarray(P["w_gate_up"][lsel]); wd = np.ascontiguousarray(P["w_down"][lsel])
    in_maps = []
    for c in range(8):
        b, r = divmod(c, 2)
        cbf, cf = _const_tables(validP=(r == 1), validN=(r == 0))
        in_maps.append({
            "xT": xTs[c],
            "pos": np.ascontiguousarray(positions[b:b + 1, r * T:(r + 1) * T]).astype(np.int32),
            "w_in": wi, "w_out": wo, "w_gu": wg, "w_dn": wd,
            "gt": gtab, "cbf": cbf, "cf": cf,
        })
    res = run_bass_kernel_spmd(nc, in_maps, core_ids=list(range(8)))
    return [np.asarray(res.results[c]["yT"]) for c in range(8)]


def kernel(x, positions, pre_mix_norm, w_in, conv_w, attn_out_norm, conv_out_norm, w_out,
           post_mix_norm, pre_ffn_norm, w_gate_up, w_down, post_ffn_norm):
    P = dict(pre_mix_norm=np.asarray(pre_mix_norm, np.float32), w_in=np.asarray(w_in, np.float32),
             conv_w=np.asarray(conv_w, np.float32), attn_out_norm=np.asarray(attn_out_norm, np.float32),
             conv_out_norm=np.asarray(conv_out_norm, np.float32), w_out=np.asarray(w_out, np.float32),
             post_mix_norm=np.asarray(post_mix_norm, np.float32), pre_ffn_norm=np.asarray(pre_ffn_norm, np.float32),
             w_gate_up=np.asarray(w_gate_up, np.float32), w_down=np.asarray(w_down, np.float32),
             post_ffn_norm=np.asarray(post_ffn_norm, np.float32))
    x = np.asarray(x, np.float32)
    positions = np.asarray(positions)
    xTs = []
    for c in range(8):
        b, r = divmod(c, 2)
        xTs.append(np.ascontiguousarray(x[b, r * T:(r + 1) * T, :].T))
    if FUSED:
        xTs = _run(DEPTH, list(range(DEPTH)), xTs, positions, P)
    else:
        for l in range(DEPTH):
            xTs = _run(1, [l], xTs, positions, P)
    out = np.empty((4, 4096, D), np.float32)
    for c in range(8):
        b, r = divmod(c, 2)
        out[b, r * T:(r + 1) * T, :] = xTs[c].T
    return out
```
